# Optimizing a Trainium2 kernel written in Bass

```python
import math
import jax, jax.numpy as jnp
from jax import lax
import numpy as np

D_MODEL = 1024
BATCH = 4
SEQ = 8192
DEPTH = 1
DEC_BATCH = 128
DEC_SEQ = 8
PAST_LEN = 16384
PAGE_SIZE = 128

N_HEADS = 8
QK_NOPE = 64
QK_ROPE = 32
V_DIM = 64
Q_LORA = 256
KV_LORA = 256
MLA_WIDTH = N_HEADS * V_DIM
ATTN_SCALE = 1.0 / math.sqrt(QK_NOPE + QK_ROPE)
ROPE_BASE = 10000.0
Q_BLOCK = 128
LRU_WIDTH = 512
LRU_BLOCKS = 8
LRU_BLOCK = LRU_WIDTH // LRU_BLOCKS
CONV_W = 4
LRU_C = 8.0
EPS = 1e-6
IN_WIDTHS = (Q_LORA, KV_LORA, QK_ROPE, MLA_WIDTH, LRU_WIDTH, LRU_WIDTH, D_MODEL, D_MODEL)
IN_COLS = sum(IN_WIDTHS)

kernel_name = 'hybrid_mla_rglru_decode_step'


def rmsnorm(x, w):
    xf = x.astype(jnp.float32)
    y = xf * lax.rsqrt(jnp.mean(xf * xf, axis=-1, keepdims=True) + EPS)
    return y.astype(x.dtype) * w


def rope_tables(pos):
    half = QK_ROPE // 2
    inv = jnp.power(ROPE_BASE, -jnp.arange(half, dtype=jnp.float32) / half)
    ang = pos.astype(jnp.float32)[:, None] * inv[None, :]
    return jnp.cos(ang), jnp.sin(ang)


def rope(x, cos, sin):
    x1, x2 = jnp.split(x.astype(jnp.float32), 2, axis=-1)
    return jnp.concatenate([x1 * cos - x2 * sin, x2 * cos + x1 * sin], axis=-1).astype(x.dtype)


def split_in(proj):
    idx = [int(v) for v in np.cumsum(IN_WIDTHS)[:-1]]
    return jnp.split(proj, idx, axis=-1)


def causal_conv(u, buf, w, b):
    s = u.shape[1]
    ext = jnp.concatenate([buf, u], axis=1)
    y = b + ext[:, 0:s] * w[0]
    for k in range(1, CONV_W):
        y = y + ext[:, k:k + s] * w[k]
    return y, ext[:, ext.shape[1] - (CONV_W - 1):]


def lru_combine(left, right):
    a1, b1 = left
    a2, b2 = right
    return a1 * a2, a2 * b1 + b2


def rg_lru(u, h0, wa, ba, wx, bx, lam):
    bsz, s, _ = u.shape
    ub = u.reshape(bsz, s, LRU_BLOCKS, LRU_BLOCK)
    r = jax.nn.sigmoid((jnp.einsum('bsnd,nde->bsne', ub, wa).reshape(bsz, s, LRU_WIDTH) + ba).astype(jnp.float32))
    i = jax.nn.sigmoid((jnp.einsum('bsnd,nde->bsne', ub, wx).reshape(bsz, s, LRU_WIDTH) + bx).astype(jnp.float32))
    log_a = -LRU_C * r * jax.nn.softplus(-lam.astype(jnp.float32))
    a = jnp.exp(log_a)
    b = jnp.sqrt(-jnp.expm1(2.0 * log_a)) * i * u.astype(jnp.float32)
    b = b.at[:, 0].add(a[:, 0] * h0.astype(jnp.float32))
    _, h = lax.associative_scan(lru_combine, (a, b), axis=1)
    return h.astype(u.dtype), h[:, -1].astype(h0.dtype)


def setup_inputs(seed: int = 0) -> dict:
    key = jax.random.key(seed)
    ks = jax.random.split(key, 32)
    f32 = jnp.float32
    n_pages = PAST_LEN // PAGE_SIZE
    n_used = DEC_BATCH * n_pages
    n_pool = n_used + n_used // 4
    nrm = lambda k, shape, sc: jax.random.normal(k, shape, f32) * sc
    page_table = jax.random.permutation(ks[0], n_pool)[:n_used].reshape(DEC_BATCH, n_pages).astype(jnp.int32)
    s8 = jax.random.uniform(ks[1], (DEPTH, LRU_WIDTH), f32, 0.9, 0.999) ** (1.0 / LRU_C)
    lru_lambda = jnp.log(s8) - jnp.log1p(-s8)
    return {
        'x_prompt': nrm(ks[2], (BATCH, SEQ, D_MODEL), 1.0),
        'x_sample': nrm(ks[3], (DEC_BATCH, DEC_SEQ, D_MODEL), 1.0),
        'c_prompt': nrm(ks[4], (BATCH, D_MODEL), 1.0),
        'c_sample': nrm(ks[5], (DEC_BATCH, D_MODEL), 1.0),
        'cache_ckv': nrm(ks[6], (DEPTH, n_pool, PAGE_SIZE, KV_LORA), 1.0),
        'cache_kpe': nrm(ks[7], (DEPTH, n_pool, PAGE_SIZE, QK_ROPE), 1.0),
        'state_conv': nrm(ks[8], (DEPTH, DEC_BATCH, CONV_W - 1, LRU_WIDTH), 1.0),
        'state_lru': nrm(ks[9], (DEPTH, DEC_BATCH, LRU_WIDTH), 0.5),
        'page_table': page_table,
        'ada_w': nrm(ks[10], (DEPTH, D_MODEL, 3 * D_MODEL), D_MODEL ** -0.5),
        'ada_b': nrm(ks[11], (DEPTH, 3 * D_MODEL), 0.02),
        'norm_w': 1.0 + nrm(ks[12], (DEPTH, D_MODEL), 0.05),
        'w_in': nrm(ks[13], (DEPTH, D_MODEL, IN_COLS), D_MODEL ** -0.5),
        'q_norm_w': 1.0 + nrm(ks[14], (DEPTH, Q_LORA), 0.05),
        'kv_norm_w': 1.0 + nrm(ks[15], (DEPTH, KV_LORA), 0.05),
        'w_uq': nrm(ks[16], (DEPTH, Q_LORA, N_HEADS, QK_NOPE + QK_ROPE), Q_LORA ** -0.5),
        'w_uk': nrm(ks[17], (DEPTH, KV_LORA, N_HEADS, QK_NOPE), KV_LORA ** -0.5),
        'w_uv': nrm(ks[18], (DEPTH, KV_LORA, N_HEADS, V_DIM), KV_LORA ** -0.5),
        'conv_w': nrm(ks[19], (DEPTH, CONV_W, LRU_WIDTH), CONV_W ** -0.5),
        'conv_b': nrm(ks[20], (DEPTH, LRU_WIDTH), 0.02),
        'lru_wa': nrm(ks[21], (DEPTH, LRU_BLOCKS, LRU_BLOCK, LRU_BLOCK), LRU_BLOCK ** -0.5),
        'lru_ba': nrm(ks[22], (DEPTH, LRU_WIDTH), 0.02),
        'lru_wx': nrm(ks[23], (DEPTH, LRU_BLOCKS, LRU_BLOCK, LRU_BLOCK), LRU_BLOCK ** -0.5),
        'lru_bx': nrm(ks[24], (DEPTH, LRU_WIDTH), 0.02),
        'lru_lambda': lru_lambda,
        'w_oa': nrm(ks[25], (DEPTH, MLA_WIDTH, D_MODEL), MLA_WIDTH ** -0.5),
        'w_ob': nrm(ks[26], (DEPTH, LRU_WIDTH, D_MODEL), LRU_WIDTH ** -0.5),
        'w_out': nrm(ks[27], (DEPTH, D_MODEL, D_MODEL), D_MODEL ** -0.5),
        'final_norm_w': 1.0 + nrm(ks[28], (D_MODEL,), 0.05),
    }


def reference(x_prompt, x_sample, c_prompt, c_sample, cache_ckv, cache_kpe, state_conv, state_lru, page_table,
              ada_w, ada_b, norm_w, w_in, q_norm_w, kv_norm_w, w_uq, w_uk, w_uv, conv_w, conv_b,
              lru_wa, lru_ba, lru_wx, lru_bx, lru_lambda, w_oa, w_ob, w_out, final_norm_w):
    f32 = jnp.float32

    def attend_prompt(l, q_nope, q_pe, c_kv, k_pe):
        bsz, s = q_nope.shape[0], q_nope.shape[1]
        k_nope = jnp.einsum('bsr,rhd->bshd', c_kv, w_uk[l])
        v = jnp.einsum('bsr,rhd->bshd', c_kv, w_uv[l])
        kpos = jnp.arange(s)

        def block(i):
            start = i * Q_BLOCK
            qn = lax.dynamic_slice_in_dim(q_nope, start, Q_BLOCK, axis=1)
            qp = lax.dynamic_slice_in_dim(q_pe, start, Q_BLOCK, axis=1)
            sc = jnp.einsum('bqhd,bkhd->bhqk', qn, k_nope) + jnp.einsum('bqhd,bkd->bhqk', qp, k_pe)
            sc = sc.astype(f32) * ATTN_SCALE
            qpos = start + jnp.arange(Q_BLOCK)
            sc = jnp.where(kpos[None, :] <= qpos[:, None], sc, -jnp.inf)
            p = jax.nn.softmax(sc, axis=-1).astype(v.dtype)
            return jnp.einsum('bhqk,bkhd->bqhd', p, v)

        o = lax.map(block, jnp.arange(s // Q_BLOCK))
        return jnp.moveaxis(o, 0, 1).reshape(bsz, s, MLA_WIDTH)

    def attend_sample(l, q_nope, q_pe, c_kv, k_pe):
        dbsz, t = q_nope.shape[0], q_nope.shape[1]
        past = page_table.shape[1] * PAGE_SIZE
        ckv_past = cache_ckv[l][page_table].reshape(dbsz, past, KV_LORA)
        kpe_past = cache_kpe[l][page_table].reshape(dbsz, past, QK_ROPE)
        q_abs = jnp.einsum('bthd,rhd->bthr', q_nope, w_uk[l])
        s_past = jnp.einsum('bthr,bpr->bhtp', q_abs, ckv_past) + jnp.einsum('bthd,bpd->bhtp', q_pe, kpe_past)
        s_new = jnp.einsum('bthr,bsr->bhts', q_abs, c_kv) + jnp.einsum('bthd,bsd->bhts', q_pe, k_pe)
        tpos = jnp.arange(t)
        s_new = jnp.where(tpos[None, :] <= tpos[:, None], s_new.astype(f32), -jnp.inf)
        sc = jnp.concatenate([s_past.astype(f32), s_new], axis=-1) * ATTN_SCALE
        p = jax.nn.softmax(sc, axis=-1).astype(c_kv.dtype)
        o_lat = jnp.einsum('bhtp,bpr->bthr', p[..., :past], ckv_past) + jnp.einsum('bhts,bsr->bthr', p[..., past:], c_kv)
        return jnp.einsum('bthr,rhd->bthd', o_lat, w_uv[l]).reshape(dbsz, t, MLA_WIDTH)

    def layer(l, x, c, pos, conv_buf, h0, attend):
        shift, scale, gate = jnp.split(jax.nn.silu(c) @ ada_w[l] + ada_b[l], 3, axis=-1)
        h = rmsnorm(x, norm_w[l]) * (1.0 + scale[:, None, :]) + shift[:, None, :]
        q_lat, kv_lat, k_pe, g_a, u, g_b, m_a, m_b = split_in(h @ w_in[l])
        cos, sin = rope_tables(pos)
        q = jnp.einsum('bsr,rhd->bshd', rmsnorm(q_lat, q_norm_w[l]), w_uq[l])
        q_nope = q[..., :QK_NOPE]
        q_pe = rope(q[..., QK_NOPE:], cos[:, None, :], sin[:, None, :])
        c_kv = rmsnorm(kv_lat, kv_norm_w[l])
        k_pe = rope(k_pe, cos, sin)
        attn = attend(l, q_nope, q_pe, c_kv, k_pe)
        y_a = (attn * jax.nn.silu(g_a)) @ w_oa[l]
        u_c, new_conv = causal_conv(u, conv_buf, conv_w[l], conv_b[l])
        lru, h_last = rg_lru(u_c, h0, lru_wa[l], lru_ba[l], lru_wx[l], lru_bx[l], lru_lambda[l])
        y_b = (lru * jax.nn.silu(g_b)) @ w_ob[l]
        merged = jax.nn.sigmoid(m_a) * y_a + jax.nn.sigmoid(m_b) * y_b
        x = x + gate[:, None, :] * (merged @ w_out[l])
        return x, c_kv, k_pe, new_conv, h_last

    bsz, s = x_prompt.shape[0], x_prompt.shape[1]
    dbsz, t = x_sample.shape[0], x_sample.shape[1]
    pos_prompt = jnp.arange(s)
    pos_sample = page_table.shape[1] * PAGE_SIZE + jnp.arange(t)

    xp, xs = x_prompt, x_sample
    ckv_p, kpe_p, conv_p, lru_p = [], [], [], []
    ckv_s, kpe_s, conv_s, lru_s = [], [], [], []
    for l in range(DEPTH):
        zeros_conv = jnp.zeros((bsz, CONV_W - 1, LRU_WIDTH), x_prompt.dtype)
        zeros_h = jnp.zeros((bsz, LRU_WIDTH), state_lru.dtype)
        xp, a1, a2, a3, a4 = layer(l, xp, c_prompt, pos_prompt, zeros_conv, zeros_h, attend_prompt)
        ckv_p.append(a1); kpe_p.append(a2); conv_p.append(a3); lru_p.append(a4)
        xs, b1, b2, b3, b4 = layer(l, xs, c_sample, pos_sample, state_conv[l], state_lru[l], attend_sample)
        ckv_s.append(b1); kpe_s.append(b2); conv_s.append(b3); lru_s.append(b4)

    y_prompt = rmsnorm(xp, final_norm_w)
    y_sample = rmsnorm(xs, final_norm_w)
    new_ckv_prompt = jnp.stack(ckv_p)
    new_kpe_prompt = jnp.stack(kpe_p)
    new_conv_prompt = jnp.stack(conv_p)
    new_lru_prompt = jnp.stack(lru_p)
    new_ckv_sample = jnp.stack(ckv_s)
    new_kpe_sample = jnp.stack(kpe_s)
    new_conv_sample = jnp.stack(conv_s)
    new_lru_sample = jnp.stack(lru_s)
    return (y_prompt, y_sample, new_ckv_prompt, new_kpe_prompt, new_conv_prompt, new_lru_prompt,
            new_ckv_sample, new_kpe_sample, new_conv_sample, new_lru_sample)
```

```python
import contextlib
import math
import numpy as np
import ml_dtypes
import concourse.bass as bass
import concourse.mybir as mybir
from concourse.bass_utils import run_bass_kernel_spmd

F32 = mybir.dt.float32
BF16 = mybir.dt.bfloat16
I32 = mybir.dt.int32
U8 = mybir.dt.uint8
AF = mybir.ActivationFunctionType
ALU = mybir.AluOpType

D = 1024
NH = 8
EPS = 1e-6
ATTN_SCALE = 1.0 / math.sqrt(96.0)
C_Q, C_KV, C_KPE, C_GA, C_U, C_GB, C_MA, C_MB = 0, 256, 512, 544, 1056, 1568, 2080, 3104


class Buf:
    __slots__ = ("name", "w", "r", "dsem", "dcnt")

    def __init__(self, name=""):
        self.name = name
        self.w = None
        self.r = []
        self.dsem = None
        self.dcnt = 0


class Trk:
    ENG = ("pe", "act", "dve", "pool", "sp")

    def __init__(self, nc):
        self.nc = nc
        self.es = contextlib.ExitStack()
        self.sems = {}
        self.cnt = {}
        self.seen = {e: {} for e in self.ENG}
        self.eng = {"pe": nc.tensor, "act": nc.scalar, "dve": nc.vector, "pool": nc.gpsimd, "sp": nc.sync}
        for e in self.ENG:
            self._mksem("E_" + e)
        self.nd = 0
        self.dsems = []
        self.free_dsems = {}
        self.semq = {}

    def _mksem(self, key):
        self.sems[key] = self.es.enter_context(self.nc.semaphore(key))
        self.cnt[key] = 0
        return key

    @staticmethod
    def _deps(reads, writes):
        deps = []
        for b in reads:
            if b.w is not None:
                deps.append(b.w)
        for b in writes:
            if b.w is not None:
                deps.append(b.w)
            deps.extend(b.r)
        return deps

    def _wait(self, e, deps, skip_self=False):
        best = {}
        for k, v in deps:
            if skip_self and k == "E_" + e:
                continue
            if v > best.get(k, 0):
                best[k] = v
        seen = self.seen[e]
        for k, v in best.items():
            if seen.get(k, 0) >= v:
                continue
            self.eng[e].wait_ge(self.sems[k], v)
            seen[k] = v

    def op(self, e, fn, reads=(), writes=()):
        deps = self._deps(reads, writes)
        self._wait(e, deps, skip_self=(e == "pe"))
        ins = fn(self.eng[e])
        k = "E_" + e
        self.cnt[k] += 1
        ins.then_inc(self.sems[k], 1)
        tok = (k, self.cnt[k])
        for b in reads:
            b.r.append(tok)
        for b in writes:
            b.w = tok
            b.r = []
        return tok

    def dma(self, q, out, in_, reads=(), writes=(), owner=None, **kw):
        deps = self._deps(reads, writes)
        self._wait(q, deps)
        if owner is None:
            owner = writes[0] if writes else reads[0]
        if owner.dsem is None:
            fl = self.free_dsems.setdefault(q, [])
            if fl:
                owner.dsem = fl.pop()
            else:
                self.nd += 1
                owner.dsem = self._mksem("D%s%d" % (q, self.nd))
                self.semq[owner.dsem] = q
            self.dsems.append(owner)
        ins = self.eng[q].dma_start(out=out, in_=in_, **kw)
        k = owner.dsem
        self.cnt[k] += 16
        ins.then_inc(self.sems[k], 16)
        tok = (k, self.cnt[k])
        for b in reads:
            b.r.append(tok)
        for b in writes:
            b.w = tok
            b.r = []
        return tok

    def wait_all(self, e, bufs):
        deps = []
        for b in bufs:
            if b.w is not None:
                deps.append(b.w)
            deps.extend(b.r)
        self._wait(e, deps)

    def barrier(self):
        deps = [("E_" + e, self.cnt["E_" + e]) for e in ("pe", "act", "dve", "pool") if self.cnt["E_" + e] > 0]
        used = sorted(set(b.dsem for b in self.dsems if b.dsem is not None))
        deps += [(k, self.cnt[k]) for k in used if self.cnt[k] > 0]
        for e in self.ENG:
            self._wait(e, deps)
        for b in self.dsems:
            b.dsem = None
        self.dsems = []
        for k in used:
            fl = self.free_dsems.setdefault(self.semq[k], [])
            if k not in fl:
                fl.append(k)

    def close(self):
        self.es.close()


class Ring:
    def __init__(self, items):
        self.items = items
        self.i = 0

    def get(self):
        it = self.items[self.i % len(self.items)]
        self.i += 1
        return it


class _Stop(Exception):
    pass


def build(SEQ, NP, NPOOL, stop=None):
    NT = SEQ // 512
    NS = NT // 2
    NKB = SEQ // 128
    NOWN = NS * 512
    nc = bass.Bass("TRN2", target_bir_lowering=False)
    tk = Trk(nc)

    def din(name, shape, dt=F32):
        return nc.dram_tensor(name, list(shape), dt, kind="ExternalInput").ap()

    def dout(name, shape, dt=F32):
        return nc.dram_tensor(name, list(shape), dt, kind="ExternalOutput").ap()

    xf = din("xf", [SEQ, D]); xo = din("xo", [NOWN, D]); xs = din("xs", [128, D])
    cp = din("cp", [1, D]); cs = din("cs", [16, D])
    HP = NPOOL // 2
    ckv0 = din("ckv0", [HP, 128, 256]); ckv1 = din("ckv1", [NPOOL - HP, 128, 256]); kpe = din("kpe", [NPOOL, 128, 32])
    sconv = din("sconv", [48, 512]); slru = din("slru", [16, 512])
    ptab = din("ptab", [16, NP], I32)
    ada_w = din("ada_w", [D, 3 * D]); ada_b = din("ada_b", [1, 3 * D]); norm_w = din("norm_w", [1, D])
    w_in = din("w_in", [D, 4128]); q_norm_w = din("q_norm_w", [1, 256]); kv_norm_w = din("kv_norm_w", [1, 256])
    w_uq = din("w_uq", [256, 768]); w_uk = din("w_uk", [256, 512]); w_uv = din("w_uv", [256, 512])
    conv_w = din("conv_w", [4, 512]); conv_b = din("conv_b", [1, 512])
    lru_wa = din("lru_wa", [8, 64, 64]); lru_ba = din("lru_ba", [1, 512])
    lru_wx = din("lru_wx", [8, 64, 64]); lru_bx = din("lru_bx", [1, 512]); lru_lambda = din("lru_lambda", [1, 512])
    w_oa = din("w_oa", [512, D]); w_ob = din("w_ob", [512, D]); w_out = din("w_out", [D, D]); fnw = din("fnw", [1, D])
    cosf = din("cosf", [SEQ, 16]); sinf = din("sinf", [SEQ, 16])
    coso = din("coso", [NOWN, 16]); sino = din("sino", [NOWN, 16])
    coss = din("coss", [128, 16]); sins = din("sins", [128, 16])
    cossq = din("cossq", [128, 16]); sinsq = din("sinsq", [128, 16])
    masks = din("masks", [2, 128, 8 * 512], BF16)
    smask = din("smask", [128, 16 * 64], BF16)

    yo = dout("yo", [NOWN, D]); ys = dout("ys", [128, D])
    ckvp = dout("ckvp", [SEQ, 256]); kpep = dout("kpep", [SEQ, 32])
    convp = dout("convp", [3, 512]); lrup = dout("lrup", [1, 512])
    ckvs = dout("ckvs", [128, 256]); kpes = dout("kpes", [128, 32])
    convs = dout("convs", [48, 512]); lrus = dout("lrus", [16, 512])

    lru_scr = nc.dram_tensor("lru_scr", [NT, 128, 4 * 512], BF16, kind="Internal").ap()
    qt_scr = nc.dram_tensor("qt_scr", [NH, 96, NOWN], BF16, kind="Internal").ap()
    lru_scr_b = lru_scr.bitcast(U8).rearrange("t p n -> (t p n)")
    ckv0_b = ckv0.bitcast(U8).rearrange("n p d -> (n p d)")
    ckv1_b = ckv1.bitcast(U8).rearrange("n p d -> (n p d)")
    kpe_b = kpe.bitcast(U8).rearrange("n p d -> (n p d)")
    b_lru_scr = [Buf("lruscr%d" % i) for i in range(NT)]
    b_qt_scr = [Buf("qtscr%d" % i) for i in range(NS)]
    own_tab = din("own_tab", [1, 16], I32)
    ao_scr = nc.dram_tensor("ao_scr", [128, 4, NOWN], BF16, kind="Internal").ap()
    b_ao_scr = Buf("ao_scr")

    out_bufs = []

    top = contextlib.ExitStack()

    uid = {"n": 0}

    def sbt(es, name, shape, dt):
        uid["n"] += 1
        return es.enter_context(nc.sbuf_tensor("%s_%d" % (name, uid["n"]), list(shape), dt))

    def pst(es, name, shape, dt):
        uid["n"] += 1
        return es.enter_context(nc.psum_tensor("%s_%d" % (name, uid["n"]), list(shape), dt))

    def ring_sb(es, name, n, shape, dt):
        return Ring([(sbt(es, "%s%d" % (name, i), shape, dt), Buf("%s%d" % (name, i))) for i in range(n)])

    def ring_ps(es, name, n, shape, dt):
        return Ring([(pst(es, "%s%d" % (name, i), shape, dt), Buf("%s%d" % (name, i))) for i in range(n)])

    rr = {"n": 0}

    def alt(*engs):
        rr["n"] += 1
        return engs[rr["n"] % len(engs)]

    def copy_op(e, out, in_, reads, writes, scale=None):
        if e == "act":
            if scale is None:
                return tk.op("act", lambda g: g.activation(out=out, in_=in_, func=AF.Copy), reads, writes)
            return tk.op("act", lambda g: g.activation(out=out, in_=in_, func=AF.Copy, scale=float(scale)), reads, writes)
        if scale is None:
            return tk.op(e, lambda g: g.tensor_copy(out=out, in_=in_), reads, writes)
        return tk.op(e, lambda g: g.tensor_scalar(out=out, in0=in_, scalar1=float(scale), scalar2=None, op0=ALU.mult), reads, writes)

    with contextlib.suppress(_Stop):
        ident = sbt(top, "ident", [128, 128], BF16); b_ident = Buf("ident")
        identf = sbt(top, "identf", [128, 128], F32); b_identf = Buf("identf")
        ones32 = sbt(top, "ones32", [128, 128], F32); b_ones = Buf("ones32")
        Ap = sbt(top, "Ap", [128, 8], F32); Bp = sbt(top, "Bp", [128, 8], F32); b_mod = Buf("mod")
        As = sbt(top, "As", [128, 8, 16], F32); Bs = sbt(top, "Bs", [128, 8, 16], F32)
        Gp = sbt(top, "Gp", [128, D], F32); Gs = sbt(top, "Gs", [128, D], F32); b_G = Buf("G")
        fnw_bc = sbt(top, "fnw_bc", [128, D], F32); b_fnw = Buf("fnw")
        qnw_bc = sbt(top, "qnw_bc", [128, 256], F32); kvw_bc = sbt(top, "kvw_bc", [128, 256], F32); b_nw = Buf("nw")
        vecT = sbt(top, "vecT", [128, 72], F32); b_vecT = Buf("vecT")
        V_ADAB, V_NW, V_CW, V_CB, V_BA, V_BX, V_LAM, V_CP = 0, 24, 32, 48, 52, 56, 60, 64
        lcoef = sbt(top, "lcoef", [128, 16], F32); b_lcoef = Buf("lcoef")
        ptab_sb = sbt(top, "ptab_sb", [16, NP], I32); b_ptab = Buf("ptab")
        own_sb = sbt(top, "own_sb", [1, 16], I32); b_own = Buf("own")

        R = {n_: top.enter_context(nc.sync.register("r_" + n_)) for n_ in ("pg", "c0", "c1", "o0", "o1", "t", "ok")}
        RV = bass.RuntimeValue.of_register_unchecked

        def dyn_off(idx_ap, mult):
            nc.sync.reg_load(R["pg"], idx_ap)
            nc.sync.reg_alu(R["o0"], R["pg"], int(mult), ALU.mult)
            return RV(R["o0"])

        def page_offsets(idx_ap):
            sp = nc.sync
            sp.reg_load(R["pg"], idx_ap)
            sp.reg_alu(R["c0"], R["pg"], HP, ALU.is_lt)
            sp.reg_alu(R["c1"], R["pg"], HP, ALU.is_ge)
            sp.reg_alu(R["o0"], R["pg"], 128 * 1024, ALU.mult)
            sp.reg_alu(R["t"], R["pg"], HP, ALU.subtract)
            sp.reg_alu(R["o1"], R["t"], 128 * 1024, ALU.mult)
            for o_, c_ in (("o0", "c0"), ("o1", "c1")):
                sp.reg_alu(R[o_], R[o_], R[c_], ALU.mult)
                sp.reg_alu(R[o_], R[o_], R[c_], ALU.add)
                sp.reg_alu(R[o_], R[o_], 1, ALU.subtract)
            sp.reg_alu(R["ok"], R["pg"], 128 * 128, ALU.mult)
            return RV(R["o0"]), RV(R["o1"]), RV(R["ok"])

        tk.op("pool", lambda g: g.memset(identf[:], 1.0), writes=[b_identf])
        tk.op("pool", lambda g: g.affine_select(out=identf[:], in_=identf[:], pattern=[[-1, 128]], compare_op=ALU.is_equal,
                                                 fill=0.0, base=0, channel_multiplier=1), reads=[b_identf], writes=[b_identf])
        tk.op("dve", lambda g: g.tensor_copy(out=ident[:], in_=identf[:]), reads=[b_identf], writes=[b_ident])
        tk.op("pool", lambda g: g.memset(ones32[:], 1.0), writes=[b_ones])
        tk.dma("sp", ptab_sb[:], ptab, writes=[b_ptab])
        tk.dma("sp", own_sb[:], own_tab, writes=[b_own])
        tk.dma("sp", fnw_bc[:], fnw.partition_broadcast(128), writes=[b_fnw])
        tk.dma("sp", qnw_bc[:], q_norm_w.partition_broadcast(128), writes=[b_nw])
        tk.dma("sp", kvw_bc[:], kv_norm_w.partition_broadcast(128), writes=[b_nw])

        with contextlib.ExitStack() as es:
            stage = sbt(es, "stage", [128, 128], F32); b_stage = Buf("stage")
            adaw = sbt(es, "adaw", [128, 8, 3 * D], BF16); b_adaw = Buf("adaw")
            cs_sb = sbt(es, "cs_sb", [16, D], F32); b_cs = Buf("cs")
            csT = sbt(es, "csT", [128, 8, 16], F32); b_csT = Buf("csT")
            scT = sbt(es, "scT", [128, 8, 17], BF16); b_scT = Buf("scT")
            scTp = sbt(es, "scTp", [128, 8, 128], BF16); b_scTp = Buf("scTp")
            scTs = sbt(es, "scTs", [128, 8, 128], BF16); b_scTs = Buf("scTs")
            adab_bc = sbt(es, "adab_bc", [128, D], F32); b_adab = Buf("adab")
            modT = sbt(es, "modT", [128, 24, 17], F32); b_modT = Buf("modT")
            tmp0 = sbt(es, "tmp0", [128, 8, 17], F32); b_tmp0 = Buf("tmp0")
            tmp1 = sbt(es, "tmp1", [128, 8, 17], F32); b_tmp1 = Buf("tmp1")
            sm = [sbt(es, "sm%d" % i, [128, 4], F32) for i in range(6)]; b_sm = [Buf("sm%d" % i) for i in range(6)]
            psA = pst(es, "psA", [128, 512], F32); b_psA = Buf("psA")
            psB = pst(es, "psB", [128, 512], F32); b_psB = Buf("psB")
            psC = pst(es, "psC", [128, 512], F32); b_psC = Buf("psC")

            tk.op("pool", lambda g: g.memset(stage[:], 0.0), writes=[b_stage])
            rows = [(V_ADAB, 24, ada_b), (V_NW, 8, norm_w), (V_CB, 4, conv_b), (V_BA, 4, lru_ba), (V_BX, 4, lru_bx),
                    (V_LAM, 4, lru_lambda), (V_CP, 8, cp)]
            for r0, n, src in rows:
                tk.dma("sp", stage[r0:r0 + n, :], src.rearrange("o (j p) -> (o j) p", p=128), writes=[b_stage])
            tk.dma("sp", stage[V_CW:V_CW + 16, :], conv_w.rearrange("k (j p) -> (k j) p", p=128), writes=[b_stage])
            tk.dma("sp", cs_sb[:], cs, writes=[b_cs])
            tk.dma("sp", adab_bc[:], ada_b[:, 2 * D:3 * D].partition_broadcast(128), writes=[b_adab])
            for k in range(8):
                tk.dma("pool", adaw[:, k, :], ada_w[k * 128:(k + 1) * 128, :], writes=[b_adaw])
            tk.op("pe", lambda g: g.transpose(out=psA[:, 0:72], in_=stage[0:72, :], identity=identf[0:72, 0:72]),
                  reads=[b_stage, b_identf], writes=[b_psA])
            tk.op("dve", lambda g: g.tensor_copy(out=vecT[:], in_=psA[:, 0:72]), reads=[b_psA], writes=[b_vecT])
            for k in range(8):
                tk.op("pe", lambda g, k=k: g.transpose(out=psB[:, k * 16:(k + 1) * 16], in_=cs_sb[:, k * 128:(k + 1) * 128],
                                                        identity=identf[0:16, 0:16]), reads=[b_cs, b_identf], writes=[b_psB])
            tk.op("dve", lambda g: g.tensor_copy(out=csT[:], in_=psB[:, 0:128].rearrange("p (k b) -> p k b", k=8)),
                  reads=[b_psB], writes=[b_csT])
            tk.op("dve", lambda g: g.tensor_copy(out=tmp0[:, :, 0:1], in_=vecT[:, V_CP:V_CP + 8].unsqueeze(2)),
                  reads=[b_vecT], writes=[b_tmp0])
            tk.op("dve", lambda g: g.tensor_copy(out=tmp0[:, :, 1:17], in_=csT[:]), reads=[b_csT], writes=[b_tmp0])
            tk.op("act", lambda g: g.activation(out=tmp1[:], in_=tmp0[:], func=AF.Tanh, scale=0.5), reads=[b_tmp0], writes=[b_tmp1])
            tk.op("dve", lambda g: g.scalar_tensor_tensor(out=tmp1[:], in0=tmp1[:], scalar=1.0, in1=tmp0[:], op0=ALU.add, op1=ALU.mult),
                  reads=[b_tmp1, b_tmp0], writes=[b_tmp1])
            tk.op("dve", lambda g: g.tensor_scalar(out=scT[:], in0=tmp1[:], scalar1=0.5, scalar2=None, op0=ALU.mult),
                  reads=[b_tmp1], writes=[b_scT])
            tk.op("dve", lambda g: g.tensor_copy(out=scTp[:], in_=scT[:, :, 0:1].to_broadcast([128, 8, 128])),
                  reads=[b_scT], writes=[b_scTp])
            for k in range(8):
                tk.op("dve", lambda g, k=k: g.tensor_copy(out=scTs[:, k, :].rearrange("p (b t) -> p b t", t=8),
                                                           in_=scT[:, k, 1:17].unsqueeze(2).to_broadcast([128, 16, 8])),
                      reads=[b_scT], writes=[b_scTs])
            for j in range(16):
                for k in range(8):
                    tk.op("pe", lambda g, j=j, k=k: g.matmul(psC[:, j * 17:(j + 1) * 17], lhsT=adaw[:, k, j * 128:(j + 1) * 128],
                                                              rhs=scT[:, k, :], start=(k == 0), stop=(k == 7)),
                          reads=[b_adaw, b_scT], writes=[b_psC])
            tk.op("dve", lambda g: g.tensor_tensor(out=modT[:, 0:16, :], in0=psC[:, 0:272].rearrange("p (j c) -> p j c", c=17),
                                                   in1=vecT[:, V_ADAB:V_ADAB + 16].unsqueeze(2).to_broadcast([128, 16, 17]), op=ALU.add),
                  reads=[b_psC, b_vecT], writes=[b_modT])
            tk.op("dve", lambda g: g.scalar_tensor_tensor(out=Ap[:], in0=modT[:, 8:16, 0], scalar=1.0, in1=vecT[:, V_NW:V_NW + 8],
                                                          op0=ALU.add, op1=ALU.mult), reads=[b_modT, b_vecT], writes=[b_mod])
            tk.op("dve", lambda g: g.tensor_copy(out=Bp[:], in_=modT[:, 0:8, 0]), reads=[b_modT], writes=[b_mod])
            tk.op("dve", lambda g: g.scalar_tensor_tensor(out=As[:], in0=modT[:, 8:16, 1:17], scalar=1.0,
                                                          in1=vecT[:, V_NW:V_NW + 8].unsqueeze(2).to_broadcast([128, 8, 16]),
                                                          op0=ALU.add, op1=ALU.mult), reads=[b_modT, b_vecT], writes=[b_mod])
            tk.op("dve", lambda g: g.tensor_copy(out=Bs[:], in_=modT[:, 0:8, 1:17]), reads=[b_modT], writes=[b_mod])
            for (sct, b_sct, Gt) in ((scTp, b_scTp, Gp), (scTs, b_scTs, Gs)):
                for n in range(2):
                    ps_, b_ps_ = (psA, b_psA) if n == 0 else (psB, b_psB)
                    for k in range(8):
                        tk.op("pe", lambda g, k=k, n=n, ps_=ps_, sct=sct: g.matmul(
                            ps_[:], lhsT=sct[:, k, :], rhs=adaw[:, k, 2 * D + n * 512:2 * D + (n + 1) * 512],
                            start=(k == 0), stop=(k == 7)), reads=[b_adaw, b_sct], writes=[b_ps_])
                    tk.op("dve", lambda g, n=n, ps_=ps_, Gt=Gt: g.tensor_tensor(
                        out=Gt[:, n * 512:(n + 1) * 512], in0=ps_[:], in1=adab_bc[:, n * 512:(n + 1) * 512], op=ALU.add),
                        reads=[b_ps_, b_adab], writes=[b_G])
                tk.op("dve", lambda g, Gt=Gt: g.tensor_scalar(out=Gt[:], in0=Gt[:], scalar1=0.25, scalar2=None, op0=ALU.mult),
                      reads=[b_G], writes=[b_G])
            z, w, w2, acc, t_, c_ = sm
            lam = vecT[:, V_LAM:V_LAM + 4]
            tk.op("act", lambda g: g.activation(out=z[:], in_=lam, func=AF.Exp, scale=-1.0), reads=[b_vecT], writes=[b_sm[0]])
            tk.op("dve", lambda g: g.tensor_scalar(out=w[:], in0=z[:], scalar1=2.0, scalar2=None, op0=ALU.add), reads=[b_sm[0]], writes=[b_sm[1]])
            tk.op("dve", lambda g: g.reciprocal(out=w[:], in_=w[:]), reads=[b_sm[1]], writes=[b_sm[1]])
            tk.op("dve", lambda g: g.tensor_tensor(out=w[:], in0=w[:], in1=z[:], op=ALU.mult), reads=[b_sm[1], b_sm[0]], writes=[b_sm[1]])
            tk.op("dve", lambda g: g.tensor_tensor(out=w2[:], in0=w[:], in1=w[:], op=ALU.mult), reads=[b_sm[1]], writes=[b_sm[2]])
            tk.op("dve", lambda g: g.memset(acc[:], 1.0 / 13.0), writes=[b_sm[3]])
            for coef in (1.0 / 11, 1.0 / 9, 1.0 / 7, 1.0 / 5, 1.0 / 3, 1.0):
                tk.op("dve", lambda g: g.tensor_tensor(out=acc[:], in0=acc[:], in1=w2[:], op=ALU.mult), reads=[b_sm[3], b_sm[2]], writes=[b_sm[3]])
                tk.op("dve", lambda g, coef=coef: g.tensor_scalar(out=acc[:], in0=acc[:], scalar1=float(coef), scalar2=None, op0=ALU.add),
                      reads=[b_sm[3]], writes=[b_sm[3]])
            tk.op("dve", lambda g: g.tensor_tensor(out=acc[:], in0=acc[:], in1=w[:], op=ALU.mult), reads=[b_sm[3], b_sm[1]], writes=[b_sm[3]])
            tk.op("dve", lambda g: g.tensor_scalar(out=lcoef[:, 0:4], in0=acc[:], scalar1=-8.0, scalar2=None, op0=ALU.mult), reads=[b_sm[3]], writes=[b_lcoef])
            tk.op("dve", lambda g: g.tensor_scalar(out=lcoef[:, 4:8], in0=acc[:], scalar1=-16.0, scalar2=None, op0=ALU.mult), reads=[b_sm[3]], writes=[b_lcoef])
            tk.op("dve", lambda g: g.tensor_scalar(out=lcoef[:, 8:12], in0=vecT[:, V_BA:V_BA + 4], scalar1=0.5, scalar2=None, op0=ALU.mult), reads=[b_vecT], writes=[b_lcoef])
            tk.op("dve", lambda g: g.tensor_scalar(out=lcoef[:, 12:16], in0=vecT[:, V_BX:V_BX + 4], scalar1=0.5, scalar2=None, op0=ALU.mult), reads=[b_vecT], writes=[b_lcoef])
            tk.barrier()
        if stop == "0":
            raise _Stop()

        def load_x(es_ring, src_ap, NSB):
            xt, b_xt = es_ring.get()
            tk.dma("sp", xt[:, 0:NSB, :], src_ap.rearrange("(s p) d -> p s d", p=128), writes=[b_xt])
            return xt, b_xt

        def norm_hT(xt, b_xt, NSB, xn, b_xn, hT, b_hT, tp_ring, st, b_st, sample):
            N = NSB * 128
            tk.op("dve", lambda g: g.memset(st[:, 0:4], 0.0), writes=[b_st])
            for s in range(NSB):
                tk.op("act", lambda g, s=s: g.activation(out=xn[:, s, :], in_=xt[:, s, :], func=AF.Square, accum_out=st[:, s:s + 1]),
                      reads=[b_xt], writes=[b_xn, b_st])
            tk.op("dve", lambda g: g.tensor_scalar(out=st[:, 4:4 + NSB], in0=st[:, 0:NSB], scalar1=1.0 / D, scalar2=EPS, op0=ALU.mult, op1=ALU.add),
                  reads=[b_st], writes=[b_st])
            tk.op("act", lambda g: g.activation(out=st[:, 4:4 + NSB], in_=st[:, 4:4 + NSB], func=AF.Sqrt), reads=[b_st], writes=[b_st])
            tk.op("dve", lambda g: g.reciprocal(out=st[:, 4:4 + NSB], in_=st[:, 4:4 + NSB]), reads=[b_st], writes=[b_st])
            for s in range(NSB):
                e = "act" if s % 2 == 0 else "pool"
                if e == "act":
                    tk.op("act", lambda g, s=s: g.activation(out=xn[:, s, :], in_=xt[:, s, :], func=AF.Copy, scale=st[:, 4 + s:5 + s]),
                          reads=[b_xt, b_st], writes=[b_xn])
                else:
                    tk.op("dve", lambda g, s=s: g.tensor_scalar(out=xn[:, s, :], in0=xt[:, s, :], scalar1=st[:, 4 + s:5 + s], scalar2=None, op0=ALU.mult),
                          reads=[b_xt, b_st], writes=[b_xn])
            for k in range(8):
                tp, b_tp = tp_ring.get()
                for s in range(NSB):
                    tk.op("pe", lambda g, s=s, k=k, tp=tp: g.transpose(out=tp[:, s * 128:(s + 1) * 128], in_=xn[:, s, k * 128:(k + 1) * 128], identity=ident[:]),
                          reads=[b_xn, b_ident], writes=[b_tp])
                if not sample:
                    if k % 2 == 0:
                        tk.op("act", lambda g, k=k, tp=tp: g.activation(out=hT[:, k, 0:N], in_=tp[:, 0:N], func=AF.Identity,
                                                                         scale=Ap[:, k:k + 1], bias=Bp[:, k:k + 1]),
                              reads=[b_tp, b_mod], writes=[b_hT])
                    else:
                        tk.op("dve", lambda g, k=k, tp=tp: g.tensor_scalar(out=hT[:, k, 0:N], in0=tp[:, 0:N], scalar1=Ap[:, k:k + 1],
                                                                            scalar2=Bp[:, k:k + 1], op0=ALU.mult, op1=ALU.add),
                              reads=[b_tp, b_mod], writes=[b_hT])
                else:
                    tk.op("dve", lambda g, k=k, tp=tp: g.tensor_tensor(out=hT[:, k, 0:128].rearrange("p (b t) -> p b t", t=8),
                                                                        in0=tp[:, 0:128].rearrange("p (b t) -> p b t", t=8),
                                                                        in1=As[:, k, :].unsqueeze(2).to_broadcast([128, 16, 8]), op=ALU.mult),
                          reads=[b_tp, b_mod], writes=[b_hT])
                    tk.op("dve", lambda g, k=k: g.tensor_tensor(out=hT[:, k, 0:128].rearrange("p (b t) -> p b t", t=8),
                                                                  in0=hT[:, k, 0:128].rearrange("p (b t) -> p b t", t=8),
                                                                  in1=Bs[:, k, :].unsqueeze(2).to_broadcast([128, 16, 8]), op=ALU.add),
                          reads=[b_hT, b_mod], writes=[b_hT])

        def rms_free(pm, b_pm, width, s, st, b_st, col, junk, b_junk):
            tk.op("dve", lambda g: g.memset(st[:, col:col + 1], 0.0), writes=[b_st])
            tk.op("act", lambda g: g.activation(out=junk[:, 0:width], in_=pm[:, 0:width], func=AF.Square, accum_out=st[:, col:col + 1]),
                  reads=[b_pm], writes=[b_junk, b_st])
            tk.op("dve", lambda g: g.tensor_scalar(out=st[:, col:col + 1], in0=st[:, col:col + 1], scalar1=1.0 / width, scalar2=EPS,
                                                   op0=ALU.mult, op1=ALU.add), reads=[b_st], writes=[b_st])
            tk.op("act", lambda g: g.activation(out=st[:, col:col + 1], in_=st[:, col:col + 1], func=AF.Sqrt), reads=[b_st], writes=[b_st])
            tk.op("dve", lambda g: g.reciprocal(out=st[:, col:col + 1], in_=st[:, col:col + 1]), reads=[b_st], writes=[b_st])

        def rope_tm(e, src4, cosb, sinb, out_lo, out_hi, tc, ts, reads, b_tc, writes):
            tk.op(e, lambda g: g.tensor_tensor(out=tc, in0=src4, in1=cosb, op=ALU.mult), reads=reads, writes=[b_tc])
            tk.op(e, lambda g: g.tensor_tensor(out=ts, in0=src4, in1=sinb, op=ALU.mult), reads=reads, writes=[b_tc])
            tk.op(e, lambda g: g.tensor_tensor(out=out_lo, in0=tc[:, :, 0, :], in1=ts[:, :, 1, :], op=ALU.subtract), reads=[b_tc], writes=writes)
            tk.op(e, lambda g: g.tensor_tensor(out=out_hi, in0=tc[:, :, 1, :], in1=ts[:, :, 0, :], op=ALU.add), reads=[b_tc], writes=writes)

        def load_w(es, name, src_ap, shape, eng="pool", b=None):
            t = sbt(es, name, shape, BF16)
            if b is None:
                b = Buf(name)
            K = shape[1]
            for k in range(K):
                tk.dma(eng, t[:, k, :], src_ap[k * 128:(k + 1) * 128, :], writes=[b])
            return t, b

        def kside_tile(NSB, hT, b_hT, Wkv, b_Wkv, mm_ring, tp_ring, cos_t, sin_t, b_cs_t, ckv32, b_ckv32, kpe32, b_kpe32,
                       ckv16, b_ckv16, kpe16, b_kpe16, st, b_st, junk, b_junk, tc, ts, b_tc,
                       ckvT_dst, b_ckvT, kpeT_dst, b_kpeT, out_ckv, out_kpe, kpe_shift):
            for s in range(NSB):
                pm, b_pm = mm_ring.get()
                for k in range(8):
                    tk.op("pe", lambda g, s=s, k=k, pm=pm: g.matmul(pm[:, 0:288], lhsT=hT[:, k, s * 128:(s + 1) * 128], rhs=Wkv[:, k, :],
                                                                    start=(k == 0), stop=(k == 7)), reads=[b_hT, b_Wkv], writes=[b_pm])
                rms_free(pm, b_pm, 256, s, st, b_st, s, junk, b_junk)
                tk.op("dve", lambda g, s=s, pm=pm: g.scalar_tensor_tensor(out=ckv32[:, s, :], in0=pm[:, 0:256], scalar=st[:, s:s + 1], in1=kvw_bc[:],
                                                                         op0=ALU.mult, op1=ALU.mult), reads=[b_pm, b_st, b_nw], writes=[b_ckv32])
                tk.op("act", lambda g, s=s, pm=pm: g.activation(out=kpe32[:, s, :], in_=pm[:, 256:288], func=AF.Copy), reads=[b_pm], writes=[b_kpe32])
            tk.op("act", lambda g: g.activation(out=ckv16[:, 0:NSB, :], in_=ckv32[:, 0:NSB, :], func=AF.Copy), reads=[b_ckv32], writes=[b_ckv16])
            src4 = kpe32[:, 0:NSB, :].rearrange("p s (h j) -> p s h j", h=2)
            cosb = cos_t[:, 0:NSB, :].unsqueeze(2).to_broadcast([128, NSB, 2, 16])
            sinb = sin_t[:, 0:NSB, :].unsqueeze(2).to_broadcast([128, NSB, 2, 16])
            rope_tm("pool", src4, cosb, sinb, kpe32[:, 0:NSB, 0:16], kpe32[:, 0:NSB, 16:32], tc[:, 0:NSB], ts[:, 0:NSB],
                    [b_kpe32, b_cs_t], b_tc, [b_kpe32])
            kc0 = 64 if kpe_shift else 0
            tk.op("pool", lambda g: g.tensor_copy(out=kpe16[:, 0:NSB, kc0:kc0 + 32], in_=kpe32[:, 0:NSB, :]), reads=[b_kpe32], writes=[b_kpe16])
            out_bufs.append(b_ckv32); out_bufs.append(b_kpe32)
            tk.dma("sp", out_ckv.rearrange("(s p) d -> p s d", p=128), ckv32[:, 0:NSB, :], reads=[b_ckv32])
            tk.dma("sp", out_kpe.rearrange("(s p) d -> p s d", p=128), kpe32[:, 0:NSB, :], reads=[b_kpe32])
            for c in range(2):
                tp, b_tp = tp_ring.get()
                for s in range(NSB):
                    tk.op("pe", lambda g, s=s, c=c, tp=tp: g.transpose(out=tp[:, s * 128:(s + 1) * 128], in_=ckv16[:, s, c * 128:(c + 1) * 128], identity=ident[:]),
                          reads=[b_ckv16, b_ident], writes=[b_tp])
                copy_op(alt("act", "dve"), ckvT_dst(c), tp[:, 0:NSB * 128], [b_tp], [b_ckvT])
            tp, b_tp = tp_ring.get()
            kw_ = kc0 + 32
            for s in range(NSB):
                tk.op("pe", lambda g, s=s, tp=tp: g.transpose(out=tp[0:kw_, s * 128:(s + 1) * 128], in_=kpe16[:, s, 0:kw_], identity=ident[:]),
                      reads=[b_kpe16, b_ident], writes=[b_tp])
            copy_op("dve", kpeT_dst, tp[kc0:kc0 + 32, 0:NSB * 128], [b_tp], [b_kpeT])

        def lru_tile(N, hT, b_hT, Wu, b_Wu, Wa, Wx, b_Wg, mm_ring, tmp_ring, uext, b_uext, hist_fn, uc32, b_uc32, uc16, b_uc16,
                     hstate, b_hstate, h16, b_h16, fix_fn=None):
            hist_fn()
            for c in range(4):
                pm, b_pm = mm_ring.get()
                for k in range(8):
                    tk.op("pe", lambda g, c=c, k=k, pm=pm: g.matmul(pm[:, 0:N], lhsT=Wu[:, k, c * 128:(c + 1) * 128], rhs=hT[:, k, 0:N],
                                                                    start=(k == 0), stop=(k == 7)), reads=[b_hT, b_Wu], writes=[b_pm])
                tk.op("act", lambda g, c=c, pm=pm: g.activation(out=uext(c, 3, N), in_=pm[:, 0:N], func=AF.Copy), reads=[b_pm], writes=[b_uext])
            for c in range(4):
                e = "dve"
                tk.op(e, lambda g, c=c: g.tensor_scalar(out=uc32[:, c, 0:N], in0=uext(c, 0, N), scalar1=vecT[:, V_CW + c:V_CW + c + 1],
                                                        scalar2=vecT[:, V_CB + c:V_CB + c + 1], op0=ALU.mult, op1=ALU.add),
                      reads=[b_uext, b_vecT], writes=[b_uc32])
                for kk in range(1, 4):
                    tk.op(e, lambda g, c=c, kk=kk: g.scalar_tensor_tensor(out=uc32[:, c, 0:N], in0=uext(c, kk, N),
                                                                          scalar=vecT[:, V_CW + 4 * kk + c:V_CW + 4 * kk + c + 1],
                                                                          in1=uc32[:, c, 0:N], op0=ALU.mult, op1=ALU.add),
                          reads=[b_uext, b_vecT, b_uc32], writes=[b_uc32])
            tk.op("act", lambda g: g.activation(out=uc16[:, :, 0:N], in_=uc32[:, :, 0:N], func=AF.Copy), reads=[b_uc32], writes=[b_uc16])
            for c in range(4):
                pr, b_pr = mm_ring.get()
                tk.op("pe", lambda g, c=c, pr=pr: g.matmul(pr[:, 0:N], lhsT=Wa[:, c, :], rhs=uc16[:, c, 0:N], start=True, stop=True),
                      reads=[b_Wg, b_uc16], writes=[b_pr])
                pi, b_pi = mm_ring.get()
                tk.op("pe", lambda g, c=c, pi=pi: g.matmul(pi[:, 0:N], lhsT=Wx[:, c, :], rhs=uc16[:, c, 0:N], start=True, stop=True),
                      reads=[b_Wg, b_uc16], writes=[b_pi])
                thr, b_thr = tmp_ring.get()
                tk.op("act", lambda g, c=c, pr=pr, thr=thr: g.activation(out=thr[:, 0:N], in_=pr[:, 0:N], func=AF.Tanh, scale=0.5, bias=lcoef[:, 8 + c:9 + c]),
                      reads=[b_pr, b_lcoef], writes=[b_thr])
                thi, b_thi = tmp_ring.get()
                tk.op("act", lambda g, c=c, pi=pi, thi=thi: g.activation(out=thi[:, 0:N], in_=pi[:, 0:N], func=AF.Tanh, scale=0.5, bias=lcoef[:, 12 + c:13 + c]),
                      reads=[b_pi, b_lcoef], writes=[b_thi])
                a_, b_a = tmp_ring.get()
                tk.op("act", lambda g, c=c, thr=thr, a_=a_: g.activation(out=a_[:, 0:N], in_=thr[:, 0:N], func=AF.Exp, scale=lcoef[:, c:c + 1], bias=lcoef[:, c:c + 1]),
                      reads=[b_thr, b_lcoef], writes=[b_a])
                a2, b_a2 = tmp_ring.get()
                tk.op("act", lambda g, c=c, thr=thr, a2=a2: g.activation(out=a2[:, 0:N], in_=thr[:, 0:N], func=AF.Exp, scale=lcoef[:, 4 + c:5 + c], bias=lcoef[:, 4 + c:5 + c]),
                      reads=[b_thr, b_lcoef], writes=[b_a2])
                e = "dve"
                tk.op("dve", lambda g, a2=a2: g.tensor_scalar(out=a2[:, 0:N], in0=a2[:, 0:N], scalar1=1.0, scalar2=-1.0, op0=ALU.min, op1=ALU.mult),
                      reads=[b_a2], writes=[b_a2])
                tk.op("act", lambda g, a2=a2: g.activation(out=a2[:, 0:N], in_=a2[:, 0:N], func=AF.Sqrt, bias=1.0), reads=[b_a2], writes=[b_a2])
                tk.op(e, lambda g, c=c, thi=thi: g.scalar_tensor_tensor(out=thi[:, 0:N], in0=thi[:, 0:N], scalar=1.0, in1=uc32[:, c, 0:N], op0=ALU.add, op1=ALU.mult),
                      reads=[b_thi, b_uc32], writes=[b_thi])
                tk.op(e, lambda g, thi=thi, a2=a2: g.scalar_tensor_tensor(out=thi[:, 0:N], in0=thi[:, 0:N], scalar=0.5, in1=a2[:, 0:N], op0=ALU.mult, op1=ALU.mult),
                      reads=[b_thi, b_a2], writes=[b_thi])
                if fix_fn is not None:
                    fix_fn(c, a_, b_a, thi, b_thi)
                    init = 0.0
                    rd = [b_a, b_thi]
                else:
                    init = hstate[:, c:c + 1]
                    rd = [b_a, b_thi, b_hstate]
                h32, b_h32 = tmp_ring.get()
                tk.op("dve", lambda g, a_=a_, thi=thi, h32=h32, init=init: g.tensor_tensor_scan(out=h32[:, 0:N], data0=a_[:, 0:N], data1=thi[:, 0:N], initial=init,
                                                                                             op0=ALU.mult, op1=ALU.add), reads=rd, writes=[b_h32])
                tk.op("pool", lambda g, c=c, h32=h32: g.tensor_copy(out=hstate[:, c:c + 1], in_=h32[:, N - 1:N]), reads=[b_h32], writes=[b_hstate])
                tk.op("act", lambda g, c=c, h32=h32: g.activation(out=h16[:, c, 0:N], in_=h32[:, 0:N], func=AF.Copy), reads=[b_h32], writes=[b_h16])
                yield c, h32, b_h32

        def q_tile(NSB, hT, b_hT, Wq, b_Wq, Wuq, b_Wuq, mm_ring, tp_ring, cosq_t, sinq_t, b_csq, st, b_st, junk, b_junk,
                   qn16, b_qn16, qlT, b_qlT, q16, b_q16, tcq, tsq, b_tcq, qpe32, b_qpe32, csx, b_csx):
            for s in range(NSB):
                pm, b_pm = mm_ring.get()
                for k in range(8):
                    tk.op("pe", lambda g, s=s, k=k, pm=pm: g.matmul(pm[:, 0:256], lhsT=hT[:, k, s * 128:(s + 1) * 128], rhs=Wq[:, k, :],
                                                                    start=(k == 0), stop=(k == 7)), reads=[b_hT, b_Wq], writes=[b_pm])
                rms_free(pm, b_pm, 256, s, st, b_st, s, junk, b_junk)
                tk.op("dve", lambda g, s=s, pm=pm: g.scalar_tensor_tensor(out=qn16[:, s, :], in0=pm[:, 0:256], scalar=st[:, s:s + 1], in1=qnw_bc[:],
                                                                         op0=ALU.mult, op1=ALU.mult), reads=[b_pm, b_st, b_nw], writes=[b_qn16])
            for c in range(2):
                tp, b_tp = tp_ring.get()
                for s in range(NSB):
                    tk.op("pe", lambda g, s=s, c=c, tp=tp: g.transpose(out=tp[:, s * 128:(s + 1) * 128], in_=qn16[:, s, c * 128:(c + 1) * 128], identity=ident[:]),
                          reads=[b_qn16, b_ident], writes=[b_tp])
                copy_op(alt("act", "dve"), qlT[:, c, 0:NSB * 128], tp[:, 0:NSB * 128], [b_tp], [b_qlT])
            tk.op("pool", lambda g: g.tensor_copy(out=csx[:, 0, 0:NSB], in_=cosq_t[:, 0:NSB, :].unsqueeze(2).to_broadcast([128, NSB, 8, 16])),
                  reads=[b_csq], writes=[b_csx])
            tk.op("pool", lambda g: g.tensor_copy(out=csx[:, 1, 0:NSB], in_=sinq_t[:, 0:NSB, :].unsqueeze(2).to_broadcast([128, NSB, 8, 16])),
                  reads=[b_csq], writes=[b_csx])
            for s in range(NSB):
                for half in range(2):
                    pm, b_pm = mm_ring.get()
                    for c in range(2):
                        tk.op("pe", lambda g, s=s, c=c, half=half, pm=pm: g.matmul(pm[:, 0:384], lhsT=qlT[:, c, s * 128:(s + 1) * 128],
                                                                                     rhs=Wuq[:, c, half * 384:(half + 1) * 384], start=(c == 0), stop=(c == 1)),
                              reads=[b_qlT, b_Wuq], writes=[b_pm])
                    pv = pm[:, 0:384].rearrange("p (h e) -> p h e", h=4)
                    tk.op("act", lambda g, s=s, half=half, pv=pv: g.activation(out=q16[:, s, half * 4:(half + 1) * 4, 0:64], in_=pv[:, :, 0:64], func=AF.Copy, scale=ATTN_SCALE),
                          reads=[b_pm], writes=[b_q16])
                    tk.op("act", lambda g, s=s, half=half, pv=pv: g.activation(out=qpe32[:, s, half * 4:(half + 1) * 4, :], in_=pv[:, :, 64:96], func=AF.Copy),
                          reads=[b_pm], writes=[b_qpe32])
            G = NSB * 8
            src4 = qpe32[:, 0:NSB].rearrange("p s h (a j) -> p (s h) a j", a=2)
            cosb = csx[:, 0, 0:NSB].rearrange("p s h j -> p (s h) j").unsqueeze(2).to_broadcast([128, G, 2, 16])
            sinb = csx[:, 1, 0:NSB].rearrange("p s h j -> p (s h) j").unsqueeze(2).to_broadcast([128, G, 2, 16])
            q16v = q16[:, 0:NSB].rearrange("p s h e -> p (s h) e")
            rope_tm("pool", src4, cosb, sinb, q16v[:, :, 64:80], q16v[:, :, 80:96], tcq[:, 0:G], tsq[:, 0:G], [b_qpe32, b_csx], b_tcq, [b_q16])

        def out_tile(NSB, xt, b_xt, hT, b_hT, Wg, b_Wg, Woa, Wob, Wout, b_Wo, mm_ring, tmp_ring, AO_v, b_AO, lru_v, b_lru,
                     ga16, b_ga, gb16, b_gb, thm, b_thm, mg16, b_mg, Gt, st, b_st, junk, b_junk, out_ap):
            N = NSB * 128
            for j in range(24):
                pm, b_pm = mm_ring.get()
                for k in range(8):
                    tk.op("pe", lambda g, j=j, k=k, pm=pm: g.matmul(pm[:, 0:N], lhsT=Wg[:, k, j * 128:(j + 1) * 128], rhs=hT[:, k, 0:N],
                                                                    start=(k == 0), stop=(k == 7)), reads=[b_hT, b_Wg], writes=[b_pm])
                if j < 8:
                    th, b_th = tmp_ring.get()
                    tk.op("act", lambda g, pm=pm, th=th: g.activation(out=th[:, 0:N], in_=pm[:, 0:N], func=AF.Tanh, scale=0.5), reads=[b_pm], writes=[b_th])
                    tk.op("dve", lambda g, pm=pm, th=th: g.scalar_tensor_tensor(out=th[:, 0:N], in0=th[:, 0:N], scalar=1.0, in1=pm[:, 0:N], op0=ALU.add, op1=ALU.mult),
                          reads=[b_th, b_pm], writes=[b_th])
                    if j < 4:
                        tk.op("dve", lambda g, j=j, th=th: g.tensor_tensor(out=ga16[:, j, 0:N], in0=th[:, 0:N], in1=AO_v(j), op=ALU.mult),
                              reads=[b_th, b_AO], writes=[b_ga])
                    else:
                        tk.op("dve", lambda g, j=j, th=th: g.tensor_tensor(out=gb16[:, j - 4, 0:N], in0=th[:, 0:N], in1=lru_v(j - 4), op=ALU.mult),
                              reads=[b_th, b_lru], writes=[b_gb])
                else:
                    tk.op("act", lambda g, j=j, pm=pm: g.activation(out=thm[:, j - 8, 0:N], in_=pm[:, 0:N], func=AF.Tanh, scale=0.5), reads=[b_pm], writes=[b_thm])
            for j in range(8):
                pa, b_pa = mm_ring.get()
                for kc in range(4):
                    tk.op("pe", lambda g, j=j, kc=kc, pa=pa: g.matmul(pa[:, 0:N], lhsT=Woa[:, kc, j * 128:(j + 1) * 128], rhs=ga16[:, kc, 0:N],
                                                                      start=(kc == 0), stop=(kc == 3)), reads=[b_Wo, b_ga], writes=[b_pa])
                pb, b_pb = mm_ring.get()
                for kc in range(4):
                    tk.op("pe", lambda g, j=j, kc=kc, pb=pb: g.matmul(pb[:, 0:N], lhsT=Wob[:, kc, j * 128:(j + 1) * 128], rhs=gb16[:, kc, 0:N],
                                                                      start=(kc == 0), stop=(kc == 3)), reads=[b_Wo, b_gb], writes=[b_pb])
                t1, b_t1 = tmp_ring.get()
                tk.op("dve", lambda g, j=j, pa=pa, t1=t1: g.scalar_tensor_tensor(out=t1[:, 0:N], in0=thm[:, j, 0:N], scalar=1.0, in1=pa[:, 0:N], op0=ALU.add, op1=ALU.mult),
                      reads=[b_thm, b_pa], writes=[b_t1])
                t2, b_t2 = tmp_ring.get()
                tk.op("dve", lambda g, j=j, pb=pb, t2=t2: g.scalar_tensor_tensor(out=t2[:, 0:N], in0=thm[:, 8 + j, 0:N], scalar=1.0, in1=pb[:, 0:N], op0=ALU.add, op1=ALU.mult),
                      reads=[b_thm, b_pb], writes=[b_t2])
                tk.op("dve", lambda g, j=j, t1=t1, t2=t2: g.tensor_tensor(out=mg16[:, j, 0:N], in0=t1[:, 0:N], in1=t2[:, 0:N], op=ALU.add),
                      reads=[b_t1, b_t2], writes=[b_mg])
            tk.op("dve", lambda g: g.memset(st[:, 0:4], 0.0), writes=[b_st])
            for s in range(NSB):
                for n in range(2):
                    po, b_po = mm_ring.get()
                    for k in range(8):
                        tk.op("pe", lambda g, s=s, n=n, k=k, po=po: g.matmul(po[:], lhsT=mg16[:, k, s * 128:(s + 1) * 128], rhs=Wout[:, k, n * 512:(n + 1) * 512],
                                                                             start=(k == 0), stop=(k == 7)), reads=[b_Wo, b_mg], writes=[b_po])
                    t1, b_t1 = tmp_ring.get()
                    tk.op("dve", lambda g, n=n, po=po, t1=t1: g.tensor_tensor(out=t1[:], in0=po[:], in1=Gt[:, n * 512:(n + 1) * 512], op=ALU.mult),
                          reads=[b_po, b_G], writes=[b_t1])
                    tk.op("dve", lambda g, s=s, n=n, t1=t1: g.tensor_tensor(out=xt[:, s, n * 512:(n + 1) * 512], in0=xt[:, s, n * 512:(n + 1) * 512], in1=t1[:], op=ALU.add),
                          reads=[b_t1, b_xt], writes=[b_xt])
                tk.op("act", lambda g, s=s: g.activation(out=junk[:, 0:D], in_=xt[:, s, :], func=AF.Square, accum_out=st[:, s:s + 1]),
                      reads=[b_xt], writes=[b_junk, b_st])
                tk.op("dve", lambda g, s=s: g.tensor_scalar(out=st[:, 4 + s:5 + s], in0=st[:, s:s + 1], scalar1=1.0 / D, scalar2=EPS, op0=ALU.mult, op1=ALU.add),
                      reads=[b_st], writes=[b_st])
                tk.op("act", lambda g, s=s: g.activation(out=st[:, 4 + s:5 + s], in_=st[:, 4 + s:5 + s], func=AF.Sqrt), reads=[b_st], writes=[b_st])
                tk.op("dve", lambda g, s=s: g.reciprocal(out=st[:, 4 + s:5 + s], in_=st[:, 4 + s:5 + s]), reads=[b_st], writes=[b_st])
                tk.op("dve", lambda g, s=s: g.scalar_tensor_tensor(out=xt[:, s, :], in0=xt[:, s, :], scalar=st[:, 4 + s:5 + s], in1=fnw_bc[:], op0=ALU.mult, op1=ALU.mult),
                      reads=[b_xt, b_st, b_fnw], writes=[b_xt])
            out_bufs.append(b_xt)
            tk.dma("sp", out_ap.rearrange("(s p) d -> p s d", p=128), xt[:, 0:NSB, :], reads=[b_xt])

        def load_gate_w(es):
            Wa = sbt(es, "Wa", [128, 4, 128], BF16); Wx = sbt(es, "Wx", [128, 4, 128], BF16); b_Wg = Buf("Wgates")
            tk.op("pool", lambda g: g.memset(Wa[:], 0.0), writes=[b_Wg])
            tk.op("pool", lambda g: g.memset(Wx[:], 0.0), writes=[b_Wg])
            for n in range(8):
                c, j = n // 2, n % 2
                tk.dma("pool", Wa[64 * j:64 * j + 64, c, 64 * j:64 * j + 64], lru_wa[n], writes=[b_Wg])
                tk.dma("pool", Wx[64 * j:64 * j + 64, c, 64 * j:64 * j + 64], lru_wx[n], writes=[b_Wg])
            return Wa, Wx, b_Wg

        with contextlib.ExitStack() as esP:
            ckvT = sbt(esP, "ckvT", [128, 2, SEQ], BF16); b_ckvT = Buf("ckvT")
            KT = sbt(esP, "KT", [96, SEQ], BF16); b_KTpe = Buf("KTpe"); b_KTn = Buf("KTn")

            with contextlib.ExitStack() as es:
                xring = ring_sb(es, "xt", 2, [128, 4, D], F32)
                xn = sbt(es, "xn", [128, 4, D], BF16); b_xn = Buf("xn")
                hring = ring_sb(es, "hT", 1, [128, 8, 512], BF16)
                Wkv, b_Wkv = load_w(es, "Wkv", w_in[:, C_KV:C_KV + 288], [128, 8, 288])
                Wu, b_Wu = load_w(es, "Wu", w_in[:, C_U:C_U + 512], [128, 8, 512])
                Wa, Wx, b_Wg = load_gate_w(es)
                tp_ring = ring_ps(es, "tpA", 2, [128, 512], BF16)
                mm_ring = ring_ps(es, "mmA", 6, [128, 512], F32)
                tmp_ring = ring_sb(es, "tmpA", 6, [128, 512], F32)
                cs_ring = ring_sb(es, "csA", 2, [128, 2, 4, 16], F32)
                ckv32r = ring_sb(es, "ckv32", 2, [128, 4, 256], F32)
                kpe32r = ring_sb(es, "kpe32", 2, [128, 4, 32], F32)
                ckv16 = sbt(es, "ckv16", [128, 4, 256], BF16); b_ckv16 = Buf("ckv16")
                kpe16 = sbt(es, "kpe16", [128, 4, 96], BF16); b_kpe16 = Buf("kpe16")
                tk.op("pool", lambda g: g.memset(kpe16[:], 0.0), writes=[b_kpe16])
                st = sbt(es, "stA", [128, 8], F32); b_st = Buf("stA")
                st2 = sbt(es, "stA2", [128, 8], F32); b_st2 = Buf("stA2")
                junk = sbt(es, "junkA", [128, 256], BF16); b_junk = Buf("junkA")
                tc = sbt(es, "tcA", [128, 4, 2, 16], F32); ts = sbt(es, "tsA", [128, 4, 2, 16], F32); b_tc = Buf("tcA")
                uextr = ring_sb(es, "uext", 1, [128, 4, 515], F32)
                uc32 = sbt(es, "uc32", [128, 4, 512], F32); b_uc32 = Buf("uc32")
                uc16 = sbt(es, "uc16", [128, 4, 512], BF16); b_uc16 = Buf("uc16")
                hstate = sbt(es, "hstate", [128, 4], F32); b_hstate = Buf("hstate")
                h16r = ring_sb(es, "h16", 1, [128, 4, 512], BF16)
                tk.op("dve", lambda g: g.memset(hstate[:], 0.0), writes=[b_hstate])
                prev_u = None
                nxt = load_x(xring, xf[0:512, :], 4)
                for t in range(NT):
                    xt, b_xt = nxt
                    if t + 1 < NT:
                        nxt = load_x(xring, xf[(t + 1) * 512:(t + 2) * 512, :], 4)
                    cst, b_cst = cs_ring.get()
                    tk.dma("sp", cst[:, 0], cosf[t * 512:(t + 1) * 512, :].rearrange("(s p) j -> p s j", p=128), writes=[b_cst])
                    tk.dma("sp", cst[:, 1], sinf[t * 512:(t + 1) * 512, :].rearrange("(s p) j -> p s j", p=128), writes=[b_cst])
                    hT, b_hT = hring.get()
                    norm_hT(xt, b_xt, 4, xn, b_xn, hT, b_hT, tp_ring, st, b_st, False)
                    ckv32, b_ckv32 = ckv32r.get(); kpe32, b_kpe32 = kpe32r.get()
                    kside_tile(4, hT, b_hT, Wkv, b_Wkv, mm_ring, tp_ring, cst[:, 0], cst[:, 1], b_cst, ckv32, b_ckv32, kpe32, b_kpe32,
                               ckv16, b_ckv16, kpe16, b_kpe16, st2, b_st2, junk, b_junk, tc, ts, b_tc,
                               lambda c, t=t: ckvT[:, c, t * 512:(t + 1) * 512], b_ckvT, KT[64:96, t * 512:(t + 1) * 512], b_KTpe,
                               ckvp[t * 512:(t + 1) * 512, :], kpep[t * 512:(t + 1) * 512, :], True)
                    ue, b_ue = uextr.get()

                    def hist(ue=ue, b_ue=b_ue, prev_u=prev_u):
                        if prev_u is None:
                            tk.op("pool", lambda g: g.memset(ue[:, :, 0:3], 0.0), writes=[b_ue])
                        else:
                            pu, b_pu = prev_u
                            tk.op("pool", lambda g: g.tensor_copy(out=ue[:, :, 0:3], in_=pu[:, :, 512:515]), reads=[b_pu], writes=[b_ue])
                    h16, b_h16 = h16r.get()
                    for _ in lru_tile(512, hT, b_hT, Wu, b_Wu, Wa, Wx, b_Wg, mm_ring, tmp_ring,
                                      lambda c, k0, N, ue=ue: ue[:, c, k0:k0 + N], b_ue, hist, uc32, b_uc32, uc16, b_uc16,
                                      hstate, b_hstate, h16, b_h16):
                        pass
                    tk.dma("sp", lru_scr[t].rearrange("p (c n) -> p c n", c=4), h16[:], reads=[b_h16], writes=[b_lru_scr[t]], owner=b_h16)
                    prev_u = (ue, b_ue)
                with nc.allow_non_contiguous_dma(reason="tiny state outputs"):
                    out_bufs.append(b_hstate); out_bufs.append(prev_u[1])
                    tk.dma("sp", lrup.rearrange("o (c p) -> p (o c)", p=128), hstate[:], reads=[b_hstate])
                    for c in range(4):
                        tk.dma("sp", convp[:, c * 128:(c + 1) * 128].rearrange("r p -> p r"), prev_u[0][:, c, 512:515], reads=[prev_u[1]])
                tk.barrier()
            if stop == "A":
                raise _Stop()

            with contextlib.ExitStack() as es:
                xring = ring_sb(es, "xtB", 2, [128, 4, D], F32)
                xn = sbt(es, "xnB", [128, 4, D], BF16); b_xn = Buf("xnB")
                hring = ring_sb(es, "hTB", 2, [128, 8, 512], BF16)
                Wq, b_Wq = load_w(es, "Wq", w_in[:, C_Q:C_Q + 256], [128, 8, 256])
                Wuq, b_Wuq = load_w(es, "Wuq", w_uq, [128, 2, 768])
                tp_ring = ring_ps(es, "tpB", 2, [128, 512], BF16)
                mm_ring = ring_ps(es, "mmB", 6, [128, 512], F32)
                csq_ring = ring_sb(es, "csB", 2, [128, 2, 4, 16], F32)
                st = sbt(es, "stB", [128, 8], F32); b_st = Buf("stB")
                st2 = sbt(es, "stB2", [128, 8], F32); b_st2 = Buf("stB2")
                junk = sbt(es, "junkB", [128, 256], BF16); b_junk = Buf("junkB")
                qn16 = sbt(es, "qn16", [128, 4, 256], BF16); b_qn16 = Buf("qn16")
                qlT = sbt(es, "qlT", [128, 2, 512], BF16); b_qlT = Buf("qlT")
                q16r = ring_sb(es, "q16", 2, [128, 4, 8, 96], BF16)
                tcq = sbt(es, "tcq", [128, 32, 2, 16], F32); tsq = sbt(es, "tsq", [128, 32, 2, 16], F32); b_tcq = Buf("tcq")
                qpe32 = sbt(es, "qpe32", [128, 4, 8, 32], F32); b_qpe32 = Buf("qpe32")
                csx = sbt(es, "csx", [128, 2, 4, 8, 16], F32); b_csx = Buf("csx")
                qstr = ring_sb(es, "qst", 2, [96, 8, 512], BF16)
                nxt = load_x(xring, xo[0:512, :], 4)
                for m in range(NS):
                    xt, b_xt = nxt
                    if m + 1 < NS:
                        nxt = load_x(xring, xo[(m + 1) * 512:(m + 2) * 512, :], 4)
                    cst, b_cst = csq_ring.get()
                    tk.dma("sp", cst[:, 0], coso[m * 512:(m + 1) * 512, :].rearrange("(s p) j -> p s j", p=128), writes=[b_cst])
                    tk.dma("sp", cst[:, 1], sino[m * 512:(m + 1) * 512, :].rearrange("(s p) j -> p s j", p=128), writes=[b_cst])
                    hT, b_hT = hring.get()
                    norm_hT(xt, b_xt, 4, xn, b_xn, hT, b_hT, tp_ring, st, b_st, False)
                    if stop == "B1":
                        tk.barrier()
                        raise _Stop()
                    q16, b_q16 = q16r.get()
                    q_tile(4, hT, b_hT, Wq, b_Wq, Wuq, b_Wuq, mm_ring, tp_ring, cst[:, 0], cst[:, 1], b_cst, st2, b_st2, junk, b_junk,
                           qn16, b_qn16, qlT, b_qlT, q16, b_q16, tcq, tsq, b_tcq, qpe32, b_qpe32, csx, b_csx)
                    if stop == "B2":
                        tk.barrier()
                        raise _Stop()
                    qst, b_qst = qstr.get()
                    for h in range(NH):
                        tp, b_tp = tp_ring.get()
                        for s in range(4):
                            tk.op("pe", lambda g, s=s, h=h, tp=tp: g.transpose(out=tp[0:96, s * 128:(s + 1) * 128], in_=q16[:, s, h, :], identity=ident[:]),
                                  reads=[b_q16, b_ident], writes=[b_tp])
                        copy_op(alt("act", "dve"), qst[:, h, :], tp[0:96, :], [b_tp], [b_qst])
                    if stop == "B3":
                        tk.barrier()
                        raise _Stop()
                    tk.dma("sp", qt_scr[:, :, m * 512:(m + 1) * 512].rearrange("h r n -> r h n"), qst[:], reads=[b_qst], writes=[b_qt_scr[m]], owner=b_qst)
                tk.barrier()
            if stop == "B":
                raise _Stop()

            with contextlib.ExitStack() as es:
                AO = sbt(es, "AO", [128, 4, NOWN], BF16); b_AO = Buf("AO")
                Wuk = sbt(es, "Wuk", [128, 2, 8, 96], BF16); b_Wuk = Buf("Wuk")
                tk.op("pool", lambda g: g.memset(Wuk[:], 0.0), writes=[b_Wuk])
                for c in range(2):
                    tk.dma("pool", Wuk[:, c, :, 0:64], w_uk[c * 128:(c + 1) * 128, :].rearrange("p (h e) -> p h e", h=8), writes=[b_Wuk])
                Wuv, b_Wuv = load_w(es, "Wuv", w_uv, [128, 2, 512])
                msk = sbt(es, "msk", [128, 2, 8 * 512], BF16); b_msk = Buf("msk")
                for par in range(2):
                    tk.dma("sp", msk[:, par, :], masks[par], writes=[b_msk])
                Vp = [sbt(es, "Vp%d" % i, [128, NKB, 128], BF16) for i in range(2)]; b_Vp = [Buf("Vp0"), Buf("Vp1")]
                tk.op("pool", lambda g: g.memset(Vp[0][:], 0.0), writes=[b_Vp[0]])
                tk.op("pool", lambda g: g.memset(Vp[0][:, :, 64:65], 1.0), writes=[b_Vp[0]])
                tk.op("pool", lambda g: g.memset(Vp[1][:], 0.0), writes=[b_Vp[1]])
                tk.op("pool", lambda g: g.memset(Vp[1][:, :, 32:33], 1.0), writes=[b_Vp[1]])
                QTr = ring_sb(es, "QT", 2, [96, NOWN], BF16)
                PTr = ring_sb(es, "PT", 4, [128, 512], BF16)
                den_r = ring_sb(es, "den", 2, [128, 512], F32)
                bc_r = ring_sb(es, "bc", 2, [128, 512], F32)
                s_ring = ring_ps(es, "psS", 3, [128, 512], F32)
                o_ring = ring_ps(es, "psO", 2, [128, 512], F32)
                x_ring = ring_ps(es, "psX", 3, [128, 512], F32)
                for h in range(NH):
                    par = h % 2
                    o_lo = 64 * par
                    den_row = 64 if par == 0 else 32
                    vlo = 0 if par == 0 else 64
                    QT, b_QT = QTr.get()
                    tk.dma("sp", QT[:], qt_scr[h], reads=b_qt_scr, writes=[b_QT])
                    for t in range(NT):
                        pk, b_pk = x_ring.get()
                        for c in range(2):
                            tk.op("pe", lambda g, t=t, c=c, h=h, pk=pk: g.matmul(pk[0:96, :], lhsT=Wuk[:, c, h, :], rhs=ckvT[:, c, t * 512:(t + 1) * 512],
                                                                                 start=(c == 0), stop=(c == 1)), reads=[b_Wuk, b_ckvT], writes=[b_pk])
                        copy_op(alt("act", "dve"), KT[0:64, t * 512:(t + 1) * 512], pk[0:64, :], [b_pk], [b_KTn])
                    for g8 in range(NKB // 8):
                        pv_, b_pv = x_ring.get()
                        for i in range(8):
                            kb = g8 * 8 + i
                            for c in range(2):
                                tk.op("pe", lambda g, kb=kb, i=i, c=c, h=h, pv_=pv_: g.matmul(pv_[:, i * 64:(i + 1) * 64], lhsT=ckvT[:, c, kb * 128:(kb + 1) * 128],
                                                                                              rhs=Wuv[:, c, h * 64:(h + 1) * 64], start=(c == 0), stop=(c == 1)),
                                      reads=[b_Wuv, b_ckvT], writes=[b_pv])
                        copy_op(alt("act", "dve"), Vp[par][:, g8 * 8:(g8 + 1) * 8, vlo:vlo + 64], pv_[:].rearrange("p (i e) -> p i e", i=8), [b_pv], [b_Vp[par]])
                    for m in range(NS):
                        n = 8 * m + 8
                        qs = slice(m * 512, (m + 1) * 512)
                        po, b_po = o_ring.get()
                        pend = []

                        def emit_qk(kb, h=h, qs=qs, QT=QT, b_QT=b_QT):
                            ps_, b_ps = s_ring.get()
                            tk.op("pe", lambda g: g.matmul(ps_[:], lhsT=KT[0:96, kb * 128:(kb + 1) * 128], rhs=QT[0:96, qs], start=True, stop=True),
                                  reads=[b_KTpe, b_KTn, b_QT], writes=[b_ps])
                            pt, b_pt = PTr.get()
                            tk.op("act", lambda g: g.activation(out=pt[:], in_=ps_[:], func=AF.Exp), reads=[b_ps], writes=[b_pt])
                            d = kb - (n - 8)
                            if d >= 0:
                                tk.op("pool", lambda g: g.tensor_tensor(out=pt[:], in0=pt[:], in1=msk[:, m % 2, d * 512:(d + 1) * 512], op=ALU.mult),
                                      reads=[b_pt, b_msk], writes=[b_pt])
                            return pt, b_pt

                        def emit_pv(kb, pt, b_pt, po=po, b_po=b_po, par=par):
                            tk.op("pe", lambda g: g.matmul(po[:], lhsT=Vp[par][:, kb, :], rhs=pt[:], start=(kb == 0), stop=(kb == n - 1)),
                                  reads=[b_Vp[par], b_pt], writes=[b_po])
                        for kb in range(n):
                            pend.append((kb,) + emit_qk(kb))
                            if len(pend) > 2:
                                emit_pv(*pend.pop(0))
                        while pend:
                            emit_pv(*pend.pop(0))
                        den, b_den = den_r.get()
                        tk.op("act", lambda g, den=den, po=po: g.activation(out=den[den_row:den_row + 1, :], in_=po[den_row:den_row + 1, :], func=AF.Copy),
                              reads=[b_po], writes=[b_den])
                        tk.op("dve", lambda g, den=den: g.reciprocal(out=den[den_row:den_row + 1, :], in_=den[den_row:den_row + 1, :]), reads=[b_den], writes=[b_den])
                        pb, b_pb = x_ring.get()
                        tk.op("pe", lambda g, den=den, pb=pb: g.matmul(pb[:], lhsT=ones32[den_row:den_row + 1, :], rhs=den[den_row:den_row + 1, :], start=True, stop=True),
                              reads=[b_ones, b_den], writes=[b_pb])
                        bc, b_bc = bc_r.get()
                        tk.op("act", lambda g, bc=bc, pb=pb: g.activation(out=bc[o_lo:o_lo + 64, :], in_=pb[o_lo:o_lo + 64, :], func=AF.Copy), reads=[b_pb], writes=[b_bc])
                        tk.op("dve", lambda g, bc=bc, po=po, qs=qs, h=h: g.tensor_tensor(out=AO[o_lo:o_lo + 64, h // 2, qs], in0=po[o_lo:o_lo + 64, :], in1=bc[o_lo:o_lo + 64, :], op=ALU.mult),
                              reads=[b_po, b_bc], writes=[b_AO])
                for j in range(4):
                    tk.dma("sp", ao_scr[:, j, :], AO[:, j, :], reads=[b_AO], writes=[b_ao_scr], owner=b_AO)
                tk.wait_all("sp", [b_ao_scr])
                tk.barrier()
            if stop == "C":
                raise _Stop()

        with contextlib.ExitStack() as es:
            Wg = sbt(es, "Wg", [128, 8, 3072], BF16); b_Wg = Buf("WgD")
            for k in range(8):
                tk.dma("pool", Wg[:, k, 0:512], w_in[k * 128:(k + 1) * 128, C_GA:C_GA + 512], writes=[b_Wg])
                tk.dma("pool", Wg[:, k, 512:3072], w_in[k * 128:(k + 1) * 128, C_GB:C_GB + 2560], writes=[b_Wg])
            Woa, b_Wo = load_w(es, "Woa", w_oa, [128, 4, D])
            Wob, _b = load_w(es, "Wob", w_ob, [128, 4, D], b=b_Wo)
            Wout, _b2 = load_w(es, "Wout", w_out, [128, 8, D], b=b_Wo)
            xring = ring_sb(es, "xtD", 2, [128, 2, D], F32)
            xn = sbt(es, "xnD", [128, 2, D], BF16); b_xn = Buf("xnD")
            hring = ring_sb(es, "hTD", 2, [128, 8, 256], BF16)
            tp_ring = ring_ps(es, "tpD", 2, [128, 512], BF16)
            mm_ring = ring_ps(es, "mmD", 6, [128, 512], F32)
            tmp_ring = ring_sb(es, "tmpD", 6, [128, 512], F32)
            lrur = ring_sb(es, "lruD", 2, [128, 4, 256], BF16)
            aor = ring_sb(es, "aoD", 2, [128, 4, 256], BF16)
            st = sbt(es, "stD", [128, 8], F32); b_st = Buf("stD")
            st2 = sbt(es, "stD2", [128, 8], F32); b_st2 = Buf("stD2")
            junk = sbt(es, "junkD", [128, D], BF16); b_junk = Buf("junkD")
            ga16 = sbt(es, "ga16", [128, 4, 256], BF16); b_ga = Buf("ga16")
            gb16 = sbt(es, "gb16", [128, 4, 256], BF16); b_gb = Buf("gb16")
            thm = sbt(es, "thm", [128, 16, 256], BF16); b_thm = Buf("thm")
            mg16 = sbt(es, "mg16", [128, 8, 256], BF16); b_mg = Buf("mg16")

            tk.wait_all("sp", [b_own])
            nxt = load_x(xring, xo[0:256, :], 2)
            for i in range(2 * NS):
                m, hf = i // 2, i % 2
                xt, b_xt = nxt
                if i + 1 < 2 * NS:
                    nxt = load_x(xring, xo[(i + 1) * 256:(i + 2) * 256, :], 2)
                lt, b_lt = lrur.get()
                tk.wait_all("sp", [lt_b for lt_b in [b_lt]])
                ov = dyn_off(own_sb[0:1, m:m + 1], 128 * 2048 * 2)
                if hf:
                    nc.sync.reg_alu(R["o0"], R["o0"], hf * 512, ALU.add)
                tk.dma("sp", lt[:].bitcast(U8),
                       lru_scr_b[bass.ds(ov, 128 * 4096)].rearrange("(p c n) -> p c n", c=4, n=1024)[:, :, 0:512],
                       reads=b_lru_scr, writes=[b_lt])
                at, b_at = aor.get()
                tk.dma("sp", at[:], ao_scr[:, :, i * 256:(i + 1) * 256], reads=[b_ao_scr], writes=[b_at])
                hT, b_hT = hring.get()
                norm_hT(xt, b_xt, 2, xn, b_xn, hT, b_hT, tp_ring, st, b_st, False)
                out_tile(2, xt, b_xt, hT, b_hT, Wg, b_Wg, Woa, Wob, Wout, b_Wo, mm_ring, tmp_ring,
                         lambda j, at=at: at[:, j, :], b_at, lambda c, lt=lt: lt[:, c, :], b_lt,
                         ga16, b_ga, gb16, b_gb, thm, b_thm, mg16, b_mg, Gp, st2, b_st2, junk, b_junk, yo[i * 256:(i + 1) * 256, :])
            tk.barrier()
        if stop == "D":
            raise _Stop()

        with contextlib.ExitStack() as es:
            NG = NP // 4
            KW = (128, 128, 32)
            xring = ring_sb(es, "xtS", 1, [128, 1, D], F32)
            xn = sbt(es, "xnS", [128, 1, D], BF16); b_xn = Buf("xnS")
            hT = sbt(es, "hTS", [128, 8, 128], BF16); b_hT = Buf("hTS")
            st = sbt(es, "stS", [128, 8], F32); b_st = Buf("stS")
            st2 = sbt(es, "stS2", [128, 8], F32); b_st2 = Buf("stS2")
            junk = sbt(es, "junkS", [128, D], BF16); b_junk = Buf("junkS")
            TN = sbt(es, "TN", [128, 3, 128], BF16); b_TN = Buf("TN")
            natn = sbt(es, "natn", [128, 257], BF16); b_natn = Buf("natn")
            h16 = sbt(es, "h16S", [128, 4, 128], BF16); b_h16 = Buf("h16S")
            QS = sbt(es, "QS", [128, 3, 16, 8, 8], BF16); b_QS = Buf("QS")
            AOs = sbt(es, "AOs", [128, 4, 128], BF16); b_AOs = Buf("AOs")
            QB = sbt(es, "QB", [128, 16, 4, 64], BF16); b_QB = Buf("QB")
            tmp_ring = ring_sb(es, "tmpS", 6, [128, 512], F32)
            xt, b_xt = load_x(xring, xs, 1)

            with contextlib.ExitStack() as e1:
                tpg = ring_ps(e1, "tpg", 2, [128, 1024], BF16)
                mm_ring = ring_ps(e1, "mmS", 2, [128, 512], F32)
                Wkv, b_Wkv = load_w(e1, "WkvS", w_in[:, C_KV:C_KV + 288], [128, 8, 288])
                Wu, b_Wu = load_w(e1, "WuS", w_in[:, C_U:C_U + 512], [128, 8, 512])
                Wq, b_Wq = load_w(e1, "WqS", w_in[:, C_Q:C_Q + 256], [128, 8, 256])
                Wuq, b_Wuq = load_w(e1, "WuqS", w_uq, [128, 2, 768])
                Wa, Wx, b_Wgt = load_gate_w(e1)
                Wuk16, b_Wuk16 = load_w(e1, "Wuk16", w_uk, [128, 2, 512])
                WukT = sbt(e1, "WukT", [64, 8, 256], BF16); b_WukT = Buf("WukT")
                cst = sbt(e1, "cstS", [128, 4, 1, 16], F32); b_cst = Buf("cstS")
                for i_, src_ in enumerate((coss, sins, cossq, sinsq)):
                    tk.dma("sp", cst[:, i_], src_.rearrange("(s p) j -> p s j", p=128), writes=[b_cst])
                ckv32 = sbt(e1, "ckv32S", [128, 1, 256], F32); b_ckv32 = Buf("ckv32S")
                kpe32 = sbt(e1, "kpe32S", [128, 1, 32], F32); b_kpe32 = Buf("kpe32S")
                ckv16 = sbt(e1, "ckv16S", [128, 1, 256], BF16); b_ckv16 = Buf("ckv16S")
                kpe16 = sbt(e1, "kpe16S", [128, 1, 96], BF16); b_kpe16 = Buf("kpe16S")
                tc = sbt(e1, "tcS", [128, 4, 2, 16], F32); ts = sbt(e1, "tsS", [128, 4, 2, 16], F32); b_tc = Buf("tcS")
                norm_hT(xt, b_xt, 1, xn, b_xn, hT, b_hT, tpg, st, b_st, True)
                kside_tile(1, hT, b_hT, Wkv, b_Wkv, mm_ring, tpg, cst[:, 0], cst[:, 1], b_cst, ckv32, b_ckv32, kpe32, b_kpe32,
                           ckv16, b_ckv16, kpe16, b_kpe16, st2, b_st2, junk, b_junk, tc, ts, b_tc,
                           lambda c: TN[:, c, :], b_TN, TN[0:32, 2, :], b_TN, ckvs, kpes, False)
                tk.op("pool", lambda g: g.memset(natn[:, 256:257], 1.0), writes=[b_natn])
                tk.op("pool", lambda g: g.tensor_copy(out=natn[:, 0:256], in_=ckv16[:, 0, :]), reads=[b_ckv16], writes=[b_natn])

                sc_sb = sbt(e1, "sc_sb", [48, 512], F32); b_sc = Buf("sc_sb")
                sl_sb = sbt(e1, "sl_sb", [16, 512], F32); b_sl = Buf("sl_sb")
                tk.dma("sp", sc_sb[:], sconv, writes=[b_sc])
                tk.dma("sp", sl_sb[:], slru, writes=[b_sl])
                uext = sbt(e1, "uextS", [128, 4, 16, 11], F32); b_uext = Buf("uextS")
                h0T = sbt(e1, "h0T", [128, 4, 16], F32); b_h0T = Buf("h0T")
                for c in range(4):
                    pm, b_pm = mm_ring.get()
                    tk.op("pe", lambda g, c=c, pm=pm: g.transpose(out=pm[:, 0:48], in_=sc_sb[:, c * 128:(c + 1) * 128], identity=identf[0:48, 0:48]),
                          reads=[b_sc, b_identf], writes=[b_pm])
                    tk.op("dve", lambda g, c=c, pm=pm: g.tensor_copy(out=uext[:, c, :, 0:3], in_=pm[:, 0:48].rearrange("p (b r) -> p b r", r=3)),
                          reads=[b_pm], writes=[b_uext])
                    pm2, b_pm2 = mm_ring.get()
                    tk.op("pe", lambda g, c=c, pm2=pm2: g.transpose(out=pm2[:, 0:16], in_=sl_sb[:, c * 128:(c + 1) * 128], identity=identf[0:16, 0:16]),
                          reads=[b_sl, b_identf], writes=[b_pm2])
                    tk.op("dve", lambda g, c=c, pm2=pm2: g.tensor_copy(out=h0T[:, c, :], in_=pm2[:, 0:16]), reads=[b_pm2], writes=[b_h0T])
                uc32 = sbt(e1, "uc32S", [128, 4, 128], F32); b_uc32 = Buf("uc32S")
                uc16 = sbt(e1, "uc16S", [128, 4, 128], BF16); b_uc16 = Buf("uc16S")
                hlast = sbt(e1, "hlast", [128, 4, 16], F32); b_hlast = Buf("hlast")
                N = 128
                for c in range(4):
                    pm, b_pm = mm_ring.get()
                    for k in range(8):
                        tk.op("pe", lambda g, c=c, k=k, pm=pm: g.matmul(pm[:, 0:N], lhsT=Wu[:, k, c * 128:(c + 1) * 128], rhs=hT[:, k, 0:N],
                                                                        start=(k == 0), stop=(k == 7)), reads=[b_hT, b_Wu], writes=[b_pm])
                    tk.op("act", lambda g, c=c, pm=pm: g.activation(out=uext[:, c, :, 3:11], in_=pm[:, 0:N].rearrange("p (b t) -> p b t", t=8), func=AF.Copy),
                          reads=[b_pm], writes=[b_uext])
                for c in range(4):
                    ucv = uc32[:, c, :].rearrange("p (b t) -> p b t", t=8)
                    tk.op("dve", lambda g, c=c, ucv=ucv: g.tensor_scalar(out=ucv, in0=uext[:, c, :, 0:8], scalar1=vecT[:, V_CW + c:V_CW + c + 1],
                                                                         scalar2=vecT[:, V_CB + c:V_CB + c + 1], op0=ALU.mult, op1=ALU.add),
                          reads=[b_uext, b_vecT], writes=[b_uc32])
                    for kk in range(1, 4):
                        tk.op("dve", lambda g, c=c, kk=kk, ucv=ucv: g.scalar_tensor_tensor(out=ucv, in0=uext[:, c, :, kk:kk + 8],
                                                                                           scalar=vecT[:, V_CW + 4 * kk + c:V_CW + 4 * kk + c + 1],
                                                                                           in1=ucv, op0=ALU.mult, op1=ALU.add),
                              reads=[b_uext, b_vecT, b_uc32], writes=[b_uc32])
                tk.op("act", lambda g: g.activation(out=uc16[:], in_=uc32[:], func=AF.Copy), reads=[b_uc32], writes=[b_uc16])
                for c in range(4):
                    pr, b_pr = mm_ring.get()
                    tk.op("pe", lambda g, c=c, pr=pr: g.matmul(pr[:, 0:N], lhsT=Wa[:, c, :], rhs=uc16[:, c, :], start=True, stop=True),
                          reads=[b_Wgt, b_uc16], writes=[b_pr])
                    thr, b_thr = tmp_ring.get()
                    tk.op("act", lambda g, c=c, pr=pr, thr=thr: g.activation(out=thr[:, 0:N], in_=pr[:, 0:N], func=AF.Tanh, scale=0.5, bias=lcoef[:, 8 + c:9 + c]),
                          reads=[b_pr, b_lcoef], writes=[b_thr])
                    pi, b_pi = mm_ring.get()
                    tk.op("pe", lambda g, c=c, pi=pi: g.matmul(pi[:, 0:N], lhsT=Wx[:, c, :], rhs=uc16[:, c, :], start=True, stop=True),
                          reads=[b_Wgt, b_uc16], writes=[b_pi])
                    thi, b_thi = tmp_ring.get()
                    tk.op("act", lambda g, c=c, pi=pi, thi=thi: g.activation(out=thi[:, 0:N], in_=pi[:, 0:N], func=AF.Tanh, scale=0.5, bias=lcoef[:, 12 + c:13 + c]),
                          reads=[b_pi, b_lcoef], writes=[b_thi])
                    a_, b_a = tmp_ring.get()
                    tk.op("act", lambda g, c=c, thr=thr, a_=a_: g.activation(out=a_[:, 0:N], in_=thr[:, 0:N], func=AF.Exp, scale=lcoef[:, c:c + 1], bias=lcoef[:, c:c + 1]),
                          reads=[b_thr, b_lcoef], writes=[b_a])
                    a2, b_a2 = tmp_ring.get()
                    tk.op("act", lambda g, c=c, thr=thr, a2=a2: g.activation(out=a2[:, 0:N], in_=thr[:, 0:N], func=AF.Exp, scale=lcoef[:, 4 + c:5 + c], bias=lcoef[:, 4 + c:5 + c]),
                          reads=[b_thr, b_lcoef], writes=[b_a2])
                    tk.op("dve", lambda g, a2=a2: g.tensor_scalar(out=a2[:, 0:N], in0=a2[:, 0:N], scalar1=1.0, scalar2=-1.0, op0=ALU.min, op1=ALU.mult),
                          reads=[b_a2], writes=[b_a2])
                    tk.op("act", lambda g, a2=a2: g.activation(out=a2[:, 0:N], in_=a2[:, 0:N], func=AF.Sqrt, bias=1.0), reads=[b_a2], writes=[b_a2])
                    tk.op("dve", lambda g, c=c, thi=thi: g.scalar_tensor_tensor(out=thi[:, 0:N], in0=thi[:, 0:N], scalar=1.0, in1=uc32[:, c, :], op0=ALU.add, op1=ALU.mult),
                          reads=[b_thi, b_uc32], writes=[b_thi])
                    tk.op("dve", lambda g, thi=thi, a2=a2: g.scalar_tensor_tensor(out=thi[:, 0:N], in0=thi[:, 0:N], scalar=0.5, in1=a2[:, 0:N], op0=ALU.mult, op1=ALU.mult),
                          reads=[b_thi, b_a2], writes=[b_thi])
                    av = a_[:, 0:N].rearrange("p (b t) -> p b t", t=8)
                    bv = thi[:, 0:N].rearrange("p (b t) -> p b t", t=8)
                    t0, b_t0 = tmp_ring.get()
                    tk.op("dve", lambda g, c=c, av=av, t0=t0: g.tensor_tensor(out=t0[:, 0:16], in0=av[:, :, 0], in1=h0T[:, c, :], op=ALU.mult),
                          reads=[b_a, b_h0T], writes=[b_t0])
                    tk.op("dve", lambda g, bv=bv, t0=t0: g.tensor_tensor(out=bv[:, :, 0], in0=bv[:, :, 0], in1=t0[:, 0:16], op=ALU.add),
                          reads=[b_thi, b_t0], writes=[b_thi])
                    tk.op("dve", lambda g, av=av: g.memset(av[:, :, 0], 0.0), reads=[b_t0], writes=[b_a])
                    h32, b_h32 = tmp_ring.get()
                    tk.op("dve", lambda g, a_=a_, thi=thi, h32=h32: g.tensor_tensor_scan(out=h32[:, 0:N], data0=a_[:, 0:N], data1=thi[:, 0:N], initial=0.0,
                                                                                       op0=ALU.mult, op1=ALU.add), reads=[b_a, b_thi], writes=[b_h32])
                    tk.op("act", lambda g, c=c, h32=h32: g.activation(out=h16[:, c, :], in_=h32[:, 0:N], func=AF.Copy), reads=[b_h32], writes=[b_h16])
                    tk.op("dve", lambda g, c=c, h32=h32: g.tensor_copy(out=hlast[:, c, :], in_=h32[:, 0:N].rearrange("p (b t) -> p b t", t=8)[:, :, 7]),
                          reads=[b_h32], writes=[b_hlast])
                cvo = sbt(e1, "cvo", [48, 512], F32); b_cvo = Buf("cvo")
                lro = sbt(e1, "lro", [16, 512], F32); b_lro = Buf("lro")
                cvi = sbt(e1, "cvi", [128, 4, 16, 3], F32); b_cvi = Buf("cvi")
                tk.op("dve", lambda g: g.tensor_copy(out=cvi[:], in_=uext[:, :, :, 8:11]), reads=[b_uext], writes=[b_cvi])
                for c in range(4):
                    pm, b_pm = mm_ring.get()
                    tk.op("pe", lambda g, c=c, pm=pm: g.transpose(out=pm[0:48, 0:128], in_=cvi[:, c].rearrange("p b r -> p (b r)"), identity=identf[:]),
                          reads=[b_cvi, b_identf], writes=[b_pm])
                    tk.op("dve", lambda g, c=c, pm=pm: g.tensor_copy(out=cvo[:, c * 128:(c + 1) * 128], in_=pm[0:48, 0:128]), reads=[b_pm], writes=[b_cvo])
                    pm2, b_pm2 = mm_ring.get()
                    tk.op("pe", lambda g, c=c, pm2=pm2: g.transpose(out=pm2[0:16, 0:128], in_=hlast[:, c, :], identity=identf[:]),
                          reads=[b_hlast, b_identf], writes=[b_pm2])
                    tk.op("dve", lambda g, c=c, pm2=pm2: g.tensor_copy(out=lro[:, c * 128:(c + 1) * 128], in_=pm2[0:16, 0:128]), reads=[b_pm2], writes=[b_lro])
                tk.dma("sp", convs, cvo[:], reads=[b_cvo])
                tk.dma("sp", lrus, lro[:], reads=[b_lro])

                qn16 = sbt(e1, "qn16S", [128, 1, 256], BF16); b_qn16 = Buf("qn16S")
                qlT = sbt(e1, "qlTS", [128, 2, 128], BF16); b_qlT = Buf("qlTS")
                q16 = sbt(e1, "q16S", [128, 1, 8, 96], BF16); b_q16 = Buf("q16S")
                tcq = sbt(e1, "tcqS", [128, 8, 2, 16], F32); tsq = sbt(e1, "tsqS", [128, 8, 2, 16], F32); b_tcq = Buf("tcqS")
                qpe32 = sbt(e1, "qpe32S", [128, 1, 8, 32], F32); b_qpe32 = Buf("qpe32S")
                csx = sbt(e1, "csxS", [128, 2, 1, 8, 16], F32); b_csx = Buf("csxS")
                q_tile(1, hT, b_hT, Wq, b_Wq, Wuq, b_Wuq, mm_ring, tpg, cst[:, 2], cst[:, 3], b_cst, st2, b_st2, junk, b_junk,
                       qn16, b_qn16, qlT, b_qlT, q16, b_q16, tcq, tsq, b_tcq, qpe32, b_qpe32, csx, b_csx)
                for h in range(NH):
                    tp, b_tp = tpg.get()
                    for c in range(2):
                        tk.op("pe", lambda g, h=h, c=c, tp=tp: g.transpose(out=tp[0:64, c * 128:(c + 1) * 128], in_=Wuk16[:, c, h * 64:(h + 1) * 64], identity=ident[:]),
                              reads=[b_Wuk16, b_ident], writes=[b_tp])
                    copy_op(alt("act", "dve"), WukT[:, h, :], tp[0:64, 0:256], [b_tp], [b_WukT])
                qnT = sbt(e1, "qnT", [64, 8, 128], BF16); b_qnT = Buf("qnT")
                for h in range(NH):
                    tp, b_tp = tpg.get()
                    tk.op("pe", lambda g, h=h, tp=tp: g.transpose(out=tp[0:64, 0:128], in_=q16[:, 0, h, 0:64], identity=ident[:]), reads=[b_q16, b_ident], writes=[b_tp])
                    tk.op("pe", lambda g, h=h, tp=tp: g.transpose(out=tp[0:32, 128:256], in_=q16[:, 0, h, 64:96], identity=ident[:]), reads=[b_q16, b_ident], writes=[b_tp])
                    copy_op("dve", qnT[:, h, :], tp[0:64, 0:128], [b_tp], [b_qnT])
                    copy_op("act", QS[0:32, 2, :, h, :], tp[0:32, 128:256].rearrange("p (b t) -> p b t", t=8), [b_tp], [b_QS])
                    for c in range(2):
                        pm, b_pm = mm_ring.get()
                        tk.op("pe", lambda g, h=h, c=c, pm=pm: g.matmul(pm[:, 0:128], lhsT=WukT[:, h, c * 128:(c + 1) * 128], rhs=qnT[:, h, :], start=True, stop=True),
                              reads=[b_WukT, b_qnT], writes=[b_pm])
                        copy_op(alt("act", "dve"), QS[:, c, :, h, :], pm[:, 0:128].rearrange("p (b t) -> p b t", t=8), [b_pm], [b_QS])
                qrep = sbt(e1, "qrep", [128, 8, 4, 32], BF16); b_qrep = Buf("qrep")
                tk.op("dve", lambda g: g.tensor_copy(out=qrep[:], in_=q16[:, 0, :, 64:96].unsqueeze(2).to_broadcast([128, 8, 4, 32])), reads=[b_q16], writes=[b_qrep])
                tk.op("pool", lambda g: g.memset(QB[:], 0.0), writes=[b_QB])
                for h in range(NH):
                    tp, b_tp = tpg.get()
                    tk.op("pe", lambda g, h=h, tp=tp: g.transpose(out=tp[:, 0:128], in_=qrep[:, h].rearrange("p j d -> p (j d)"), identity=ident[:]),
                          reads=[b_qrep, b_ident], writes=[b_tp])
                    for j in range(4):
                        copy_op(alt("act", "dve"), QB[32 * j:32 * j + 32, :, j, h * 8:(h + 1) * 8],
                                tp[32 * j:32 * j + 32, 0:128].rearrange("p (b t) -> p b t", t=8), [b_tp], [b_QB])
                tk.wait_all("sp", [b_cvo, b_lro, b_ckv32, b_kpe32])
                tk.barrier()
            if stop == "S1":
                raise _Stop()

            with contextlib.ExitStack() as e2:
                tpT = ring_ps(e2, "tpT", 2, [128, 2048], BF16)
                mm_ring = ring_ps(e2, "mmS2", 1, [128, 512], F32)
                sS_t = pst(e2, "sS", [128, 512], F32)
                sS = Ring([(sS_t[:, 0:256], Buf("sS0")), (sS_t[:, 256:512], Buf("sS1"))])
                oS = ring_ps(e2, "oS", 2, [128, 512], F32)
                Wuvp = sbt(e2, "Wuvp", [128, 2, 8, 128], BF16); b_Wuvp = Buf("Wuvp")
                tk.op("pool", lambda g: g.memset(Wuvp[:], 0.0), writes=[b_Wuvp])
                for c in range(2):
                    for h in range(NH):
                        par = h % 2
                        tk.dma("pool", Wuvp[:, c, h, 64 * par:64 * par + 64], w_uv[c * 128:(c + 1) * 128, h * 64:(h + 1) * 64], writes=[b_Wuvp])
                smk = sbt(e2, "smk", [128, 16 * 64], BF16); b_smk = Buf("smk")
                tk.dma("sp", smk[:], smask, writes=[b_smk])
                n32r = ring_sb(e2, "n32", 3, [128, 4, 288], F32)
                n16r = ring_sb(e2, "n16", 3, [128, 1156], BF16)
                for it in n16r.items:
                    tk.op("pool", lambda g, it=it: g.memset(it[0][:, 0:1028].rearrange("p (j e) -> p j e", e=257)[:, :, 256:257], 1.0), writes=[it[1]])
                T16r = ring_sb(e2, "T16", 2, [128, 9, 128], BF16)
                PSr = ring_sb(e2, "PSs", 2, [128, 256], BF16)
                OT = sbt(e2, "OT", [128, 2, 8, 16, 8], BF16); b_OT = Buf("OT")
                ol16 = sbt(e2, "ol16", [64, 256], BF16); b_ol16 = Buf("ol16")
                rden = sbt(e2, "rden", [64, 1], F32); b_rden = Buf("rden")
                tk.wait_all("sp", [b_ptab])

                def stage1(b, gq):
                    n32, b_n32 = n32r.get()
                    tk.wait_all("sp", [b_n32])
                    for j in range(4):
                        v0, v1, v2 = page_offsets(ptab_sb[b:b + 1, gq * 4 + j:gq * 4 + j + 1])
                        tk.dma("sp", n32[:, j, 0:256].bitcast(U8), ckv0_b[bass.ds(v0, 128 * 1024)].rearrange("(p d) -> p d", d=1024), writes=[b_n32],
                               bounds_check="skip_entire_dma")
                        tk.dma("sp", n32[:, j, 0:256].bitcast(U8), ckv1_b[bass.ds(v1, 128 * 1024)].rearrange("(p d) -> p d", d=1024), writes=[b_n32],
                               bounds_check="skip_entire_dma")
                        tk.dma("sp", n32[:, j, 256:288].bitcast(U8), kpe_b[bass.ds(v2, 128 * 128)].rearrange("(p d) -> p d", d=128), writes=[b_n32])
                    n16, b_n16 = n16r.get()
                    tk.op("pool", lambda g: g.tensor_copy(out=n16[:, 0:1028].rearrange("p (j e) -> p j e", e=257)[:, :, 0:256], in_=n32[:, :, 0:256]),
                          reads=[b_n32], writes=[b_n16])
                    tk.op("pool", lambda g: g.tensor_copy(out=n16[:, 1028:1156].rearrange("p (j d) -> p j d", d=32), in_=n32[:, :, 256:288]),
                          reads=[b_n32], writes=[b_n16])
                    tp, b_tp = tpT.get()
                    for j in range(4):
                        for c in range(2):
                            blk = 2 * j + c
                            tk.op("pe", lambda g, j=j, c=c, blk=blk: g.transpose(out=tp[:, blk * 128:(blk + 1) * 128], in_=n16[:, j * 257 + c * 128:j * 257 + (c + 1) * 128], identity=ident[:]),
                                  reads=[b_n16, b_ident], writes=[b_tp])
                    tk.op("pe", lambda g: g.transpose(out=tp[:, 1024:1152], in_=n16[:, 1028:1156], identity=ident[:]), reads=[b_n16, b_ident], writes=[b_tp])
                    T16, b_T16 = T16r.get()
                    copy_op("act", T16[:, 0:8, :], tp[:, 0:1024].rearrange("p (k n) -> p k n", k=8), [b_tp], [b_T16])
                    copy_op("dve", T16[:, 8, :], tp[:, 1024:1152], [b_tp], [b_T16])
                    return n16, b_n16, T16, b_T16

                def stage2(b, st1, po, b_po, first):
                    n16, b_n16, T16, b_T16 = st1
                    ps_, b_ps = sS.get()
                    tk.op("pe", lambda g: g.matmul(ps_[:, 0:256], lhsT=T16[:, 8, :], rhs=QB[:, b].rearrange("p j q -> p (j q)"), start=True, stop=False),
                          reads=[b_T16, b_QB], writes=[b_ps])
                    for j in range(4):
                        for c in range(2):
                            tk.op("pe", lambda g, j=j, c=c: g.matmul(ps_[:, j * 64:(j + 1) * 64], lhsT=T16[:, 2 * j + c, :],
                                                                    rhs=QS[:, c, b].rearrange("p h t -> p (h t)"), start=False, stop=(c == 1 and j == 3)),
                                  reads=[b_T16, b_QS], writes=[b_ps])
                    pts, b_pts = PSr.get()
                    tk.op("act", lambda g: g.activation(out=pts[:], in_=ps_, func=AF.Exp), reads=[b_ps], writes=[b_pts])
                    for j in range(4):
                        tk.op("pe", lambda g, j=j: g.matmul(po[0:64, 0:257], lhsT=pts[:, j * 64:(j + 1) * 64], rhs=n16[:, j * 257:(j + 1) * 257],
                                                            start=(first and j == 0), stop=False),
                              reads=[b_pts, b_n16], writes=[b_po])

                groups = [(b, gq) for b in range(16) for gq in range(NG)]
                pend = stage1(*groups[0]) if groups else None
                po = b_po = None
                for gi, (b, gq) in enumerate(groups):
                    cur = pend
                    pend = stage1(*groups[gi + 1]) if gi + 1 < len(groups) else None
                    if gq == 0:
                        po, b_po = oS.get()
                    stage2(b, cur, po, b_po, gq == 0)
                    if gq != NG - 1:
                        continue
                    ps_, b_ps = sS.get()
                    for c3 in range(3):
                        cw = KW[c3]
                        tk.op("pe", lambda g, c3=c3, cw=cw, ps_=ps_, b=b: g.matmul(ps_[:, 0:64], lhsT=TN[0:cw, c3, :], rhs=QS[0:cw, c3, b].rearrange("p h t -> p (h t)"),
                                                                                  start=(c3 == 0), stop=(c3 == 2)), reads=[b_TN, b_QS], writes=[b_ps])
                    pts, b_pts = PSr.get()
                    tk.op("act", lambda g, pts=pts, ps_=ps_: g.activation(out=pts[:, 0:64], in_=ps_[:, 0:64], func=AF.Exp), reads=[b_ps], writes=[b_pts])
                    tk.op("pool", lambda g, pts=pts, b=b: g.tensor_tensor(out=pts[:, 0:64], in0=pts[:, 0:64], in1=smk[:, b * 64:(b + 1) * 64], op=ALU.mult),
                          reads=[b_pts, b_smk], writes=[b_pts])
                    tk.op("pe", lambda g, pts=pts, po=po: g.matmul(po[0:64, 0:257], lhsT=pts[:, 0:64], rhs=natn[:], start=(NG == 0), stop=True),
                          reads=[b_pts, b_natn], writes=[b_po])
                    tk.op("dve", lambda g, po=po: g.reciprocal(out=rden[:], in_=po[0:64, 256:257]), reads=[b_po], writes=[b_rden])
                    tk.op("dve", lambda g, po=po: g.tensor_scalar(out=ol16[:], in0=po[0:64, 0:256], scalar1=rden[:, 0:1], scalar2=None, op0=ALU.mult),
                          reads=[b_po, b_rden], writes=[b_ol16])
                    tp, b_tp = tpT.get()
                    for c in range(2):
                        tk.op("pe", lambda g, c=c, tp=tp: g.transpose(out=tp[:, c * 64:(c + 1) * 64], in_=ol16[:, c * 128:(c + 1) * 128], identity=ident[0:64, 0:64]),
                              reads=[b_ol16, b_ident], writes=[b_tp])
                    tk.op("dve", lambda g, tp=tp, b=b: g.tensor_copy(out=OT[:, :, :, b, :], in_=tp[:, 0:128].rearrange("p (c h t) -> p c h t", c=2, h=8)),
                          reads=[b_tp], writes=[b_OT])
                for pr_ in range(4):
                    pm, b_pm = mm_ring.get()
                    i = 0
                    for hh in range(2):
                        h = 2 * pr_ + hh
                        for c in range(2):
                            tk.op("pe", lambda g, h=h, c=c, pm=pm, i=i: g.matmul(pm[:, 0:128], lhsT=Wuvp[:, c, h, :], rhs=OT[:, c, h].rearrange("p b t -> p (b t)"),
                                                                                 start=(i == 0), stop=(i == 3)), reads=[b_Wuvp, b_OT], writes=[b_pm])
                            i += 1
                    copy_op(alt("act", "dve"), AOs[:, pr_, :], pm[:, 0:128], [b_pm], [b_AOs])
                tk.barrier()
            if stop == "S2":
                raise _Stop()

            with contextlib.ExitStack() as e3:
                mm_ring = ring_ps(e3, "mmS3", 4, [128, 512], F32)
                Wg = sbt(e3, "WgS", [128, 8, 3072], BF16); b_Wg = Buf("WgS")
                for k in range(8):
                    tk.dma("pool", Wg[:, k, 0:512], w_in[k * 128:(k + 1) * 128, C_GA:C_GA + 512], writes=[b_Wg])
                    tk.dma("pool", Wg[:, k, 512:3072], w_in[k * 128:(k + 1) * 128, C_GB:C_GB + 2560], writes=[b_Wg])
                Woa, b_Wo = load_w(e3, "WoaS", w_oa, [128, 4, D])
                Wob, _b = load_w(e3, "WobS", w_ob, [128, 4, D], b=b_Wo)
                Wout, _b2 = load_w(e3, "WoutS", w_out, [128, 8, D], b=b_Wo)
                ga16 = sbt(e3, "ga16S", [128, 4, 128], BF16); b_ga = Buf("ga16S")
                gb16 = sbt(e3, "gb16S", [128, 4, 128], BF16); b_gb = Buf("gb16S")
                thm = sbt(e3, "thmS", [128, 16, 128], BF16); b_thm = Buf("thmS")
                mg16 = sbt(e3, "mg16S", [128, 8, 128], BF16); b_mg = Buf("mg16S")
                out_tile(1, xt, b_xt, hT, b_hT, Wg, b_Wg, Woa, Wob, Wout, b_Wo, mm_ring, tmp_ring,
                         lambda j: AOs[:, j, :], b_AOs, lambda c: h16[:, c, :], b_h16,
                         ga16, b_ga, gb16, b_gb, thm, b_thm, mg16, b_mg, Gs, st2, b_st2, junk, b_junk, ys)
                tk.wait_all("sp", out_bufs)
                tk.barrier()
    tk.wait_all("sp", out_bufs)
    tk.barrier()
    tk.close()
    return nc


_CACHE = {}
_DEV_STOP = None


def _rope_tables(pos):
    inv = np.power(np.float32(10000.0), -np.arange(16, dtype=np.float32) / np.float32(16.0)).astype(np.float32)
    ang = pos.astype(np.float32)[:, None] * inv[None, :]
    return np.cos(ang).astype(np.float32), np.sin(ang).astype(np.float32)


def own_tiles(role, NT):
    res = []
    for m in range(NT // 2):
        res.append(2 * m + (m % 2 if role == 0 else 1 - (m % 2)))
    return res


def kernel(x_prompt, x_sample, c_prompt, c_sample, cache_ckv, cache_kpe, state_conv, state_lru, page_table,
           ada_w, ada_b, norm_w, w_in, q_norm_w, kv_norm_w, w_uq, w_uk, w_uv, conv_w, conv_b,
           lru_wa, lru_ba, lru_wx, lru_bx, lru_lambda, w_oa, w_ob, w_out, final_norm_w):
    f32 = np.float32
    A = lambda v: np.ascontiguousarray(np.asarray(v))
    x_prompt = A(x_prompt); x_sample = A(x_sample)
    B, SEQ, _ = x_prompt.shape
    DB, T, _ = x_sample.shape
    NP = page_table.shape[1]
    NPOOL = cache_ckv.shape[1]
    assert B == 4 and DB == 128 and T == 8 and SEQ % 1024 == 0 and NP % 4 == 0
    NT = SEQ // 512
    NS = NT // 2
    key = (SEQ, NP, NPOOL)
    if key not in _CACHE:
        _CACHE[key] = build(SEQ, NP, NPOOL, stop=_DEV_STOP)
    nc = _CACHE[key]

    ckv_pool = A(cache_ckv)[0]
    kpe_pool = A(cache_kpe)[0]
    cosf, sinf = _rope_tables(np.arange(SEQ))
    past = NP * 128
    cs_, ss_ = _rope_tables(past + np.arange(T))
    coss = np.tile(cs_, (16, 1)); sins = np.tile(ss_, (16, 1))
    sc = np.float32(ATTN_SCALE)
    kb_, ks_ = np.divmod(np.arange(128), 8)
    smask = np.zeros((128, 16, 8, 8), np.float32)
    for b in range(16):
        for t in range(8):
            smask[:, b, :, t] = ((kb_ == b) & (ks_ <= t))[:, None]
    smask = smask.reshape(128, 16 * 64).astype(ml_dtypes.bfloat16)
    shared = dict(
        ckv0=ckv_pool[:NPOOL // 2], ckv1=ckv_pool[NPOOL // 2:], kpe=kpe_pool,
        ada_w=A(ada_w)[0], ada_b=A(ada_b)[0][None, :], norm_w=A(norm_w)[0][None, :], w_in=A(w_in)[0],
        q_norm_w=A(q_norm_w)[0][None, :], kv_norm_w=A(kv_norm_w)[0][None, :],
        w_uq=A(w_uq)[0].reshape(256, 768), w_uk=A(w_uk)[0].reshape(256, 512), w_uv=A(w_uv)[0].reshape(256, 512),
        conv_w=A(conv_w)[0], conv_b=A(conv_b)[0][None, :], lru_wa=A(lru_wa)[0], lru_ba=A(lru_ba)[0][None, :],
        lru_wx=A(lru_wx)[0], lru_bx=A(lru_bx)[0][None, :], lru_lambda=A(lru_lambda)[0][None, :],
        w_oa=A(w_oa)[0], w_ob=A(w_ob)[0], w_out=A(w_out)[0], fnw=A(final_norm_w)[None, :],
        cosf=cosf, sinf=sinf, coss=coss, sins=sins, cossq=coss * sc, sinsq=sins * sc, smask=smask,
    )
    in_maps = []
    pj = np.arange(128)[:, None, None]
    dj = np.arange(8)[None, :, None]
    jj = np.arange(512)[None, None, :]
    for c in range(8):
        b, role = c // 2, c % 2
        tiles = own_tiles(role, NT)
        idx = np.concatenate([np.arange(t * 512, (t + 1) * 512) for t in tiles])
        mk = np.zeros((2, 128, 8, 512), np.float32)
        for par in range(2):
            m = par
            off = 512 * (tiles[m] - 2 * m) if m < NS else 0
            mk[par] = (jj + off >= 128 * dj + pj)
        own_tab = np.zeros((1, 16), np.int32)
        own_tab[0, :NS] = tiles
        d = dict(shared)
        d.update(
            xf=x_prompt[b], xo=np.ascontiguousarray(x_prompt[b][idx]), xs=x_sample[16 * c:16 * c + 16].reshape(128, D),
            cp=A(c_prompt)[b:b + 1], cs=A(c_sample)[16 * c:16 * c + 16],
            sconv=A(state_conv)[0, 16 * c:16 * c + 16].reshape(48, 512), slru=A(state_lru)[0, 16 * c:16 * c + 16],
            ptab=np.ascontiguousarray(A(page_table)[16 * c:16 * c + 16].astype(np.int32)),
            coso=cosf[idx] * sc, sino=sinf[idx] * sc,
            masks=mk.reshape(2, 128, 8 * 512).astype(ml_dtypes.bfloat16), own_tab=own_tab,
        )
        in_maps.append(d)
    res = run_bass_kernel_spmd(nc, in_maps, core_ids=list(range(8)))
    R = res.results
    y_prompt = np.zeros((B, SEQ, D), f32)
    y_sample = np.zeros((DB, T, D), f32)
    n_ckv_p = np.zeros((1, B, SEQ, 256), f32); n_kpe_p = np.zeros((1, B, SEQ, 32), f32)
    n_conv_p = np.zeros((1, B, 3, 512), f32); n_lru_p = np.zeros((1, B, 512), f32)
    n_ckv_s = np.zeros((1, DB, T, 256), f32); n_kpe_s = np.zeros((1, DB, T, 32), f32)
    n_conv_s = np.zeros((1, DB, 3, 512), f32); n_lru_s = np.zeros((1, DB, 512), f32)
    for c in range(8):
        b, role = c // 2, c % 2
        tiles = own_tiles(role, NT)
        r = R[c]
        for m, t in enumerate(tiles):
            y_prompt[b, t * 512:(t + 1) * 512] = r["yo"][m * 512:(m + 1) * 512]
            n_ckv_p[0, b, t * 512:(t + 1) * 512] = r["ckvp"][t * 512:(t + 1) * 512]
            n_kpe_p[0, b, t * 512:(t + 1) * 512] = r["kpep"][t * 512:(t + 1) * 512]
        if role == 0:
            n_conv_p[0, b] = r["convp"]
            n_lru_p[0, b] = r["lrup"][0]
        y_sample[16 * c:16 * c + 16] = r["ys"].reshape(16, 8, D)
        n_ckv_s[0, 16 * c:16 * c + 16] = r["ckvs"].reshape(16, 8, 256)
        n_kpe_s[0, 16 * c:16 * c + 16] = r["kpes"].reshape(16, 8, 32)
        n_conv_s[0, 16 * c:16 * c + 16] = r["convs"].reshape(16, 3, 512)
        n_lru_s[0, 16 * c:16 * c + 16] = r["lrus"]
    return (y_prompt, y_sample, n_ckv_p, n_kpe_p, n_conv_p, n_lru_p, n_ckv_s, n_kpe_s, n_conv_s, n_lru_s)
```

```python
import contextlib
import math
import numpy as np
import ml_dtypes
import concourse.bass as bass
import concourse.mybir as mybir
from concourse.bass_utils import run_bass_kernel_spmd

F32 = mybir.dt.float32
BF16 = mybir.dt.bfloat16
I32 = mybir.dt.int32
U8 = mybir.dt.uint8
AF = mybir.ActivationFunctionType
ALU = mybir.AluOpType

D = 1024
NH = 8
EPS = 1e-6
ATTN_SCALE = 1.0 / math.sqrt(96.0)
C_Q, C_KV, C_KPE, C_GA, C_U, C_GB, C_MA, C_MB = 0, 256, 512, 544, 1056, 1568, 2080, 3104


class Buf:
    __slots__ = ("name", "w", "r", "dsem", "dcnt")

    def __init__(self, name=""):
        self.name = name
        self.w = None
        self.r = []
        self.dsem = None
        self.dcnt = 0


class Trk:
    ENG = ("pe", "act", "dve", "pool", "sp")

    def __init__(self, nc):
        self.nc = nc
        self.es = contextlib.ExitStack()
        self.sems = {}
        self.cnt = {}
        self.seen = {e: {} for e in self.ENG}
        self.eng = {"pe": nc.tensor, "act": nc.scalar, "dve": nc.vector, "pool": nc.gpsimd, "sp": nc.sync}
        for e in self.ENG:
            self._mksem("E_" + e)
        self.nd = 0
        self.dsems = []
        self.free_dsems = {}
        self.semq = {}

    def _mksem(self, key):
        self.sems[key] = self.es.enter_context(self.nc.semaphore(key))
        self.cnt[key] = 0
        return key

    @staticmethod
    def _deps(reads, writes):
        deps = []
        for b in reads:
            if b.w is not None:
                deps.append(b.w)
        for b in writes:
            if b.w is not None:
                deps.append(b.w)
            deps.extend(b.r)
        return deps

    def _wait(self, e, deps, skip_self=False):
        best = {}
        for k, v in deps:
            if skip_self and k == "E_" + e:
                continue
            if v > best.get(k, 0):
                best[k] = v
        seen = self.seen[e]
        for k, v in best.items():
            if seen.get(k, 0) >= v:
                continue
            self.eng[e].wait_ge(self.sems[k], v)
            seen[k] = v

    def op(self, e, fn, reads=(), writes=()):
        deps = self._deps(reads, writes)
        self._wait(e, deps, skip_self=(e == "pe"))
        ins = fn(self.eng[e])
        k = "E_" + e
        self.cnt[k] += 1
        ins.then_inc(self.sems[k], 1)
        tok = (k, self.cnt[k])
        for b in reads:
            b.r.append(tok)
        for b in writes:
            b.w = tok
            b.r = []
        return tok

    def dma(self, q, out, in_, reads=(), writes=(), owner=None, **kw):
        deps = self._deps(reads, writes)
        self._wait(q, deps)
        if owner is None:
            owner = writes[0] if writes else reads[0]
        if owner.dsem is None:
            fl = self.free_dsems.setdefault(q, [])
            if fl:
                owner.dsem = fl.pop()
            else:
                self.nd += 1
                owner.dsem = self._mksem("D%s%d" % (q, self.nd))
                self.semq[owner.dsem] = q
            self.dsems.append(owner)
        ins = self.eng[q].dma_start(out=out, in_=in_, **kw)
        k = owner.dsem
        self.cnt[k] += 16
        ins.then_inc(self.sems[k], 16)
        tok = (k, self.cnt[k])
        for b in reads:
            b.r.append(tok)
        for b in writes:
            b.w = tok
            b.r = []
        return tok

    def wait_all(self, e, bufs):
        deps = []
        for b in bufs:
            if b.w is not None:
                deps.append(b.w)
            deps.extend(b.r)
        self._wait(e, deps)

    def barrier(self):
        deps = [("E_" + e, self.cnt["E_" + e]) for e in ("pe", "act", "dve", "pool") if self.cnt["E_" + e] > 0]
        used = sorted(set(b.dsem for b in self.dsems if b.dsem is not None))
        deps += [(k, self.cnt[k]) for k in used if self.cnt[k] > 0]
        for e in self.ENG:
            self._wait(e, deps)
        for b in self.dsems:
            b.dsem = None
        self.dsems = []
        for k in used:
            fl = self.free_dsems.setdefault(self.semq[k], [])
            if k not in fl:
                fl.append(k)

    def close(self):
        self.es.close()


class Ring:
    def __init__(self, items):
        self.items = items
        self.i = 0

    def get(self):
        it = self.items[self.i % len(self.items)]
        self.i += 1
        return it


class _Stop(Exception):
    pass


def build(SEQ, NP, NPOOL, stop=None):
    NT = SEQ // 512
    NS = NT // 2
    NKB = SEQ // 128
    NOWN = NS * 512
    nc = bass.Bass("TRN2", target_bir_lowering=False)
    tk = Trk(nc)

    def din(name, shape, dt=F32):
        return nc.dram_tensor(name, list(shape), dt, kind="ExternalInput").ap()

    def dout(name, shape, dt=F32):
        return nc.dram_tensor(name, list(shape), dt, kind="ExternalOutput").ap()

    xf = din("xf", [SEQ, D]); xo = din("xo", [NOWN, D]); xs = din("xs", [128, D])
    cp = din("cp", [1, D]); cs = din("cs", [16, D])
    HP = NPOOL // 2
    pl0 = din("pl0", [HP, 128, 288]); pl1 = din("pl1", [NPOOL - HP, 128, 288])
    sconv = din("sconv", [48, 512]); slru = din("slru", [16, 512])
    ptab = din("ptab", [16, NP], I32)
    ada_w = din("ada_w", [D, 3 * D]); ada_b = din("ada_b", [1, 3 * D]); norm_w = din("norm_w", [1, D])
    w_in = din("w_in", [D, 4128]); q_norm_w = din("q_norm_w", [1, 256]); kv_norm_w = din("kv_norm_w", [1, 256])
    w_uq = din("w_uq", [256, 768]); w_uk = din("w_uk", [256, 512]); w_uv = din("w_uv", [256, 512])
    conv_w = din("conv_w", [4, 512]); conv_b = din("conv_b", [1, 512])
    lru_wa = din("lru_wa", [8, 64, 64]); lru_ba = din("lru_ba", [1, 512])
    lru_wx = din("lru_wx", [8, 64, 64]); lru_bx = din("lru_bx", [1, 512]); lru_lambda = din("lru_lambda", [1, 512])
    w_oa = din("w_oa", [512, D]); w_ob = din("w_ob", [512, D]); w_out = din("w_out", [D, D]); fnw = din("fnw", [1, D])
    cosf = din("cosf", [SEQ, 16]); sinf = din("sinf", [SEQ, 16])
    coso = din("coso", [NOWN, 16]); sino = din("sino", [NOWN, 16])
    coss = din("coss", [128, 16]); sins = din("sins", [128, 16])
    cossq = din("cossq", [128, 16]); sinsq = din("sinsq", [128, 16])
    masks = din("masks", [2, 128, 8 * 512], BF16)
    smask = din("smask", [128, 16 * 64], BF16)

    yo = dout("yo", [NOWN, D]); ys = dout("ys", [128, D])
    ckvp = dout("ckvp", [SEQ, 256]); kpep = dout("kpep", [SEQ, 32])
    convp = dout("convp", [3, 512]); lrup = dout("lrup", [1, 512])
    ckvs = dout("ckvs", [128, 256]); kpes = dout("kpes", [128, 32])
    convs = dout("convs", [48, 512]); lrus = dout("lrus", [16, 512])

    lru_scr = nc.dram_tensor("lru_scr", [NT, 128, 4 * 512], BF16, kind="Internal").ap()
    qt_scr = nc.dram_tensor("qt_scr", [NH, 96, NOWN], BF16, kind="Internal").ap()
    lru_scr_b = lru_scr.bitcast(U8).rearrange("t p n -> (t p n)")
    pl0_b = pl0.bitcast(U8).rearrange("n p d -> (n p d)")
    pl1_b = pl1.bitcast(U8).rearrange("n p d -> (n p d)")
    PGB = 128 * 288 * 4
    b_lru_scr = [Buf("lruscr%d" % i) for i in range(NT)]
    b_qt_scr = [Buf("qtscr%d" % i) for i in range(NS)]
    own_tab = din("own_tab", [1, 16], I32)
    ao_scr = nc.dram_tensor("ao_scr", [128, 4, NOWN], BF16, kind="Internal").ap()
    b_ao_scr = Buf("ao_scr")

    out_bufs = []

    top = contextlib.ExitStack()

    uid = {"n": 0}

    def sbt(es, name, shape, dt):
        uid["n"] += 1
        return es.enter_context(nc.sbuf_tensor("%s_%d" % (name, uid["n"]), list(shape), dt))

    def pst(es, name, shape, dt):
        uid["n"] += 1
        return es.enter_context(nc.psum_tensor("%s_%d" % (name, uid["n"]), list(shape), dt))

    def ring_sb(es, name, n, shape, dt):
        return Ring([(sbt(es, "%s%d" % (name, i), shape, dt), Buf("%s%d" % (name, i))) for i in range(n)])

    def ring_ps(es, name, n, shape, dt):
        return Ring([(pst(es, "%s%d" % (name, i), shape, dt), Buf("%s%d" % (name, i))) for i in range(n)])

    rr = {"n": 0}

    def alt(*engs):
        rr["n"] += 1
        return engs[rr["n"] % len(engs)]

    def copy_op(e, out, in_, reads, writes, scale=None):
        if e == "act":
            if scale is None:
                return tk.op("act", lambda g: g.activation(out=out, in_=in_, func=AF.Copy), reads, writes)
            return tk.op("act", lambda g: g.activation(out=out, in_=in_, func=AF.Copy, scale=float(scale)), reads, writes)
        if scale is None:
            return tk.op(e, lambda g: g.tensor_copy(out=out, in_=in_), reads, writes)
        return tk.op(e, lambda g: g.tensor_scalar(out=out, in0=in_, scalar1=float(scale), scalar2=None, op0=ALU.mult), reads, writes)

    with contextlib.suppress(_Stop):
        ident = sbt(top, "ident", [128, 128], BF16); b_ident = Buf("ident")
        identf = sbt(top, "identf", [128, 128], F32); b_identf = Buf("identf")
        ones32 = sbt(top, "ones32", [128, 128], F32); b_ones = Buf("ones32")
        Ap = sbt(top, "Ap", [128, 8], F32); Bp = sbt(top, "Bp", [128, 8], F32); b_mod = Buf("mod")
        As = sbt(top, "As", [128, 8, 16], F32); Bs = sbt(top, "Bs", [128, 8, 16], F32)
        Gp = sbt(top, "Gp", [128, D], F32); Gs = sbt(top, "Gs", [128, D], F32); b_G = Buf("G")
        fnw_bc = sbt(top, "fnw_bc", [128, D], F32); b_fnw = Buf("fnw")
        qnw_bc = sbt(top, "qnw_bc", [128, 256], F32); kvw_bc = sbt(top, "kvw_bc", [128, 256], F32); b_nw = Buf("nw")
        vecT = sbt(top, "vecT", [128, 72], F32); b_vecT = Buf("vecT")
        V_ADAB, V_NW, V_CW, V_CB, V_BA, V_BX, V_LAM, V_CP = 0, 24, 32, 48, 52, 56, 60, 64
        lcoef = sbt(top, "lcoef", [128, 16], F32); b_lcoef = Buf("lcoef")
        ptab_sb = sbt(top, "ptab_sb", [16, NP], I32); b_ptab = Buf("ptab")
        own_sb = sbt(top, "own_sb", [1, 16], I32); b_own = Buf("own")

        R = {n_: top.enter_context(nc.sync.register("r_" + n_)) for n_ in ("pg", "c0", "c1", "o0", "o1", "t", "ok")}
        RV = bass.RuntimeValue.of_register_unchecked

        def dyn_off(idx_ap, mult):
            nc.sync.reg_load(R["pg"], idx_ap)
            nc.sync.reg_alu(R["o0"], R["pg"], int(mult), ALU.mult)
            return RV(R["o0"])

        RA = {n_: top.enter_context(nc.scalar.register("ra_" + n_)) for n_ in ("pg", "o0", "o1", "ok")}

        def page_offsets(idx_ap, q="sp"):
            e, rg = (nc.sync, R) if q == "sp" else (nc.scalar, RA)
            e.reg_load(rg["pg"], idx_ap)
            e.reg_alu(rg["o0"], rg["pg"], PGB, ALU.mult)
            e.reg_alu(rg["o1"], rg["o0"], HP * PGB, ALU.subtract)
            return RV(rg["o0"]), RV(rg["o1"])

        tk.op("pool", lambda g: g.memset(identf[:], 1.0), writes=[b_identf])
        tk.op("pool", lambda g: g.affine_select(out=identf[:], in_=identf[:], pattern=[[-1, 128]], compare_op=ALU.is_equal,
                                                 fill=0.0, base=0, channel_multiplier=1), reads=[b_identf], writes=[b_identf])
        tk.op("dve", lambda g: g.tensor_copy(out=ident[:], in_=identf[:]), reads=[b_identf], writes=[b_ident])
        tk.op("pool", lambda g: g.memset(ones32[:], 1.0), writes=[b_ones])
        tk.dma("sp", ptab_sb[:], ptab, writes=[b_ptab])
        tk.dma("sp", own_sb[:], own_tab, writes=[b_own])
        tk.dma("sp", fnw_bc[:], fnw.partition_broadcast(128), writes=[b_fnw])
        tk.dma("sp", qnw_bc[:], q_norm_w.partition_broadcast(128), writes=[b_nw])
        tk.dma("sp", kvw_bc[:], kv_norm_w.partition_broadcast(128), writes=[b_nw])

        with contextlib.ExitStack() as es:
            stage = sbt(es, "stage", [128, 128], F32); b_stage = Buf("stage")
            adaw = sbt(es, "adaw", [128, 8, 3 * D], BF16); b_adaw = Buf("adaw")
            cs_sb = sbt(es, "cs_sb", [16, D], F32); b_cs = Buf("cs")
            csT = sbt(es, "csT", [128, 8, 16], F32); b_csT = Buf("csT")
            scT = sbt(es, "scT", [128, 8, 17], BF16); b_scT = Buf("scT")
            scTp = sbt(es, "scTp", [128, 8, 128], BF16); b_scTp = Buf("scTp")
            scTs = sbt(es, "scTs", [128, 8, 128], BF16); b_scTs = Buf("scTs")
            adab_bc = sbt(es, "adab_bc", [128, D], F32); b_adab = Buf("adab")
            modT = sbt(es, "modT", [128, 24, 17], F32); b_modT = Buf("modT")
            tmp0 = sbt(es, "tmp0", [128, 8, 17], F32); b_tmp0 = Buf("tmp0")
            tmp1 = sbt(es, "tmp1", [128, 8, 17], F32); b_tmp1 = Buf("tmp1")
            sm = [sbt(es, "sm%d" % i, [128, 4], F32) for i in range(6)]; b_sm = [Buf("sm%d" % i) for i in range(6)]
            psA = pst(es, "psA", [128, 512], F32); b_psA = Buf("psA")
            psB = pst(es, "psB", [128, 512], F32); b_psB = Buf("psB")
            psC = pst(es, "psC", [128, 512], F32); b_psC = Buf("psC")

            tk.op("pool", lambda g: g.memset(stage[:], 0.0), writes=[b_stage])
            rows = [(V_ADAB, 24, ada_b), (V_NW, 8, norm_w), (V_CB, 4, conv_b), (V_BA, 4, lru_ba), (V_BX, 4, lru_bx),
                    (V_LAM, 4, lru_lambda), (V_CP, 8, cp)]
            for r0, n, src in rows:
                tk.dma("sp", stage[r0:r0 + n, :], src.rearrange("o (j p) -> (o j) p", p=128), writes=[b_stage])
            tk.dma("sp", stage[V_CW:V_CW + 16, :], conv_w.rearrange("k (j p) -> (k j) p", p=128), writes=[b_stage])
            tk.dma("sp", cs_sb[:], cs, writes=[b_cs])
            tk.dma("sp", adab_bc[:], ada_b[:, 2 * D:3 * D].partition_broadcast(128), writes=[b_adab])
            for k in range(8):
                tk.dma("pool", adaw[:, k, :], ada_w[k * 128:(k + 1) * 128, :], writes=[b_adaw])
            tk.op("pe", lambda g: g.transpose(out=psA[:, 0:72], in_=stage[0:72, :], identity=identf[0:72, 0:72]),
                  reads=[b_stage, b_identf], writes=[b_psA])
            tk.op("dve", lambda g: g.tensor_copy(out=vecT[:], in_=psA[:, 0:72]), reads=[b_psA], writes=[b_vecT])
            for k in range(8):
                tk.op("pe", lambda g, k=k: g.transpose(out=psB[:, k * 16:(k + 1) * 16], in_=cs_sb[:, k * 128:(k + 1) * 128],
                                                        identity=identf[0:16, 0:16]), reads=[b_cs, b_identf], writes=[b_psB])
            tk.op("dve", lambda g: g.tensor_copy(out=csT[:], in_=psB[:, 0:128].rearrange("p (k b) -> p k b", k=8)),
                  reads=[b_psB], writes=[b_csT])
            tk.op("dve", lambda g: g.tensor_copy(out=tmp0[:, :, 0:1], in_=vecT[:, V_CP:V_CP + 8].unsqueeze(2)),
                  reads=[b_vecT], writes=[b_tmp0])
            tk.op("dve", lambda g: g.tensor_copy(out=tmp0[:, :, 1:17], in_=csT[:]), reads=[b_csT], writes=[b_tmp0])
            tk.op("act", lambda g: g.activation(out=tmp1[:], in_=tmp0[:], func=AF.Tanh, scale=0.5), reads=[b_tmp0], writes=[b_tmp1])
            tk.op("dve", lambda g: g.scalar_tensor_tensor(out=tmp1[:], in0=tmp1[:], scalar=1.0, in1=tmp0[:], op0=ALU.add, op1=ALU.mult),
                  reads=[b_tmp1, b_tmp0], writes=[b_tmp1])
            tk.op("dve", lambda g: g.tensor_scalar(out=scT[:], in0=tmp1[:], scalar1=0.5, scalar2=None, op0=ALU.mult),
                  reads=[b_tmp1], writes=[b_scT])
            tk.op("dve", lambda g: g.tensor_copy(out=scTp[:], in_=scT[:, :, 0:1].to_broadcast([128, 8, 128])),
                  reads=[b_scT], writes=[b_scTp])
            for k in range(8):
                tk.op("dve", lambda g, k=k: g.tensor_copy(out=scTs[:, k, :].rearrange("p (b t) -> p b t", t=8),
                                                           in_=scT[:, k, 1:17].unsqueeze(2).to_broadcast([128, 16, 8])),
                      reads=[b_scT], writes=[b_scTs])
            for j in range(16):
                for k in range(8):
                    tk.op("pe", lambda g, j=j, k=k: g.matmul(psC[:, j * 17:(j + 1) * 17], lhsT=adaw[:, k, j * 128:(j + 1) * 128],
                                                              rhs=scT[:, k, :], start=(k == 0), stop=(k == 7)),
                          reads=[b_adaw, b_scT], writes=[b_psC])
            tk.op("dve", lambda g: g.tensor_tensor(out=modT[:, 0:16, :], in0=psC[:, 0:272].rearrange("p (j c) -> p j c", c=17),
                                                   in1=vecT[:, V_ADAB:V_ADAB + 16].unsqueeze(2).to_broadcast([128, 16, 17]), op=ALU.add),
                  reads=[b_psC, b_vecT], writes=[b_modT])
            tk.op("dve", lambda g: g.scalar_tensor_tensor(out=Ap[:], in0=modT[:, 8:16, 0], scalar=1.0, in1=vecT[:, V_NW:V_NW + 8],
                                                          op0=ALU.add, op1=ALU.mult), reads=[b_modT, b_vecT], writes=[b_mod])
            tk.op("dve", lambda g: g.tensor_copy(out=Bp[:], in_=modT[:, 0:8, 0]), reads=[b_modT], writes=[b_mod])
            tk.op("dve", lambda g: g.scalar_tensor_tensor(out=As[:], in0=modT[:, 8:16, 1:17], scalar=1.0,
                                                          in1=vecT[:, V_NW:V_NW + 8].unsqueeze(2).to_broadcast([128, 8, 16]),
                                                          op0=ALU.add, op1=ALU.mult), reads=[b_modT, b_vecT], writes=[b_mod])
            tk.op("dve", lambda g: g.tensor_copy(out=Bs[:], in_=modT[:, 0:8, 1:17]), reads=[b_modT], writes=[b_mod])
            for (sct, b_sct, Gt) in ((scTp, b_scTp, Gp), (scTs, b_scTs, Gs)):
                for n in range(2):
                    ps_, b_ps_ = (psA, b_psA) if n == 0 else (psB, b_psB)
                    for k in range(8):
                        tk.op("pe", lambda g, k=k, n=n, ps_=ps_, sct=sct: g.matmul(
                            ps_[:], lhsT=sct[:, k, :], rhs=adaw[:, k, 2 * D + n * 512:2 * D + (n + 1) * 512],
                            start=(k == 0), stop=(k == 7)), reads=[b_adaw, b_sct], writes=[b_ps_])
                    tk.op("dve", lambda g, n=n, ps_=ps_, Gt=Gt: g.tensor_tensor(
                        out=Gt[:, n * 512:(n + 1) * 512], in0=ps_[:], in1=adab_bc[:, n * 512:(n + 1) * 512], op=ALU.add),
                        reads=[b_ps_, b_adab], writes=[b_G])
                tk.op("dve", lambda g, Gt=Gt: g.tensor_scalar(out=Gt[:], in0=Gt[:], scalar1=0.25, scalar2=None, op0=ALU.mult),
                      reads=[b_G], writes=[b_G])
            z, w, w2, acc, t_, c_ = sm
            lam = vecT[:, V_LAM:V_LAM + 4]
            tk.op("act", lambda g: g.activation(out=z[:], in_=lam, func=AF.Exp, scale=-1.0), reads=[b_vecT], writes=[b_sm[0]])
            tk.op("dve", lambda g: g.tensor_scalar(out=w[:], in0=z[:], scalar1=2.0, scalar2=None, op0=ALU.add), reads=[b_sm[0]], writes=[b_sm[1]])
            tk.op("dve", lambda g: g.reciprocal(out=w[:], in_=w[:]), reads=[b_sm[1]], writes=[b_sm[1]])
            tk.op("dve", lambda g: g.tensor_tensor(out=w[:], in0=w[:], in1=z[:], op=ALU.mult), reads=[b_sm[1], b_sm[0]], writes=[b_sm[1]])
            tk.op("dve", lambda g: g.tensor_tensor(out=w2[:], in0=w[:], in1=w[:], op=ALU.mult), reads=[b_sm[1]], writes=[b_sm[2]])
            tk.op("dve", lambda g: g.memset(acc[:], 1.0 / 13.0), writes=[b_sm[3]])
            for coef in (1.0 / 11, 1.0 / 9, 1.0 / 7, 1.0 / 5, 1.0 / 3, 1.0):
                tk.op("dve", lambda g: g.tensor_tensor(out=acc[:], in0=acc[:], in1=w2[:], op=ALU.mult), reads=[b_sm[3], b_sm[2]], writes=[b_sm[3]])
                tk.op("dve", lambda g, coef=coef: g.tensor_scalar(out=acc[:], in0=acc[:], scalar1=float(coef), scalar2=None, op0=ALU.add),
                      reads=[b_sm[3]], writes=[b_sm[3]])
            tk.op("dve", lambda g: g.tensor_tensor(out=acc[:], in0=acc[:], in1=w[:], op=ALU.mult), reads=[b_sm[3], b_sm[1]], writes=[b_sm[3]])
            tk.op("dve", lambda g: g.tensor_scalar(out=lcoef[:, 0:4], in0=acc[:], scalar1=-8.0, scalar2=None, op0=ALU.mult), reads=[b_sm[3]], writes=[b_lcoef])
            tk.op("dve", lambda g: g.tensor_scalar(out=lcoef[:, 4:8], in0=acc[:], scalar1=-16.0, scalar2=None, op0=ALU.mult), reads=[b_sm[3]], writes=[b_lcoef])
            tk.op("dve", lambda g: g.tensor_scalar(out=lcoef[:, 8:12], in0=vecT[:, V_BA:V_BA + 4], scalar1=0.5, scalar2=None, op0=ALU.mult), reads=[b_vecT], writes=[b_lcoef])
            tk.op("dve", lambda g: g.tensor_scalar(out=lcoef[:, 12:16], in0=vecT[:, V_BX:V_BX + 4], scalar1=0.5, scalar2=None, op0=ALU.mult), reads=[b_vecT], writes=[b_lcoef])
            tk.barrier()
        if stop == "0":
            raise _Stop()

        def load_x(es_ring, src_ap, NSB):
            xt, b_xt = es_ring.get()
            tk.dma("sp", xt[:, 0:NSB, :], src_ap.rearrange("(s p) d -> p s d", p=128), writes=[b_xt])
            return xt, b_xt

        def norm_hT(xt, b_xt, NSB, xn, b_xn, hT, b_hT, tp_ring, st, b_st, sample):
            N = NSB * 128
            tk.op("dve", lambda g: g.memset(st[:, 0:4], 0.0), writes=[b_st])
            for s in range(NSB):
                tk.op("act", lambda g, s=s: g.activation(out=xn[:, s, :], in_=xt[:, s, :], func=AF.Square, accum_out=st[:, s:s + 1]),
                      reads=[b_xt], writes=[b_xn, b_st])
            tk.op("dve", lambda g: g.tensor_scalar(out=st[:, 4:4 + NSB], in0=st[:, 0:NSB], scalar1=1.0 / D, scalar2=EPS, op0=ALU.mult, op1=ALU.add),
                  reads=[b_st], writes=[b_st])
            tk.op("act", lambda g: g.activation(out=st[:, 4:4 + NSB], in_=st[:, 4:4 + NSB], func=AF.Sqrt), reads=[b_st], writes=[b_st])
            tk.op("dve", lambda g: g.reciprocal(out=st[:, 4:4 + NSB], in_=st[:, 4:4 + NSB]), reads=[b_st], writes=[b_st])
            for s in range(NSB):
                e = "act" if s % 2 == 0 else "pool"
                if e == "act":
                    tk.op("act", lambda g, s=s: g.activation(out=xn[:, s, :], in_=xt[:, s, :], func=AF.Copy, scale=st[:, 4 + s:5 + s]),
                          reads=[b_xt, b_st], writes=[b_xn])
                else:
                    tk.op("dve", lambda g, s=s: g.tensor_scalar(out=xn[:, s, :], in0=xt[:, s, :], scalar1=st[:, 4 + s:5 + s], scalar2=None, op0=ALU.mult),
                          reads=[b_xt, b_st], writes=[b_xn])
            for k in range(8):
                tp, b_tp = tp_ring.get()
                for s in range(NSB):
                    tk.op("pe", lambda g, s=s, k=k, tp=tp: g.transpose(out=tp[:, s * 128:(s + 1) * 128], in_=xn[:, s, k * 128:(k + 1) * 128], identity=ident[:]),
                          reads=[b_xn, b_ident], writes=[b_tp])
                if not sample:
                    if k % 2 == 0:
                        tk.op("act", lambda g, k=k, tp=tp: g.activation(out=hT[:, k, 0:N], in_=tp[:, 0:N], func=AF.Identity,
                                                                         scale=Ap[:, k:k + 1], bias=Bp[:, k:k + 1]),
                              reads=[b_tp, b_mod], writes=[b_hT])
                    else:
                        tk.op("dve", lambda g, k=k, tp=tp: g.tensor_scalar(out=hT[:, k, 0:N], in0=tp[:, 0:N], scalar1=Ap[:, k:k + 1],
                                                                            scalar2=Bp[:, k:k + 1], op0=ALU.mult, op1=ALU.add),
                              reads=[b_tp, b_mod], writes=[b_hT])
                else:
                    tk.op("dve", lambda g, k=k, tp=tp: g.tensor_tensor(out=hT[:, k, 0:128].rearrange("p (b t) -> p b t", t=8),
                                                                        in0=tp[:, 0:128].rearrange("p (b t) -> p b t", t=8),
                                                                        in1=As[:, k, :].unsqueeze(2).to_broadcast([128, 16, 8]), op=ALU.mult),
                          reads=[b_tp, b_mod], writes=[b_hT])
                    tk.op("dve", lambda g, k=k: g.tensor_tensor(out=hT[:, k, 0:128].rearrange("p (b t) -> p b t", t=8),
                                                                  in0=hT[:, k, 0:128].rearrange("p (b t) -> p b t", t=8),
                                                                  in1=Bs[:, k, :].unsqueeze(2).to_broadcast([128, 16, 8]), op=ALU.add),
                          reads=[b_hT, b_mod], writes=[b_hT])

        def rms_free(pm, b_pm, width, s, st, b_st, col, junk, b_junk):
            tk.op("dve", lambda g: g.memset(st[:, col:col + 1], 0.0), writes=[b_st])
            tk.op("act", lambda g: g.activation(out=junk[:, 0:width], in_=pm[:, 0:width], func=AF.Square, accum_out=st[:, col:col + 1]),
                  reads=[b_pm], writes=[b_junk, b_st])
            tk.op("dve", lambda g: g.tensor_scalar(out=st[:, col:col + 1], in0=st[:, col:col + 1], scalar1=1.0 / width, scalar2=EPS,
                                                   op0=ALU.mult, op1=ALU.add), reads=[b_st], writes=[b_st])
            tk.op("act", lambda g: g.activation(out=st[:, col:col + 1], in_=st[:, col:col + 1], func=AF.Sqrt), reads=[b_st], writes=[b_st])
            tk.op("dve", lambda g: g.reciprocal(out=st[:, col:col + 1], in_=st[:, col:col + 1]), reads=[b_st], writes=[b_st])

        def rope_tm(e, src4, cosb, sinb, out_lo, out_hi, tc, ts, reads, b_tc, writes):
            tk.op(e, lambda g: g.tensor_tensor(out=tc, in0=src4, in1=cosb, op=ALU.mult), reads=reads, writes=[b_tc])
            tk.op(e, lambda g: g.tensor_tensor(out=ts, in0=src4, in1=sinb, op=ALU.mult), reads=reads, writes=[b_tc])
            tk.op(e, lambda g: g.tensor_tensor(out=out_lo, in0=tc[:, :, 0, :], in1=ts[:, :, 1, :], op=ALU.subtract), reads=[b_tc], writes=writes)
            tk.op(e, lambda g: g.tensor_tensor(out=out_hi, in0=tc[:, :, 1, :], in1=ts[:, :, 0, :], op=ALU.add), reads=[b_tc], writes=writes)

        def load_w(es, name, src_ap, shape, eng="pool", b=None):
            t = sbt(es, name, shape, BF16)
            if b is None:
                b = Buf(name)
            K = shape[1]
            for k in range(K):
                tk.dma(eng, t[:, k, :], src_ap[k * 128:(k + 1) * 128, :], writes=[b])
            return t, b

        def kside_tile(NSB, hT, b_hT, Wkv, b_Wkv, mm_ring, tp_ring, cos_t, sin_t, b_cs_t, ckv32, b_ckv32, kpe32, b_kpe32,
                       ckv16, b_ckv16, kpe16, b_kpe16, st, b_st, junk, b_junk, tc, ts, b_tc,
                       ckvT_dst, b_ckvT, kpeT_dst, b_kpeT, out_ckv, out_kpe, kpe_shift):
            for s in range(NSB):
                pm, b_pm = mm_ring.get()
                for k in range(8):
                    tk.op("pe", lambda g, s=s, k=k, pm=pm: g.matmul(pm[:, 0:288], lhsT=hT[:, k, s * 128:(s + 1) * 128], rhs=Wkv[:, k, :],
                                                                    start=(k == 0), stop=(k == 7)), reads=[b_hT, b_Wkv], writes=[b_pm])
                rms_free(pm, b_pm, 256, s, st, b_st, s, junk, b_junk)
                tk.op("dve", lambda g, s=s, pm=pm: g.scalar_tensor_tensor(out=ckv32[:, s, :], in0=pm[:, 0:256], scalar=st[:, s:s + 1], in1=kvw_bc[:],
                                                                         op0=ALU.mult, op1=ALU.mult), reads=[b_pm, b_st, b_nw], writes=[b_ckv32])
                tk.op("act", lambda g, s=s, pm=pm: g.activation(out=kpe32[:, s, :], in_=pm[:, 256:288], func=AF.Copy), reads=[b_pm], writes=[b_kpe32])
            tk.op("act", lambda g: g.activation(out=ckv16[:, 0:NSB, :], in_=ckv32[:, 0:NSB, :], func=AF.Copy), reads=[b_ckv32], writes=[b_ckv16])
            src4 = kpe32[:, 0:NSB, :].rearrange("p s (h j) -> p s h j", h=2)
            cosb = cos_t[:, 0:NSB, :].unsqueeze(2).to_broadcast([128, NSB, 2, 16])
            sinb = sin_t[:, 0:NSB, :].unsqueeze(2).to_broadcast([128, NSB, 2, 16])
            rope_tm("pool", src4, cosb, sinb, kpe32[:, 0:NSB, 0:16], kpe32[:, 0:NSB, 16:32], tc[:, 0:NSB], ts[:, 0:NSB],
                    [b_kpe32, b_cs_t], b_tc, [b_kpe32])
            kc0 = 64 if kpe_shift else 0
            tk.op("pool", lambda g: g.tensor_copy(out=kpe16[:, 0:NSB, kc0:kc0 + 32], in_=kpe32[:, 0:NSB, :]), reads=[b_kpe32], writes=[b_kpe16])
            out_bufs.append(b_ckv32); out_bufs.append(b_kpe32)
            tk.dma("sp", out_ckv.rearrange("(s p) d -> p s d", p=128), ckv32[:, 0:NSB, :], reads=[b_ckv32])
            tk.dma("sp", out_kpe.rearrange("(s p) d -> p s d", p=128), kpe32[:, 0:NSB, :], reads=[b_kpe32])
            for c in range(2):
                tp, b_tp = tp_ring.get()
                for s in range(NSB):
                    tk.op("pe", lambda g, s=s, c=c, tp=tp: g.transpose(out=tp[:, s * 128:(s + 1) * 128], in_=ckv16[:, s, c * 128:(c + 1) * 128], identity=ident[:]),
                          reads=[b_ckv16, b_ident], writes=[b_tp])
                copy_op(alt("act", "dve"), ckvT_dst(c), tp[:, 0:NSB * 128], [b_tp], [b_ckvT])
            tp, b_tp = tp_ring.get()
            kw_ = kc0 + 32
            for s in range(NSB):
                tk.op("pe", lambda g, s=s, tp=tp: g.transpose(out=tp[0:kw_, s * 128:(s + 1) * 128], in_=kpe16[:, s, 0:kw_], identity=ident[:]),
                      reads=[b_kpe16, b_ident], writes=[b_tp])
            copy_op("dve", kpeT_dst, tp[kc0:kc0 + 32, 0:NSB * 128], [b_tp], [b_kpeT])

        def lru_tile(N, hT, b_hT, Wu, b_Wu, Wa, Wx, b_Wg, mm_ring, tmp_ring, uext, b_uext, hist_fn, uc32, b_uc32, uc16, b_uc16,
                     hstate, b_hstate, h16, b_h16, fix_fn=None):
            hist_fn()
            for c in range(4):
                pm, b_pm = mm_ring.get()
                for k in range(8):
                    tk.op("pe", lambda g, c=c, k=k, pm=pm: g.matmul(pm[:, 0:N], lhsT=Wu[:, k, c * 128:(c + 1) * 128], rhs=hT[:, k, 0:N],
                                                                    start=(k == 0), stop=(k == 7)), reads=[b_hT, b_Wu], writes=[b_pm])
                tk.op("act", lambda g, c=c, pm=pm: g.activation(out=uext(c, 3, N), in_=pm[:, 0:N], func=AF.Copy), reads=[b_pm], writes=[b_uext])
            for c in range(4):
                e = "dve"
                tk.op(e, lambda g, c=c: g.tensor_scalar(out=uc32[:, c, 0:N], in0=uext(c, 0, N), scalar1=vecT[:, V_CW + c:V_CW + c + 1],
                                                        scalar2=vecT[:, V_CB + c:V_CB + c + 1], op0=ALU.mult, op1=ALU.add),
                      reads=[b_uext, b_vecT], writes=[b_uc32])
                for kk in range(1, 4):
                    tk.op(e, lambda g, c=c, kk=kk: g.scalar_tensor_tensor(out=uc32[:, c, 0:N], in0=uext(c, kk, N),
                                                                          scalar=vecT[:, V_CW + 4 * kk + c:V_CW + 4 * kk + c + 1],
                                                                          in1=uc32[:, c, 0:N], op0=ALU.mult, op1=ALU.add),
                          reads=[b_uext, b_vecT, b_uc32], writes=[b_uc32])
            tk.op("act", lambda g: g.activation(out=uc16[:, :, 0:N], in_=uc32[:, :, 0:N], func=AF.Copy), reads=[b_uc32], writes=[b_uc16])
            for c in range(4):
                pr, b_pr = mm_ring.get()
                tk.op("pe", lambda g, c=c, pr=pr: g.matmul(pr[:, 0:N], lhsT=Wa[:, c, :], rhs=uc16[:, c, 0:N], start=True, stop=True),
                      reads=[b_Wg, b_uc16], writes=[b_pr])
                pi, b_pi = mm_ring.get()
                tk.op("pe", lambda g, c=c, pi=pi: g.matmul(pi[:, 0:N], lhsT=Wx[:, c, :], rhs=uc16[:, c, 0:N], start=True, stop=True),
                      reads=[b_Wg, b_uc16], writes=[b_pi])
                thr, b_thr = tmp_ring.get()
                tk.op("act", lambda g, c=c, pr=pr, thr=thr: g.activation(out=thr[:, 0:N], in_=pr[:, 0:N], func=AF.Tanh, scale=0.5, bias=lcoef[:, 8 + c:9 + c]),
                      reads=[b_pr, b_lcoef], writes=[b_thr])
                thi, b_thi = tmp_ring.get()
                tk.op("act", lambda g, c=c, pi=pi, thi=thi: g.activation(out=thi[:, 0:N], in_=pi[:, 0:N], func=AF.Tanh, scale=0.5, bias=lcoef[:, 12 + c:13 + c]),
                      reads=[b_pi, b_lcoef], writes=[b_thi])
                a_, b_a = tmp_ring.get()
                tk.op("act", lambda g, c=c, thr=thr, a_=a_: g.activation(out=a_[:, 0:N], in_=thr[:, 0:N], func=AF.Exp, scale=lcoef[:, c:c + 1], bias=lcoef[:, c:c + 1]),
                      reads=[b_thr, b_lcoef], writes=[b_a])
                a2, b_a2 = tmp_ring.get()
                tk.op("act", lambda g, c=c, thr=thr, a2=a2: g.activation(out=a2[:, 0:N], in_=thr[:, 0:N], func=AF.Exp, scale=lcoef[:, 4 + c:5 + c], bias=lcoef[:, 4 + c:5 + c]),
                      reads=[b_thr, b_lcoef], writes=[b_a2])
                e = "dve"
                tk.op("dve", lambda g, a2=a2: g.tensor_scalar(out=a2[:, 0:N], in0=a2[:, 0:N], scalar1=1.0, scalar2=-1.0, op0=ALU.min, op1=ALU.mult),
                      reads=[b_a2], writes=[b_a2])
                tk.op("act", lambda g, a2=a2: g.activation(out=a2[:, 0:N], in_=a2[:, 0:N], func=AF.Sqrt, bias=1.0), reads=[b_a2], writes=[b_a2])
                tk.op(e, lambda g, c=c, thi=thi: g.scalar_tensor_tensor(out=thi[:, 0:N], in0=thi[:, 0:N], scalar=1.0, in1=uc32[:, c, 0:N], op0=ALU.add, op1=ALU.mult),
                      reads=[b_thi, b_uc32], writes=[b_thi])
                tk.op(e, lambda g, thi=thi, a2=a2: g.scalar_tensor_tensor(out=thi[:, 0:N], in0=thi[:, 0:N], scalar=0.5, in1=a2[:, 0:N], op0=ALU.mult, op1=ALU.mult),
                      reads=[b_thi, b_a2], writes=[b_thi])
                if fix_fn is not None:
                    fix_fn(c, a_, b_a, thi, b_thi)
                    init = 0.0
                    rd = [b_a, b_thi]
                else:
                    init = hstate[:, c:c + 1]
                    rd = [b_a, b_thi, b_hstate]
                h32, b_h32 = tmp_ring.get()
                tk.op("dve", lambda g, a_=a_, thi=thi, h32=h32, init=init: g.tensor_tensor_scan(out=h32[:, 0:N], data0=a_[:, 0:N], data1=thi[:, 0:N], initial=init,
                                                                                             op0=ALU.mult, op1=ALU.add), reads=rd, writes=[b_h32])
                tk.op("pool", lambda g, c=c, h32=h32: g.tensor_copy(out=hstate[:, c:c + 1], in_=h32[:, N - 1:N]), reads=[b_h32], writes=[b_hstate])
                tk.op("act", lambda g, c=c, h32=h32: g.activation(out=h16[:, c, 0:N], in_=h32[:, 0:N], func=AF.Copy), reads=[b_h32], writes=[b_h16])
                yield c, h32, b_h32

        def q_tile(NSB, hT, b_hT, Wq, b_Wq, Wuq, b_Wuq, mm_ring, tp_ring, cosq_t, sinq_t, b_csq, st, b_st, junk, b_junk,
                   qn16, b_qn16, qlT, b_qlT, q16, b_q16, tcq, tsq, b_tcq, qpe32, b_qpe32, csx, b_csx):
            for s in range(NSB):
                pm, b_pm = mm_ring.get()
                for k in range(8):
                    tk.op("pe", lambda g, s=s, k=k, pm=pm: g.matmul(pm[:, 0:256], lhsT=hT[:, k, s * 128:(s + 1) * 128], rhs=Wq[:, k, :],
                                                                    start=(k == 0), stop=(k == 7)), reads=[b_hT, b_Wq], writes=[b_pm])
                rms_free(pm, b_pm, 256, s, st, b_st, s, junk, b_junk)
                tk.op("dve", lambda g, s=s, pm=pm: g.scalar_tensor_tensor(out=qn16[:, s, :], in0=pm[:, 0:256], scalar=st[:, s:s + 1], in1=qnw_bc[:],
                                                                         op0=ALU.mult, op1=ALU.mult), reads=[b_pm, b_st, b_nw], writes=[b_qn16])
            for c in range(2):
                tp, b_tp = tp_ring.get()
                for s in range(NSB):
                    tk.op("pe", lambda g, s=s, c=c, tp=tp: g.transpose(out=tp[:, s * 128:(s + 1) * 128], in_=qn16[:, s, c * 128:(c + 1) * 128], identity=ident[:]),
                          reads=[b_qn16, b_ident], writes=[b_tp])
                copy_op(alt("act", "dve"), qlT[:, c, 0:NSB * 128], tp[:, 0:NSB * 128], [b_tp], [b_qlT])
            tk.op("pool", lambda g: g.tensor_copy(out=csx[:, 0, 0:NSB], in_=cosq_t[:, 0:NSB, :].unsqueeze(2).to_broadcast([128, NSB, 8, 16])),
                  reads=[b_csq], writes=[b_csx])
            tk.op("pool", lambda g: g.tensor_copy(out=csx[:, 1, 0:NSB], in_=sinq_t[:, 0:NSB, :].unsqueeze(2).to_broadcast([128, NSB, 8, 16])),
                  reads=[b_csq], writes=[b_csx])
            for s in range(NSB):
                for half in range(2):
                    pm, b_pm = mm_ring.get()
                    for c in range(2):
                        tk.op("pe", lambda g, s=s, c=c, half=half, pm=pm: g.matmul(pm[:, 0:384], lhsT=qlT[:, c, s * 128:(s + 1) * 128],
                                                                                     rhs=Wuq[:, c, half * 384:(half + 1) * 384], start=(c == 0), stop=(c == 1)),
                              reads=[b_qlT, b_Wuq], writes=[b_pm])
                    pv = pm[:, 0:384].rearrange("p (h e) -> p h e", h=4)
                    tk.op("act", lambda g, s=s, half=half, pv=pv: g.activation(out=q16[:, s, half * 4:(half + 1) * 4, 0:64], in_=pv[:, :, 0:64], func=AF.Copy, scale=ATTN_SCALE),
                          reads=[b_pm], writes=[b_q16])
                    tk.op("act", lambda g, s=s, half=half, pv=pv: g.activation(out=qpe32[:, s, half * 4:(half + 1) * 4, :], in_=pv[:, :, 64:96], func=AF.Copy),
                          reads=[b_pm], writes=[b_qpe32])
            G = NSB * 8
            src4 = qpe32[:, 0:NSB].rearrange("p s h (a j) -> p (s h) a j", a=2)
            cosb = csx[:, 0, 0:NSB].rearrange("p s h j -> p (s h) j").unsqueeze(2).to_broadcast([128, G, 2, 16])
            sinb = csx[:, 1, 0:NSB].rearrange("p s h j -> p (s h) j").unsqueeze(2).to_broadcast([128, G, 2, 16])
            q16v = q16[:, 0:NSB].rearrange("p s h e -> p (s h) e")
            rope_tm("pool", src4, cosb, sinb, q16v[:, :, 64:80], q16v[:, :, 80:96], tcq[:, 0:G], tsq[:, 0:G], [b_qpe32, b_csx], b_tcq, [b_q16])

        def out_tile(NSB, xt, b_xt, hT, b_hT, Wg, b_Wg, Woa, Wob, Wout, b_Wo, mm_ring, tmp_ring, AO_v, b_AO, lru_v, b_lru,
                     ga16, b_ga, gb16, b_gb, thm, b_thm, mg16, b_mg, Gt, st, b_st, junk, b_junk, out_ap):
            N = NSB * 128
            for j in range(24):
                pm, b_pm = mm_ring.get()
                for k in range(8):
                    tk.op("pe", lambda g, j=j, k=k, pm=pm: g.matmul(pm[:, 0:N], lhsT=Wg[:, k, j * 128:(j + 1) * 128], rhs=hT[:, k, 0:N],
                                                                    start=(k == 0), stop=(k == 7)), reads=[b_hT, b_Wg], writes=[b_pm])
                if j < 8:
                    th, b_th = tmp_ring.get()
                    tk.op("act", lambda g, pm=pm, th=th: g.activation(out=th[:, 0:N], in_=pm[:, 0:N], func=AF.Tanh, scale=0.5), reads=[b_pm], writes=[b_th])
                    tk.op("dve", lambda g, pm=pm, th=th: g.scalar_tensor_tensor(out=th[:, 0:N], in0=th[:, 0:N], scalar=1.0, in1=pm[:, 0:N], op0=ALU.add, op1=ALU.mult),
                          reads=[b_th, b_pm], writes=[b_th])
                    if j < 4:
                        tk.op("dve", lambda g, j=j, th=th: g.tensor_tensor(out=ga16[:, j, 0:N], in0=th[:, 0:N], in1=AO_v(j), op=ALU.mult),
                              reads=[b_th, b_AO], writes=[b_ga])
                    else:
                        tk.op("dve", lambda g, j=j, th=th: g.tensor_tensor(out=gb16[:, j - 4, 0:N], in0=th[:, 0:N], in1=lru_v(j - 4), op=ALU.mult),
                              reads=[b_th, b_lru], writes=[b_gb])
                else:
                    tk.op("act", lambda g, j=j, pm=pm: g.activation(out=thm[:, j - 8, 0:N], in_=pm[:, 0:N], func=AF.Tanh, scale=0.5), reads=[b_pm], writes=[b_thm])
            for j in range(8):
                pa, b_pa = mm_ring.get()
                for kc in range(4):
                    tk.op("pe", lambda g, j=j, kc=kc, pa=pa: g.matmul(pa[:, 0:N], lhsT=Woa[:, kc, j * 128:(j + 1) * 128], rhs=ga16[:, kc, 0:N],
                                                                      start=(kc == 0), stop=(kc == 3)), reads=[b_Wo, b_ga], writes=[b_pa])
                pb, b_pb = mm_ring.get()
                for kc in range(4):
                    tk.op("pe", lambda g, j=j, kc=kc, pb=pb: g.matmul(pb[:, 0:N], lhsT=Wob[:, kc, j * 128:(j + 1) * 128], rhs=gb16[:, kc, 0:N],
                                                                      start=(kc == 0), stop=(kc == 3)), reads=[b_Wo, b_gb], writes=[b_pb])
                t1, b_t1 = tmp_ring.get()
                tk.op("dve", lambda g, j=j, pa=pa, t1=t1: g.scalar_tensor_tensor(out=t1[:, 0:N], in0=thm[:, j, 0:N], scalar=1.0, in1=pa[:, 0:N], op0=ALU.add, op1=ALU.mult),
                      reads=[b_thm, b_pa], writes=[b_t1])
                t2, b_t2 = tmp_ring.get()
                tk.op("dve", lambda g, j=j, pb=pb, t2=t2: g.scalar_tensor_tensor(out=t2[:, 0:N], in0=thm[:, 8 + j, 0:N], scalar=1.0, in1=pb[:, 0:N], op0=ALU.add, op1=ALU.mult),
                      reads=[b_thm, b_pb], writes=[b_t2])
                tk.op("dve", lambda g, j=j, t1=t1, t2=t2: g.tensor_tensor(out=mg16[:, j, 0:N], in0=t1[:, 0:N], in1=t2[:, 0:N], op=ALU.add),
                      reads=[b_t1, b_t2], writes=[b_mg])
            tk.op("dve", lambda g: g.memset(st[:, 0:4], 0.0), writes=[b_st])
            for s in range(NSB):
                for n in range(2):
                    po, b_po = mm_ring.get()
                    for k in range(8):
                        tk.op("pe", lambda g, s=s, n=n, k=k, po=po: g.matmul(po[:], lhsT=mg16[:, k, s * 128:(s + 1) * 128], rhs=Wout[:, k, n * 512:(n + 1) * 512],
                                                                             start=(k == 0), stop=(k == 7)), reads=[b_Wo, b_mg], writes=[b_po])
                    t1, b_t1 = tmp_ring.get()
                    tk.op("dve", lambda g, n=n, po=po, t1=t1: g.tensor_tensor(out=t1[:], in0=po[:], in1=Gt[:, n * 512:(n + 1) * 512], op=ALU.mult),
                          reads=[b_po, b_G], writes=[b_t1])
                    tk.op("dve", lambda g, s=s, n=n, t1=t1: g.tensor_tensor(out=xt[:, s, n * 512:(n + 1) * 512], in0=xt[:, s, n * 512:(n + 1) * 512], in1=t1[:], op=ALU.add),
                          reads=[b_t1, b_xt], writes=[b_xt])
                tk.op("act", lambda g, s=s: g.activation(out=junk[:, 0:D], in_=xt[:, s, :], func=AF.Square, accum_out=st[:, s:s + 1]),
                      reads=[b_xt], writes=[b_junk, b_st])
                tk.op("dve", lambda g, s=s: g.tensor_scalar(out=st[:, 4 + s:5 + s], in0=st[:, s:s + 1], scalar1=1.0 / D, scalar2=EPS, op0=ALU.mult, op1=ALU.add),
                      reads=[b_st], writes=[b_st])
                tk.op("act", lambda g, s=s: g.activation(out=st[:, 4 + s:5 + s], in_=st[:, 4 + s:5 + s], func=AF.Sqrt), reads=[b_st], writes=[b_st])
                tk.op("dve", lambda g, s=s: g.reciprocal(out=st[:, 4 + s:5 + s], in_=st[:, 4 + s:5 + s]), reads=[b_st], writes=[b_st])
                tk.op("dve", lambda g, s=s: g.scalar_tensor_tensor(out=xt[:, s, :], in0=xt[:, s, :], scalar=st[:, 4 + s:5 + s], in1=fnw_bc[:], op0=ALU.mult, op1=ALU.mult),
                      reads=[b_xt, b_st, b_fnw], writes=[b_xt])
            out_bufs.append(b_xt)
            tk.dma("sp", out_ap.rearrange("(s p) d -> p s d", p=128), xt[:, 0:NSB, :], reads=[b_xt])

        def load_gate_w(es):
            Wa = sbt(es, "Wa", [128, 4, 128], BF16); Wx = sbt(es, "Wx", [128, 4, 128], BF16); b_Wg = Buf("Wgates")
            tk.op("pool", lambda g: g.memset(Wa[:], 0.0), writes=[b_Wg])
            tk.op("pool", lambda g: g.memset(Wx[:], 0.0), writes=[b_Wg])
            for n in range(8):
                c, j = n // 2, n % 2
                tk.dma("pool", Wa[64 * j:64 * j + 64, c, 64 * j:64 * j + 64], lru_wa[n], writes=[b_Wg])
                tk.dma("pool", Wx[64 * j:64 * j + 64, c, 64 * j:64 * j + 64], lru_wx[n], writes=[b_Wg])
            return Wa, Wx, b_Wg

        with contextlib.ExitStack() as esP:
            ckvT = sbt(esP, "ckvT", [128, 2, SEQ], BF16); b_ckvT = Buf("ckvT")
            KT = sbt(esP, "KT", [96, SEQ], BF16); b_KTpe = Buf("KTpe"); b_KTn = Buf("KTn")

            with contextlib.ExitStack() as es:
                xring = ring_sb(es, "xt", 2, [128, 4, D], F32)
                xn = sbt(es, "xn", [128, 4, D], BF16); b_xn = Buf("xn")
                hring = ring_sb(es, "hT", 1, [128, 8, 512], BF16)
                Wkv, b_Wkv = load_w(es, "Wkv", w_in[:, C_KV:C_KV + 288], [128, 8, 288])
                Wu, b_Wu = load_w(es, "Wu", w_in[:, C_U:C_U + 512], [128, 8, 512])
                Wa, Wx, b_Wg = load_gate_w(es)
                tp_ring = ring_ps(es, "tpA", 2, [128, 512], BF16)
                mm_ring = ring_ps(es, "mmA", 6, [128, 512], F32)
                tmp_ring = ring_sb(es, "tmpA", 6, [128, 512], F32)
                cs_ring = ring_sb(es, "csA", 2, [128, 2, 4, 16], F32)
                ckv32r = ring_sb(es, "ckv32", 2, [128, 4, 256], F32)
                kpe32r = ring_sb(es, "kpe32", 2, [128, 4, 32], F32)
                ckv16 = sbt(es, "ckv16", [128, 4, 256], BF16); b_ckv16 = Buf("ckv16")
                kpe16 = sbt(es, "kpe16", [128, 4, 96], BF16); b_kpe16 = Buf("kpe16")
                tk.op("pool", lambda g: g.memset(kpe16[:], 0.0), writes=[b_kpe16])
                st = sbt(es, "stA", [128, 8], F32); b_st = Buf("stA")
                st2 = sbt(es, "stA2", [128, 8], F32); b_st2 = Buf("stA2")
                junk = sbt(es, "junkA", [128, 256], BF16); b_junk = Buf("junkA")
                tc = sbt(es, "tcA", [128, 4, 2, 16], F32); ts = sbt(es, "tsA", [128, 4, 2, 16], F32); b_tc = Buf("tcA")
                uextr = ring_sb(es, "uext", 1, [128, 4, 515], F32)
                uc32 = sbt(es, "uc32", [128, 4, 512], F32); b_uc32 = Buf("uc32")
                uc16 = sbt(es, "uc16", [128, 4, 512], BF16); b_uc16 = Buf("uc16")
                hstate = sbt(es, "hstate", [128, 4], F32); b_hstate = Buf("hstate")
                h16r = ring_sb(es, "h16", 1, [128, 4, 512], BF16)
                tk.op("dve", lambda g: g.memset(hstate[:], 0.0), writes=[b_hstate])
                prev_u = None
                nxt = load_x(xring, xf[0:512, :], 4)
                for t in range(NT):
                    xt, b_xt = nxt
                    if t + 1 < NT:
                        nxt = load_x(xring, xf[(t + 1) * 512:(t + 2) * 512, :], 4)
                    cst, b_cst = cs_ring.get()
                    tk.dma("sp", cst[:, 0], cosf[t * 512:(t + 1) * 512, :].rearrange("(s p) j -> p s j", p=128), writes=[b_cst])
                    tk.dma("sp", cst[:, 1], sinf[t * 512:(t + 1) * 512, :].rearrange("(s p) j -> p s j", p=128), writes=[b_cst])
                    hT, b_hT = hring.get()
                    norm_hT(xt, b_xt, 4, xn, b_xn, hT, b_hT, tp_ring, st, b_st, False)
                    ckv32, b_ckv32 = ckv32r.get(); kpe32, b_kpe32 = kpe32r.get()
                    kside_tile(4, hT, b_hT, Wkv, b_Wkv, mm_ring, tp_ring, cst[:, 0], cst[:, 1], b_cst, ckv32, b_ckv32, kpe32, b_kpe32,
                               ckv16, b_ckv16, kpe16, b_kpe16, st2, b_st2, junk, b_junk, tc, ts, b_tc,
                               lambda c, t=t: ckvT[:, c, t * 512:(t + 1) * 512], b_ckvT, KT[64:96, t * 512:(t + 1) * 512], b_KTpe,
                               ckvp[t * 512:(t + 1) * 512, :], kpep[t * 512:(t + 1) * 512, :], True)
                    ue, b_ue = uextr.get()

                    def hist(ue=ue, b_ue=b_ue, prev_u=prev_u):
                        if prev_u is None:
                            tk.op("pool", lambda g: g.memset(ue[:, :, 0:3], 0.0), writes=[b_ue])
                        else:
                            pu, b_pu = prev_u
                            tk.op("pool", lambda g: g.tensor_copy(out=ue[:, :, 0:3], in_=pu[:, :, 512:515]), reads=[b_pu], writes=[b_ue])
                    h16, b_h16 = h16r.get()
                    for _ in lru_tile(512, hT, b_hT, Wu, b_Wu, Wa, Wx, b_Wg, mm_ring, tmp_ring,
                                      lambda c, k0, N, ue=ue: ue[:, c, k0:k0 + N], b_ue, hist, uc32, b_uc32, uc16, b_uc16,
                                      hstate, b_hstate, h16, b_h16):
                        pass
                    tk.dma("sp", lru_scr[t].rearrange("p (c n) -> p c n", c=4), h16[:], reads=[b_h16], writes=[b_lru_scr[t]], owner=b_h16)
                    prev_u = (ue, b_ue)
                with nc.allow_non_contiguous_dma(reason="tiny state outputs"):
                    out_bufs.append(b_hstate); out_bufs.append(prev_u[1])
                    tk.dma("sp", lrup.rearrange("o (c p) -> p (o c)", p=128), hstate[:], reads=[b_hstate])
                    for c in range(4):
                        tk.dma("sp", convp[:, c * 128:(c + 1) * 128].rearrange("r p -> p r"), prev_u[0][:, c, 512:515], reads=[prev_u[1]])
                tk.barrier()
            if stop == "A":
                raise _Stop()

            with contextlib.ExitStack() as es:
                xring = ring_sb(es, "xtB", 2, [128, 4, D], F32)
                xn = sbt(es, "xnB", [128, 4, D], BF16); b_xn = Buf("xnB")
                hring = ring_sb(es, "hTB", 2, [128, 8, 512], BF16)
                Wq, b_Wq = load_w(es, "Wq", w_in[:, C_Q:C_Q + 256], [128, 8, 256])
                Wuq, b_Wuq = load_w(es, "Wuq", w_uq, [128, 2, 768])
                tp_ring = ring_ps(es, "tpB", 2, [128, 512], BF16)
                mm_ring = ring_ps(es, "mmB", 6, [128, 512], F32)
                csq_ring = ring_sb(es, "csB", 2, [128, 2, 4, 16], F32)
                st = sbt(es, "stB", [128, 8], F32); b_st = Buf("stB")
                st2 = sbt(es, "stB2", [128, 8], F32); b_st2 = Buf("stB2")
                junk = sbt(es, "junkB", [128, 256], BF16); b_junk = Buf("junkB")
                qn16 = sbt(es, "qn16", [128, 4, 256], BF16); b_qn16 = Buf("qn16")
                qlT = sbt(es, "qlT", [128, 2, 512], BF16); b_qlT = Buf("qlT")
                q16r = ring_sb(es, "q16", 2, [128, 4, 8, 96], BF16)
                tcq = sbt(es, "tcq", [128, 32, 2, 16], F32); tsq = sbt(es, "tsq", [128, 32, 2, 16], F32); b_tcq = Buf("tcq")
                qpe32 = sbt(es, "qpe32", [128, 4, 8, 32], F32); b_qpe32 = Buf("qpe32")
                csx = sbt(es, "csx", [128, 2, 4, 8, 16], F32); b_csx = Buf("csx")
                qstr = ring_sb(es, "qst", 2, [96, 8, 512], BF16)
                nxt = load_x(xring, xo[0:512, :], 4)
                for m in range(NS):
                    xt, b_xt = nxt
                    if m + 1 < NS:
                        nxt = load_x(xring, xo[(m + 1) * 512:(m + 2) * 512, :], 4)
                    cst, b_cst = csq_ring.get()
                    tk.dma("sp", cst[:, 0], coso[m * 512:(m + 1) * 512, :].rearrange("(s p) j -> p s j", p=128), writes=[b_cst])
                    tk.dma("sp", cst[:, 1], sino[m * 512:(m + 1) * 512, :].rearrange("(s p) j -> p s j", p=128), writes=[b_cst])
                    hT, b_hT = hring.get()
                    norm_hT(xt, b_xt, 4, xn, b_xn, hT, b_hT, tp_ring, st, b_st, False)
                    if stop == "B1":
                        tk.barrier()
                        raise _Stop()
                    q16, b_q16 = q16r.get()
                    q_tile(4, hT, b_hT, Wq, b_Wq, Wuq, b_Wuq, mm_ring, tp_ring, cst[:, 0], cst[:, 1], b_cst, st2, b_st2, junk, b_junk,
                           qn16, b_qn16, qlT, b_qlT, q16, b_q16, tcq, tsq, b_tcq, qpe32, b_qpe32, csx, b_csx)
                    if stop == "B2":
                        tk.barrier()
                        raise _Stop()
                    qst, b_qst = qstr.get()
                    for h in range(NH):
                        tp, b_tp = tp_ring.get()
                        for s in range(4):
                            tk.op("pe", lambda g, s=s, h=h, tp=tp: g.transpose(out=tp[0:96, s * 128:(s + 1) * 128], in_=q16[:, s, h, :], identity=ident[:]),
                                  reads=[b_q16, b_ident], writes=[b_tp])
                        copy_op(alt("act", "dve"), qst[:, h, :], tp[0:96, :], [b_tp], [b_qst])
                    if stop == "B3":
                        tk.barrier()
                        raise _Stop()
                    tk.dma("sp", qt_scr[:, :, m * 512:(m + 1) * 512].rearrange("h r n -> r h n"), qst[:], reads=[b_qst], writes=[b_qt_scr[m]], owner=b_qst)
                tk.barrier()
            if stop == "B":
                raise _Stop()

            with contextlib.ExitStack() as es:
                AO = sbt(es, "AO", [128, 4, NOWN], BF16); b_AO = Buf("AO")
                Wuk = sbt(es, "Wuk", [128, 2, 8, 96], BF16); b_Wuk = Buf("Wuk")
                tk.op("pool", lambda g: g.memset(Wuk[:], 0.0), writes=[b_Wuk])
                for c in range(2):
                    tk.dma("pool", Wuk[:, c, :, 0:64], w_uk[c * 128:(c + 1) * 128, :].rearrange("p (h e) -> p h e", h=8), writes=[b_Wuk])
                Wuv, b_Wuv = load_w(es, "Wuv", w_uv, [128, 2, 512])
                msk = sbt(es, "msk", [128, 2, 8 * 512], BF16); b_msk = Buf("msk")
                for par in range(2):
                    tk.dma("sp", msk[:, par, :], masks[par], writes=[b_msk])
                Vp = [sbt(es, "Vp%d" % i, [128, NKB, 128], BF16) for i in range(2)]; b_Vp = [Buf("Vp0"), Buf("Vp1")]
                tk.op("pool", lambda g: g.memset(Vp[0][:], 0.0), writes=[b_Vp[0]])
                tk.op("pool", lambda g: g.memset(Vp[0][:, :, 64:65], 1.0), writes=[b_Vp[0]])
                tk.op("pool", lambda g: g.memset(Vp[1][:], 0.0), writes=[b_Vp[1]])
                tk.op("pool", lambda g: g.memset(Vp[1][:, :, 32:33], 1.0), writes=[b_Vp[1]])
                QTr = ring_sb(es, "QT", 2, [96, NOWN], BF16)
                PTr = ring_sb(es, "PT", 4, [128, 512], BF16)
                den_r = ring_sb(es, "den", 2, [128, 512], F32)
                bc_r = ring_sb(es, "bc", 2, [128, 512], F32)
                s_ring = ring_ps(es, "psS", 3, [128, 512], F32)
                o_ring = ring_ps(es, "psO", 2, [128, 512], F32)
                x_ring = ring_ps(es, "psX", 3, [128, 512], F32)
                for h in range(NH):
                    par = h % 2
                    o_lo = 64 * par
                    den_row = 64 if par == 0 else 32
                    vlo = 0 if par == 0 else 64
                    QT, b_QT = QTr.get()
                    tk.dma("sp", QT[:], qt_scr[h], reads=b_qt_scr, writes=[b_QT])
                    for t in range(NT):
                        pk, b_pk = x_ring.get()
                        for c in range(2):
                            tk.op("pe", lambda g, t=t, c=c, h=h, pk=pk: g.matmul(pk[0:96, :], lhsT=Wuk[:, c, h, :], rhs=ckvT[:, c, t * 512:(t + 1) * 512],
                                                                                 start=(c == 0), stop=(c == 1)), reads=[b_Wuk, b_ckvT], writes=[b_pk])
                        copy_op(alt("act", "dve"), KT[0:64, t * 512:(t + 1) * 512], pk[0:64, :], [b_pk], [b_KTn])
                    for g8 in range(NKB // 8):
                        pv_, b_pv = x_ring.get()
                        for i in range(8):
                            kb = g8 * 8 + i
                            for c in range(2):
                                tk.op("pe", lambda g, kb=kb, i=i, c=c, h=h, pv_=pv_: g.matmul(pv_[:, i * 64:(i + 1) * 64], lhsT=ckvT[:, c, kb * 128:(kb + 1) * 128],
                                                                                              rhs=Wuv[:, c, h * 64:(h + 1) * 64], start=(c == 0), stop=(c == 1)),
                                      reads=[b_Wuv, b_ckvT], writes=[b_pv])
                        copy_op(alt("act", "dve"), Vp[par][:, g8 * 8:(g8 + 1) * 8, vlo:vlo + 64], pv_[:].rearrange("p (i e) -> p i e", i=8), [b_pv], [b_Vp[par]])
                    for m in range(NS):
                        n = 8 * m + 8
                        qs = slice(m * 512, (m + 1) * 512)
                        po, b_po = o_ring.get()
                        pend = []

                        def emit_qk(kb, h=h, qs=qs, QT=QT, b_QT=b_QT):
                            ps_, b_ps = s_ring.get()
                            tk.op("pe", lambda g: g.matmul(ps_[:], lhsT=KT[0:96, kb * 128:(kb + 1) * 128], rhs=QT[0:96, qs], start=True, stop=True),
                                  reads=[b_KTpe, b_KTn, b_QT], writes=[b_ps])
                            pt, b_pt = PTr.get()
                            tk.op("act", lambda g: g.activation(out=pt[:], in_=ps_[:], func=AF.Exp), reads=[b_ps], writes=[b_pt])
                            d = kb - (n - 8)
                            if d >= 0:
                                tk.op("pool", lambda g: g.tensor_tensor(out=pt[:], in0=pt[:], in1=msk[:, m % 2, d * 512:(d + 1) * 512], op=ALU.mult),
                                      reads=[b_pt, b_msk], writes=[b_pt])
                            return pt, b_pt

                        def emit_pv(kb, pt, b_pt, po=po, b_po=b_po, par=par):
                            tk.op("pe", lambda g: g.matmul(po[:], lhsT=Vp[par][:, kb, :], rhs=pt[:], start=(kb == 0), stop=(kb == n - 1)),
                                  reads=[b_Vp[par], b_pt], writes=[b_po])
                        for kb in range(n):
                            pend.append((kb,) + emit_qk(kb))
                            if len(pend) > 2:
                                emit_pv(*pend.pop(0))
                        while pend:
                            emit_pv(*pend.pop(0))
                        den, b_den = den_r.get()
                        tk.op("act", lambda g, den=den, po=po: g.activation(out=den[den_row:den_row + 1, :], in_=po[den_row:den_row + 1, :], func=AF.Copy),
                              reads=[b_po], writes=[b_den])
                        tk.op("dve", lambda g, den=den: g.reciprocal(out=den[den_row:den_row + 1, :], in_=den[den_row:den_row + 1, :]), reads=[b_den], writes=[b_den])
                        pb, b_pb = x_ring.get()
                        tk.op("pe", lambda g, den=den, pb=pb: g.matmul(pb[:], lhsT=ones32[den_row:den_row + 1, :], rhs=den[den_row:den_row + 1, :], start=True, stop=True),
                              reads=[b_ones, b_den], writes=[b_pb])
                        bc, b_bc = bc_r.get()
                        tk.op("act", lambda g, bc=bc, pb=pb: g.activation(out=bc[o_lo:o_lo + 64, :], in_=pb[o_lo:o_lo + 64, :], func=AF.Copy), reads=[b_pb], writes=[b_bc])
                        tk.op("dve", lambda g, bc=bc, po=po, qs=qs, h=h: g.tensor_tensor(out=AO[o_lo:o_lo + 64, h // 2, qs], in0=po[o_lo:o_lo + 64, :], in1=bc[o_lo:o_lo + 64, :], op=ALU.mult),
                              reads=[b_po, b_bc], writes=[b_AO])
                for j in range(4):
                    tk.dma("sp", ao_scr[:, j, :], AO[:, j, :], reads=[b_AO], writes=[b_ao_scr], owner=b_AO)
                tk.wait_all("sp", [b_ao_scr])
                tk.barrier()
            if stop == "C":
                raise _Stop()

        with contextlib.ExitStack() as es:
            Wg = sbt(es, "Wg", [128, 8, 3072], BF16); b_Wg = Buf("WgD")
            for k in range(8):
                tk.dma("pool", Wg[:, k, 0:512], w_in[k * 128:(k + 1) * 128, C_GA:C_GA + 512], writes=[b_Wg])
                tk.dma("pool", Wg[:, k, 512:3072], w_in[k * 128:(k + 1) * 128, C_GB:C_GB + 2560], writes=[b_Wg])
            Woa, b_Wo = load_w(es, "Woa", w_oa, [128, 4, D])
            Wob, _b = load_w(es, "Wob", w_ob, [128, 4, D], b=b_Wo)
            Wout, _b2 = load_w(es, "Wout", w_out, [128, 8, D], b=b_Wo)
            xring = ring_sb(es, "xtD", 2, [128, 2, D], F32)
            xn = sbt(es, "xnD", [128, 2, D], BF16); b_xn = Buf("xnD")
            hring = ring_sb(es, "hTD", 2, [128, 8, 256], BF16)
            tp_ring = ring_ps(es, "tpD", 2, [128, 512], BF16)
            mm_ring = ring_ps(es, "mmD", 6, [128, 512], F32)
            tmp_ring = ring_sb(es, "tmpD", 6, [128, 512], F32)
            lrur = ring_sb(es, "lruD", 2, [128, 4, 256], BF16)
            aor = ring_sb(es, "aoD", 2, [128, 4, 256], BF16)
            st = sbt(es, "stD", [128, 8], F32); b_st = Buf("stD")
            st2 = sbt(es, "stD2", [128, 8], F32); b_st2 = Buf("stD2")
            junk = sbt(es, "junkD", [128, D], BF16); b_junk = Buf("junkD")
            ga16 = sbt(es, "ga16", [128, 4, 256], BF16); b_ga = Buf("ga16")
            gb16 = sbt(es, "gb16", [128, 4, 256], BF16); b_gb = Buf("gb16")
            thm = sbt(es, "thm", [128, 16, 256], BF16); b_thm = Buf("thm")
            mg16 = sbt(es, "mg16", [128, 8, 256], BF16); b_mg = Buf("mg16")

            tk.wait_all("sp", [b_own])
            nxt = load_x(xring, xo[0:256, :], 2)
            for i in range(2 * NS):
                m, hf = i // 2, i % 2
                xt, b_xt = nxt
                if i + 1 < 2 * NS:
                    nxt = load_x(xring, xo[(i + 1) * 256:(i + 2) * 256, :], 2)
                lt, b_lt = lrur.get()
                tk.wait_all("sp", [lt_b for lt_b in [b_lt]])
                ov = dyn_off(own_sb[0:1, m:m + 1], 128 * 2048 * 2)
                if hf:
                    nc.sync.reg_alu(R["o0"], R["o0"], hf * 512, ALU.add)
                tk.dma("sp", lt[:].bitcast(U8),
                       lru_scr_b[bass.ds(ov, 128 * 4096)].rearrange("(p c n) -> p c n", c=4, n=1024)[:, :, 0:512],
                       reads=b_lru_scr, writes=[b_lt])
                at, b_at = aor.get()
                tk.dma("sp", at[:], ao_scr[:, :, i * 256:(i + 1) * 256], reads=[b_ao_scr], writes=[b_at])
                hT, b_hT = hring.get()
                norm_hT(xt, b_xt, 2, xn, b_xn, hT, b_hT, tp_ring, st, b_st, False)
                out_tile(2, xt, b_xt, hT, b_hT, Wg, b_Wg, Woa, Wob, Wout, b_Wo, mm_ring, tmp_ring,
                         lambda j, at=at: at[:, j, :], b_at, lambda c, lt=lt: lt[:, c, :], b_lt,
                         ga16, b_ga, gb16, b_gb, thm, b_thm, mg16, b_mg, Gp, st2, b_st2, junk, b_junk, yo[i * 256:(i + 1) * 256, :])
            tk.barrier()
        if stop == "D":
            raise _Stop()

        with contextlib.ExitStack() as es:
            NG = NP // 4
            KW = (128, 128, 32)
            xring = ring_sb(es, "xtS", 1, [128, 1, D], F32)
            xn = sbt(es, "xnS", [128, 1, D], BF16); b_xn = Buf("xnS")
            hT = sbt(es, "hTS", [128, 8, 128], BF16); b_hT = Buf("hTS")
            st = sbt(es, "stS", [128, 8], F32); b_st = Buf("stS")
            st2 = sbt(es, "stS2", [128, 8], F32); b_st2 = Buf("stS2")
            junk = sbt(es, "junkS", [128, D], BF16); b_junk = Buf("junkS")
            TN = sbt(es, "TN", [128, 3, 128], BF16); b_TN = Buf("TN")
            natn = sbt(es, "natn", [128, 257], BF16); b_natn = Buf("natn")
            h16 = sbt(es, "h16S", [128, 4, 128], BF16); b_h16 = Buf("h16S")
            QS = sbt(es, "QS", [128, 3, 16, 8, 8], BF16); b_QS = Buf("QS")
            AOs = sbt(es, "AOs", [128, 4, 128], BF16); b_AOs = Buf("AOs")
            QB = sbt(es, "QB", [128, 16, 4, 64], BF16); b_QB = Buf("QB")
            tmp_ring = ring_sb(es, "tmpS", 6, [128, 512], F32)
            xt, b_xt = load_x(xring, xs, 1)

            with contextlib.ExitStack() as e1:
                tpg = ring_ps(e1, "tpg", 2, [128, 1024], BF16)
                mm_ring = ring_ps(e1, "mmS", 2, [128, 512], F32)
                Wkv, b_Wkv = load_w(e1, "WkvS", w_in[:, C_KV:C_KV + 288], [128, 8, 288])
                Wu, b_Wu = load_w(e1, "WuS", w_in[:, C_U:C_U + 512], [128, 8, 512])
                Wq, b_Wq = load_w(e1, "WqS", w_in[:, C_Q:C_Q + 256], [128, 8, 256])
                Wuq, b_Wuq = load_w(e1, "WuqS", w_uq, [128, 2, 768])
                Wa, Wx, b_Wgt = load_gate_w(e1)
                Wuk16, b_Wuk16 = load_w(e1, "Wuk16", w_uk, [128, 2, 512])
                WukT = sbt(e1, "WukT", [64, 8, 256], BF16); b_WukT = Buf("WukT")
                cst = sbt(e1, "cstS", [128, 4, 1, 16], F32); b_cst = Buf("cstS")
                for i_, src_ in enumerate((coss, sins, cossq, sinsq)):
                    tk.dma("sp", cst[:, i_], src_.rearrange("(s p) j -> p s j", p=128), writes=[b_cst])
                ckv32 = sbt(e1, "ckv32S", [128, 1, 256], F32); b_ckv32 = Buf("ckv32S")
                kpe32 = sbt(e1, "kpe32S", [128, 1, 32], F32); b_kpe32 = Buf("kpe32S")
                ckv16 = sbt(e1, "ckv16S", [128, 1, 256], BF16); b_ckv16 = Buf("ckv16S")
                kpe16 = sbt(e1, "kpe16S", [128, 1, 96], BF16); b_kpe16 = Buf("kpe16S")
                tc = sbt(e1, "tcS", [128, 4, 2, 16], F32); ts = sbt(e1, "tsS", [128, 4, 2, 16], F32); b_tc = Buf("tcS")
                norm_hT(xt, b_xt, 1, xn, b_xn, hT, b_hT, tpg, st, b_st, True)
                kside_tile(1, hT, b_hT, Wkv, b_Wkv, mm_ring, tpg, cst[:, 0], cst[:, 1], b_cst, ckv32, b_ckv32, kpe32, b_kpe32,
                           ckv16, b_ckv16, kpe16, b_kpe16, st2, b_st2, junk, b_junk, tc, ts, b_tc,
                           lambda c: TN[:, c, :], b_TN, TN[0:32, 2, :], b_TN, ckvs, kpes, False)
                tk.op("pool", lambda g: g.memset(natn[:, 256:257], 1.0), writes=[b_natn])
                tk.op("pool", lambda g: g.tensor_copy(out=natn[:, 0:256], in_=ckv16[:, 0, :]), reads=[b_ckv16], writes=[b_natn])

                sc_sb = sbt(e1, "sc_sb", [48, 512], F32); b_sc = Buf("sc_sb")
                sl_sb = sbt(e1, "sl_sb", [16, 512], F32); b_sl = Buf("sl_sb")
                tk.dma("sp", sc_sb[:], sconv, writes=[b_sc])
                tk.dma("sp", sl_sb[:], slru, writes=[b_sl])
                uext = sbt(e1, "uextS", [128, 4, 16, 11], F32); b_uext = Buf("uextS")
                h0T = sbt(e1, "h0T", [128, 4, 16], F32); b_h0T = Buf("h0T")
                for c in range(4):
                    pm, b_pm = mm_ring.get()
                    tk.op("pe", lambda g, c=c, pm=pm: g.transpose(out=pm[:, 0:48], in_=sc_sb[:, c * 128:(c + 1) * 128], identity=identf[0:48, 0:48]),
                          reads=[b_sc, b_identf], writes=[b_pm])
                    tk.op("dve", lambda g, c=c, pm=pm: g.tensor_copy(out=uext[:, c, :, 0:3], in_=pm[:, 0:48].rearrange("p (b r) -> p b r", r=3)),
                          reads=[b_pm], writes=[b_uext])
                    pm2, b_pm2 = mm_ring.get()
                    tk.op("pe", lambda g, c=c, pm2=pm2: g.transpose(out=pm2[:, 0:16], in_=sl_sb[:, c * 128:(c + 1) * 128], identity=identf[0:16, 0:16]),
                          reads=[b_sl, b_identf], writes=[b_pm2])
                    tk.op("dve", lambda g, c=c, pm2=pm2: g.tensor_copy(out=h0T[:, c, :], in_=pm2[:, 0:16]), reads=[b_pm2], writes=[b_h0T])
                uc32 = sbt(e1, "uc32S", [128, 4, 128], F32); b_uc32 = Buf("uc32S")
                uc16 = sbt(e1, "uc16S", [128, 4, 128], BF16); b_uc16 = Buf("uc16S")
                hlast = sbt(e1, "hlast", [128, 4, 16], F32); b_hlast = Buf("hlast")
                N = 128
                for c in range(4):
                    pm, b_pm = mm_ring.get()
                    for k in range(8):
                        tk.op("pe", lambda g, c=c, k=k, pm=pm: g.matmul(pm[:, 0:N], lhsT=Wu[:, k, c * 128:(c + 1) * 128], rhs=hT[:, k, 0:N],
                                                                        start=(k == 0), stop=(k == 7)), reads=[b_hT, b_Wu], writes=[b_pm])
                    tk.op("act", lambda g, c=c, pm=pm: g.activation(out=uext[:, c, :, 3:11], in_=pm[:, 0:N].rearrange("p (b t) -> p b t", t=8), func=AF.Copy),
                          reads=[b_pm], writes=[b_uext])
                for c in range(4):
                    ucv = uc32[:, c, :].rearrange("p (b t) -> p b t", t=8)
                    tk.op("dve", lambda g, c=c, ucv=ucv: g.tensor_scalar(out=ucv, in0=uext[:, c, :, 0:8], scalar1=vecT[:, V_CW + c:V_CW + c + 1],
                                                                         scalar2=vecT[:, V_CB + c:V_CB + c + 1], op0=ALU.mult, op1=ALU.add),
                          reads=[b_uext, b_vecT], writes=[b_uc32])
                    for kk in range(1, 4):
                        tk.op("dve", lambda g, c=c, kk=kk, ucv=ucv: g.scalar_tensor_tensor(out=ucv, in0=uext[:, c, :, kk:kk + 8],
                                                                                           scalar=vecT[:, V_CW + 4 * kk + c:V_CW + 4 * kk + c + 1],
                                                                                           in1=ucv, op0=ALU.mult, op1=ALU.add),
                              reads=[b_uext, b_vecT, b_uc32], writes=[b_uc32])
                tk.op("act", lambda g: g.activation(out=uc16[:], in_=uc32[:], func=AF.Copy), reads=[b_uc32], writes=[b_uc16])
                for c in range(4):
                    pr, b_pr = mm_ring.get()
                    tk.op("pe", lambda g, c=c, pr=pr: g.matmul(pr[:, 0:N], lhsT=Wa[:, c, :], rhs=uc16[:, c, :], start=True, stop=True),
                          reads=[b_Wgt, b_uc16], writes=[b_pr])
                    thr, b_thr = tmp_ring.get()
                    tk.op("act", lambda g, c=c, pr=pr, thr=thr: g.activation(out=thr[:, 0:N], in_=pr[:, 0:N], func=AF.Tanh, scale=0.5, bias=lcoef[:, 8 + c:9 + c]),
                          reads=[b_pr, b_lcoef], writes=[b_thr])
                    pi, b_pi = mm_ring.get()
                    tk.op("pe", lambda g, c=c, pi=pi: g.matmul(pi[:, 0:N], lhsT=Wx[:, c, :], rhs=uc16[:, c, :], start=True, stop=True),
                          reads=[b_Wgt, b_uc16], writes=[b_pi])
                    thi, b_thi = tmp_ring.get()
                    tk.op("act", lambda g, c=c, pi=pi, thi=thi: g.activation(out=thi[:, 0:N], in_=pi[:, 0:N], func=AF.Tanh, scale=0.5, bias=lcoef[:, 12 + c:13 + c]),
                          reads=[b_pi, b_lcoef], writes=[b_thi])
                    a_, b_a = tmp_ring.get()
                    tk.op("act", lambda g, c=c, thr=thr, a_=a_: g.activation(out=a_[:, 0:N], in_=thr[:, 0:N], func=AF.Exp, scale=lcoef[:, c:c + 1], bias=lcoef[:, c:c + 1]),
                          reads=[b_thr, b_lcoef], writes=[b_a])
                    a2, b_a2 = tmp_ring.get()
                    tk.op("act", lambda g, c=c, thr=thr, a2=a2: g.activation(out=a2[:, 0:N], in_=thr[:, 0:N], func=AF.Exp, scale=lcoef[:, 4 + c:5 + c], bias=lcoef[:, 4 + c:5 + c]),
                          reads=[b_thr, b_lcoef], writes=[b_a2])
                    tk.op("dve", lambda g, a2=a2: g.tensor_scalar(out=a2[:, 0:N], in0=a2[:, 0:N], scalar1=1.0, scalar2=-1.0, op0=ALU.min, op1=ALU.mult),
                          reads=[b_a2], writes=[b_a2])
                    tk.op("act", lambda g, a2=a2: g.activation(out=a2[:, 0:N], in_=a2[:, 0:N], func=AF.Sqrt, bias=1.0), reads=[b_a2], writes=[b_a2])
                    tk.op("dve", lambda g, c=c, thi=thi: g.scalar_tensor_tensor(out=thi[:, 0:N], in0=thi[:, 0:N], scalar=1.0, in1=uc32[:, c, :], op0=ALU.add, op1=ALU.mult),
                          reads=[b_thi, b_uc32], writes=[b_thi])
                    tk.op("dve", lambda g, thi=thi, a2=a2: g.scalar_tensor_tensor(out=thi[:, 0:N], in0=thi[:, 0:N], scalar=0.5, in1=a2[:, 0:N], op0=ALU.mult, op1=ALU.mult),
                          reads=[b_thi, b_a2], writes=[b_thi])
                    av = a_[:, 0:N].rearrange("p (b t) -> p b t", t=8)
                    bv = thi[:, 0:N].rearrange("p (b t) -> p b t", t=8)
                    t0, b_t0 = tmp_ring.get()
                    tk.op("dve", lambda g, c=c, av=av, t0=t0: g.tensor_tensor(out=t0[:, 0:16], in0=av[:, :, 0], in1=h0T[:, c, :], op=ALU.mult),
                          reads=[b_a, b_h0T], writes=[b_t0])
                    tk.op("dve", lambda g, bv=bv, t0=t0: g.tensor_tensor(out=bv[:, :, 0], in0=bv[:, :, 0], in1=t0[:, 0:16], op=ALU.add),
                          reads=[b_thi, b_t0], writes=[b_thi])
                    tk.op("dve", lambda g, av=av: g.memset(av[:, :, 0], 0.0), reads=[b_t0], writes=[b_a])
                    h32, b_h32 = tmp_ring.get()
                    tk.op("dve", lambda g, a_=a_, thi=thi, h32=h32: g.tensor_tensor_scan(out=h32[:, 0:N], data0=a_[:, 0:N], data1=thi[:, 0:N], initial=0.0,
                                                                                       op0=ALU.mult, op1=ALU.add), reads=[b_a, b_thi], writes=[b_h32])
                    tk.op("act", lambda g, c=c, h32=h32: g.activation(out=h16[:, c, :], in_=h32[:, 0:N], func=AF.Copy), reads=[b_h32], writes=[b_h16])
                    tk.op("dve", lambda g, c=c, h32=h32: g.tensor_copy(out=hlast[:, c, :], in_=h32[:, 0:N].rearrange("p (b t) -> p b t", t=8)[:, :, 7]),
                          reads=[b_h32], writes=[b_hlast])
                cvo = sbt(e1, "cvo", [48, 512], F32); b_cvo = Buf("cvo")
                lro = sbt(e1, "lro", [16, 512], F32); b_lro = Buf("lro")
                cvi = sbt(e1, "cvi", [128, 4, 16, 3], F32); b_cvi = Buf("cvi")
                tk.op("dve", lambda g: g.tensor_copy(out=cvi[:], in_=uext[:, :, :, 8:11]), reads=[b_uext], writes=[b_cvi])
                for c in range(4):
                    pm, b_pm = mm_ring.get()
                    tk.op("pe", lambda g, c=c, pm=pm: g.transpose(out=pm[0:48, 0:128], in_=cvi[:, c].rearrange("p b r -> p (b r)"), identity=identf[:]),
                          reads=[b_cvi, b_identf], writes=[b_pm])
                    tk.op("dve", lambda g, c=c, pm=pm: g.tensor_copy(out=cvo[:, c * 128:(c + 1) * 128], in_=pm[0:48, 0:128]), reads=[b_pm], writes=[b_cvo])
                    pm2, b_pm2 = mm_ring.get()
                    tk.op("pe", lambda g, c=c, pm2=pm2: g.transpose(out=pm2[0:16, 0:128], in_=hlast[:, c, :], identity=identf[:]),
                          reads=[b_hlast, b_identf], writes=[b_pm2])
                    tk.op("dve", lambda g, c=c, pm2=pm2: g.tensor_copy(out=lro[:, c * 128:(c + 1) * 128], in_=pm2[0:16, 0:128]), reads=[b_pm2], writes=[b_lro])
                tk.dma("sp", convs, cvo[:], reads=[b_cvo])
                tk.dma("sp", lrus, lro[:], reads=[b_lro])

                qn16 = sbt(e1, "qn16S", [128, 1, 256], BF16); b_qn16 = Buf("qn16S")
                qlT = sbt(e1, "qlTS", [128, 2, 128], BF16); b_qlT = Buf("qlTS")
                q16 = sbt(e1, "q16S", [128, 1, 8, 96], BF16); b_q16 = Buf("q16S")
                tcq = sbt(e1, "tcqS", [128, 8, 2, 16], F32); tsq = sbt(e1, "tsqS", [128, 8, 2, 16], F32); b_tcq = Buf("tcqS")
                qpe32 = sbt(e1, "qpe32S", [128, 1, 8, 32], F32); b_qpe32 = Buf("qpe32S")
                csx = sbt(e1, "csxS", [128, 2, 1, 8, 16], F32); b_csx = Buf("csxS")
                q_tile(1, hT, b_hT, Wq, b_Wq, Wuq, b_Wuq, mm_ring, tpg, cst[:, 2], cst[:, 3], b_cst, st2, b_st2, junk, b_junk,
                       qn16, b_qn16, qlT, b_qlT, q16, b_q16, tcq, tsq, b_tcq, qpe32, b_qpe32, csx, b_csx)
                for h in range(NH):
                    tp, b_tp = tpg.get()
                    for c in range(2):
                        tk.op("pe", lambda g, h=h, c=c, tp=tp: g.transpose(out=tp[0:64, c * 128:(c + 1) * 128], in_=Wuk16[:, c, h * 64:(h + 1) * 64], identity=ident[:]),
                              reads=[b_Wuk16, b_ident], writes=[b_tp])
                    copy_op(alt("act", "dve"), WukT[:, h, :], tp[0:64, 0:256], [b_tp], [b_WukT])
                qnT = sbt(e1, "qnT", [64, 8, 128], BF16); b_qnT = Buf("qnT")
                for h in range(NH):
                    tp, b_tp = tpg.get()
                    tk.op("pe", lambda g, h=h, tp=tp: g.transpose(out=tp[0:64, 0:128], in_=q16[:, 0, h, 0:64], identity=ident[:]), reads=[b_q16, b_ident], writes=[b_tp])
                    tk.op("pe", lambda g, h=h, tp=tp: g.transpose(out=tp[0:32, 128:256], in_=q16[:, 0, h, 64:96], identity=ident[:]), reads=[b_q16, b_ident], writes=[b_tp])
                    copy_op("dve", qnT[:, h, :], tp[0:64, 0:128], [b_tp], [b_qnT])
                    copy_op("act", QS[0:32, 2, :, h, :], tp[0:32, 128:256].rearrange("p (b t) -> p b t", t=8), [b_tp], [b_QS])
                    for c in range(2):
                        pm, b_pm = mm_ring.get()
                        tk.op("pe", lambda g, h=h, c=c, pm=pm: g.matmul(pm[:, 0:128], lhsT=WukT[:, h, c * 128:(c + 1) * 128], rhs=qnT[:, h, :], start=True, stop=True),
                              reads=[b_WukT, b_qnT], writes=[b_pm])
                        copy_op(alt("act", "dve"), QS[:, c, :, h, :], pm[:, 0:128].rearrange("p (b t) -> p b t", t=8), [b_pm], [b_QS])
                qrep = sbt(e1, "qrep", [128, 8, 4, 32], BF16); b_qrep = Buf("qrep")
                tk.op("dve", lambda g: g.tensor_copy(out=qrep[:], in_=q16[:, 0, :, 64:96].unsqueeze(2).to_broadcast([128, 8, 4, 32])), reads=[b_q16], writes=[b_qrep])
                tk.op("pool", lambda g: g.memset(QB[:], 0.0), writes=[b_QB])
                for h in range(NH):
                    tp, b_tp = tpg.get()
                    tk.op("pe", lambda g, h=h, tp=tp: g.transpose(out=tp[:, 0:128], in_=qrep[:, h].rearrange("p j d -> p (j d)"), identity=ident[:]),
                          reads=[b_qrep, b_ident], writes=[b_tp])
                    for j in range(4):
                        copy_op(alt("act", "dve"), QB[32 * j:32 * j + 32, :, j, h * 8:(h + 1) * 8],
                                tp[32 * j:32 * j + 32, 0:128].rearrange("p (b t) -> p b t", t=8), [b_tp], [b_QB])
                tk.wait_all("sp", [b_cvo, b_lro, b_ckv32, b_kpe32])
                tk.barrier()
            if stop == "S1":
                raise _Stop()

            with contextlib.ExitStack() as e2:
                tpT = ring_ps(e2, "tpT", 2, [128, 2048], BF16)
                mm_ring = ring_ps(e2, "mmS2", 1, [128, 512], F32)
                sS_t = pst(e2, "sS", [128, 512], F32)
                sS = Ring([(sS_t[:, 0:256], Buf("sS0")), (sS_t[:, 256:512], Buf("sS1"))])
                oS = ring_ps(e2, "oS", 2, [128, 512], F32)
                Wuvp = sbt(e2, "Wuvp", [128, 2, 8, 128], BF16); b_Wuvp = Buf("Wuvp")
                tk.op("pool", lambda g: g.memset(Wuvp[:], 0.0), writes=[b_Wuvp])
                for c in range(2):
                    for h in range(NH):
                        par = h % 2
                        tk.dma("pool", Wuvp[:, c, h, 64 * par:64 * par + 64], w_uv[c * 128:(c + 1) * 128, h * 64:(h + 1) * 64], writes=[b_Wuvp])
                smk = sbt(e2, "smk", [128, 16 * 64], BF16); b_smk = Buf("smk")
                tk.dma("sp", smk[:], smask, writes=[b_smk])
                n32r = Ring([(sbt(e2, "n32_%d" % i_, [128, 4, 288], F32), Buf("n32a%d" % i_), Buf("n32b%d" % i_)) for i_ in range(4)])
                n16r = ring_sb(e2, "n16", 3, [128, 1156], BF16)
                for it in n16r.items:
                    tk.op("pool", lambda g, it=it: g.memset(it[0][:, 0:1028].rearrange("p (j e) -> p j e", e=257)[:, :, 256:257], 1.0), writes=[it[1]])
                T16r = ring_sb(e2, "T16", 2, [128, 9, 128], BF16)
                PSr = ring_sb(e2, "PSs", 2, [128, 256], BF16)
                OT = sbt(e2, "OT", [128, 2, 8, 16, 8], BF16); b_OT = Buf("OT")
                ol16 = sbt(e2, "ol16", [64, 256], BF16); b_ol16 = Buf("ol16")
                rden = sbt(e2, "rden", [64, 1], F32); b_rden = Buf("rden")
                tk.wait_all("sp", [b_ptab])

                def stage1(b, gq):
                    n32, b_n32a, b_n32b = n32r.get()
                    for j in range(4):
                        q, bb = ("sp", b_n32a) if j < 2 else ("act", b_n32b)
                        tk.wait_all(q, [bb])
                        v0, v1 = page_offsets(ptab_sb[b:b + 1, gq * 4 + j:gq * 4 + j + 1], q)
                        tk.dma(q, n32[:, j, :].bitcast(U8), pl0_b[bass.ds(v0, PGB)].rearrange("(p d) -> p d", d=1152), writes=[bb],
                               bounds_check="skip_entire_dma")
                        tk.dma(q, n32[:, j, :].bitcast(U8), pl1_b[bass.ds(v1, PGB)].rearrange("(p d) -> p d", d=1152), writes=[bb],
                               bounds_check="skip_entire_dma")
                    n16, b_n16 = n16r.get()
                    tk.op("dve", lambda g: g.tensor_copy(out=n16[:, 0:1028].rearrange("p (j e) -> p j e", e=257)[:, :, 0:256], in_=n32[:, :, 0:256]),
                          reads=[b_n32a, b_n32b], writes=[b_n16])
                    tk.op("pool", lambda g: g.tensor_copy(out=n16[:, 1028:1156].rearrange("p (j d) -> p j d", d=32), in_=n32[:, :, 256:288]),
                          reads=[b_n32a, b_n32b], writes=[b_n16])
                    tp, b_tp = tpT.get()
                    for j in range(4):
                        for c in range(2):
                            blk = 2 * j + c
                            tk.op("pe", lambda g, j=j, c=c, blk=blk: g.transpose(out=tp[:, blk * 128:(blk + 1) * 128], in_=n16[:, j * 257 + c * 128:j * 257 + (c + 1) * 128], identity=ident[:]),
                                  reads=[b_n16, b_ident], writes=[b_tp])
                    tk.op("pe", lambda g: g.transpose(out=tp[:, 1024:1152], in_=n16[:, 1028:1156], identity=ident[:]), reads=[b_n16, b_ident], writes=[b_tp])
                    T16, b_T16 = T16r.get()
                    copy_op("dve", T16[:, 0:8, :], tp[:, 0:1024].rearrange("p (k n) -> p k n", k=8), [b_tp], [b_T16])
                    copy_op("dve", T16[:, 8, :], tp[:, 1024:1152], [b_tp], [b_T16])
                    return n16, b_n16, T16, b_T16

                def stage2(b, st1, po, b_po, first):
                    n16, b_n16, T16, b_T16 = st1
                    ps_, b_ps = sS.get()
                    tk.op("pe", lambda g: g.matmul(ps_[:, 0:256], lhsT=T16[:, 8, :], rhs=QB[:, b].rearrange("p j q -> p (j q)"), start=True, stop=False),
                          reads=[b_T16, b_QB], writes=[b_ps])
                    for j in range(4):
                        for c in range(2):
                            tk.op("pe", lambda g, j=j, c=c: g.matmul(ps_[:, j * 64:(j + 1) * 64], lhsT=T16[:, 2 * j + c, :],
                                                                    rhs=QS[:, c, b].rearrange("p h t -> p (h t)"), start=False, stop=(c == 1 and j == 3)),
                                  reads=[b_T16, b_QS], writes=[b_ps])
                    pts, b_pts = PSr.get()
                    tk.op("act", lambda g: g.activation(out=pts[:], in_=ps_, func=AF.Exp), reads=[b_ps], writes=[b_pts])
                    for j in range(4):
                        tk.op("pe", lambda g, j=j: g.matmul(po[0:64, 0:257], lhsT=pts[:, j * 64:(j + 1) * 64], rhs=n16[:, j * 257:(j + 1) * 257],
                                                            start=(first and j == 0), stop=False),
                              reads=[b_pts, b_n16], writes=[b_po])

                groups = [(b, gq) for b in range(16) for gq in range(NG)]
                pend = stage1(*groups[0]) if groups else None
                po = b_po = None
                for gi, (b, gq) in enumerate(groups):
                    cur = pend
                    pend = stage1(*groups[gi + 1]) if gi + 1 < len(groups) else None
                    if gq == 0:
                        po, b_po = oS.get()
                    stage2(b, cur, po, b_po, gq == 0)
                    if gq != NG - 1:
                        continue
                    ps_, b_ps = sS.get()
                    for c3 in range(3):
                        cw = KW[c3]
                        tk.op("pe", lambda g, c3=c3, cw=cw, ps_=ps_, b=b: g.matmul(ps_[:, 0:64], lhsT=TN[0:cw, c3, :], rhs=QS[0:cw, c3, b].rearrange("p h t -> p (h t)"),
                                                                                  start=(c3 == 0), stop=(c3 == 2)), reads=[b_TN, b_QS], writes=[b_ps])
                    pts, b_pts = PSr.get()
                    tk.op("act", lambda g, pts=pts, ps_=ps_: g.activation(out=pts[:, 0:64], in_=ps_[:, 0:64], func=AF.Exp), reads=[b_ps], writes=[b_pts])
                    tk.op("pool", lambda g, pts=pts, b=b: g.tensor_tensor(out=pts[:, 0:64], in0=pts[:, 0:64], in1=smk[:, b * 64:(b + 1) * 64], op=ALU.mult),
                          reads=[b_pts, b_smk], writes=[b_pts])
                    tk.op("pe", lambda g, pts=pts, po=po: g.matmul(po[0:64, 0:257], lhsT=pts[:, 0:64], rhs=natn[:], start=(NG == 0), stop=True),
                          reads=[b_pts, b_natn], writes=[b_po])
                    tk.op("dve", lambda g, po=po: g.reciprocal(out=rden[:], in_=po[0:64, 256:257]), reads=[b_po], writes=[b_rden])
                    tk.op("dve", lambda g, po=po: g.tensor_scalar(out=ol16[:], in0=po[0:64, 0:256], scalar1=rden[:, 0:1], scalar2=None, op0=ALU.mult),
                          reads=[b_po, b_rden], writes=[b_ol16])
                    tp, b_tp = tpT.get()
                    for c in range(2):
                        tk.op("pe", lambda g, c=c, tp=tp: g.transpose(out=tp[:, c * 64:(c + 1) * 64], in_=ol16[:, c * 128:(c + 1) * 128], identity=ident[0:64, 0:64]),
                              reads=[b_ol16, b_ident], writes=[b_tp])
                    tk.op("dve", lambda g, tp=tp, b=b: g.tensor_copy(out=OT[:, :, :, b, :], in_=tp[:, 0:128].rearrange("p (c h t) -> p c h t", c=2, h=8)),
                          reads=[b_tp], writes=[b_OT])
                for pr_ in range(4):
                    pm, b_pm = mm_ring.get()
                    i = 0
                    for hh in range(2):
                        h = 2 * pr_ + hh
                        for c in range(2):
                            tk.op("pe", lambda g, h=h, c=c, pm=pm, i=i: g.matmul(pm[:, 0:128], lhsT=Wuvp[:, c, h, :], rhs=OT[:, c, h].rearrange("p b t -> p (b t)"),
                                                                                 start=(i == 0), stop=(i == 3)), reads=[b_Wuvp, b_OT], writes=[b_pm])
                            i += 1
                    copy_op(alt("act", "dve"), AOs[:, pr_, :], pm[:, 0:128], [b_pm], [b_AOs])
                tk.barrier()
            if stop == "S2":
                raise _Stop()

            with contextlib.ExitStack() as e3:
                mm_ring = ring_ps(e3, "mmS3", 4, [128, 512], F32)
                Wg = sbt(e3, "WgS", [128, 8, 3072], BF16); b_Wg = Buf("WgS")
                for k in range(8):
                    tk.dma("pool", Wg[:, k, 0:512], w_in[k * 128:(k + 1) * 128, C_GA:C_GA + 512], writes=[b_Wg])
                    tk.dma("pool", Wg[:, k, 512:3072], w_in[k * 128:(k + 1) * 128, C_GB:C_GB + 2560], writes=[b_Wg])
                Woa, b_Wo = load_w(e3, "WoaS", w_oa, [128, 4, D])
                Wob, _b = load_w(e3, "WobS", w_ob, [128, 4, D], b=b_Wo)
                Wout, _b2 = load_w(e3, "WoutS", w_out, [128, 8, D], b=b_Wo)
                ga16 = sbt(e3, "ga16S", [128, 4, 128], BF16); b_ga = Buf("ga16S")
                gb16 = sbt(e3, "gb16S", [128, 4, 128], BF16); b_gb = Buf("gb16S")
                thm = sbt(e3, "thmS", [128, 16, 128], BF16); b_thm = Buf("thmS")
                mg16 = sbt(e3, "mg16S", [128, 8, 128], BF16); b_mg = Buf("mg16S")
                out_tile(1, xt, b_xt, hT, b_hT, Wg, b_Wg, Woa, Wob, Wout, b_Wo, mm_ring, tmp_ring,
                         lambda j: AOs[:, j, :], b_AOs, lambda c: h16[:, c, :], b_h16,
                         ga16, b_ga, gb16, b_gb, thm, b_thm, mg16, b_mg, Gs, st2, b_st2, junk, b_junk, ys)
                tk.wait_all("sp", out_bufs)
                tk.barrier()
    tk.wait_all("sp", out_bufs)
    tk.barrier()
    tk.close()
    return nc


_CACHE = {}
_DEV_STOP = None


def _rope_tables(pos):
    inv = np.power(np.float32(10000.0), -np.arange(16, dtype=np.float32) / np.float32(16.0)).astype(np.float32)
    ang = pos.astype(np.float32)[:, None] * inv[None, :]
    return np.cos(ang).astype(np.float32), np.sin(ang).astype(np.float32)


def own_tiles(role, NT):
    res = []
    for m in range(NT // 2):
        res.append(2 * m + (m % 2 if role == 0 else 1 - (m % 2)))
    return res


def kernel(x_prompt, x_sample, c_prompt, c_sample, cache_ckv, cache_kpe, state_conv, state_lru, page_table,
           ada_w, ada_b, norm_w, w_in, q_norm_w, kv_norm_w, w_uq, w_uk, w_uv, conv_w, conv_b,
           lru_wa, lru_ba, lru_wx, lru_bx, lru_lambda, w_oa, w_ob, w_out, final_norm_w):
    f32 = np.float32
    A = lambda v: np.ascontiguousarray(np.asarray(v))
    x_prompt = A(x_prompt); x_sample = A(x_sample)
    B, SEQ, _ = x_prompt.shape
    DB, T, _ = x_sample.shape
    NP = page_table.shape[1]
    NPOOL = cache_ckv.shape[1]
    assert B == 4 and DB == 128 and T == 8 and SEQ % 1024 == 0 and NP % 4 == 0
    NT = SEQ // 512
    NS = NT // 2
    key = (SEQ, NP, NPOOL)
    if key not in _CACHE:
        _CACHE[key] = build(SEQ, NP, NPOOL, stop=_DEV_STOP)
    nc = _CACHE[key]

    pool = np.concatenate([A(cache_ckv)[0], A(cache_kpe)[0]], axis=-1)
    cosf, sinf = _rope_tables(np.arange(SEQ))
    past = NP * 128
    cs_, ss_ = _rope_tables(past + np.arange(T))
    coss = np.tile(cs_, (16, 1)); sins = np.tile(ss_, (16, 1))
    sc = np.float32(ATTN_SCALE)
    kb_, ks_ = np.divmod(np.arange(128), 8)
    smask = np.zeros((128, 16, 8, 8), np.float32)
    for b in range(16):
        for t in range(8):
            smask[:, b, :, t] = ((kb_ == b) & (ks_ <= t))[:, None]
    smask = smask.reshape(128, 16 * 64).astype(ml_dtypes.bfloat16)
    shared = dict(
        pl0=pool[:NPOOL // 2], pl1=pool[NPOOL // 2:],
        ada_w=A(ada_w)[0], ada_b=A(ada_b)[0][None, :], norm_w=A(norm_w)[0][None, :], w_in=A(w_in)[0],
        q_norm_w=A(q_norm_w)[0][None, :], kv_norm_w=A(kv_norm_w)[0][None, :],
        w_uq=A(w_uq)[0].reshape(256, 768), w_uk=A(w_uk)[0].reshape(256, 512), w_uv=A(w_uv)[0].reshape(256, 512),
        conv_w=A(conv_w)[0], conv_b=A(conv_b)[0][None, :], lru_wa=A(lru_wa)[0], lru_ba=A(lru_ba)[0][None, :],
        lru_wx=A(lru_wx)[0], lru_bx=A(lru_bx)[0][None, :], lru_lambda=A(lru_lambda)[0][None, :],
        w_oa=A(w_oa)[0], w_ob=A(w_ob)[0], w_out=A(w_out)[0], fnw=A(final_norm_w)[None, :],
        cosf=cosf, sinf=sinf, coss=coss, sins=sins, cossq=coss * sc, sinsq=sins * sc, smask=smask,
    )
    in_maps = []
    pj = np.arange(128)[:, None, None]
    dj = np.arange(8)[None, :, None]
    jj = np.arange(512)[None, None, :]
    for c in range(8):
        b, role = c // 2, c % 2
        tiles = own_tiles(role, NT)
        idx = np.concatenate([np.arange(t * 512, (t + 1) * 512) for t in tiles])
        mk = np.zeros((2, 128, 8, 512), np.float32)
        for par in range(2):
            m = par
            off = 512 * (tiles[m] - 2 * m) if m < NS else 0
            mk[par] = (jj + off >= 128 * dj + pj)
        own_tab = np.zeros((1, 16), np.int32)
        own_tab[0, :NS] = tiles
        d = dict(shared)
        d.update(
            xf=x_prompt[b], xo=np.ascontiguousarray(x_prompt[b][idx]), xs=x_sample[16 * c:16 * c + 16].reshape(128, D),
            cp=A(c_prompt)[b:b + 1], cs=A(c_sample)[16 * c:16 * c + 16],
            sconv=A(state_conv)[0, 16 * c:16 * c + 16].reshape(48, 512), slru=A(state_lru)[0, 16 * c:16 * c + 16],
            ptab=np.ascontiguousarray(A(page_table)[16 * c:16 * c + 16].astype(np.int32)),
            coso=cosf[idx] * sc, sino=sinf[idx] * sc,
            masks=mk.reshape(2, 128, 8 * 512).astype(ml_dtypes.bfloat16), own_tab=own_tab,
        )
        in_maps.append(d)
    res = run_bass_kernel_spmd(nc, in_maps, core_ids=list(range(8)))
    R = res.results
    y_prompt = np.zeros((B, SEQ, D), f32)
    y_sample = np.zeros((DB, T, D), f32)
    n_ckv_p = np.zeros((1, B, SEQ, 256), f32); n_kpe_p = np.zeros((1, B, SEQ, 32), f32)
    n_conv_p = np.zeros((1, B, 3, 512), f32); n_lru_p = np.zeros((1, B, 512), f32)
    n_ckv_s = np.zeros((1, DB, T, 256), f32); n_kpe_s = np.zeros((1, DB, T, 32), f32)
    n_conv_s = np.zeros((1, DB, 3, 512), f32); n_lru_s = np.zeros((1, DB, 512), f32)
    for c in range(8):
        b, role = c // 2, c % 2
        tiles = own_tiles(role, NT)
        r = R[c]
        for m, t in enumerate(tiles):
            y_prompt[b, t * 512:(t + 1) * 512] = r["yo"][m * 512:(m + 1) * 512]
            n_ckv_p[0, b, t * 512:(t + 1) * 512] = r["ckvp"][t * 512:(t + 1) * 512]
            n_kpe_p[0, b, t * 512:(t + 1) * 512] = r["kpep"][t * 512:(t + 1) * 512]
        if role == 0:
            n_conv_p[0, b] = r["convp"]
            n_lru_p[0, b] = r["lrup"][0]
        y_sample[16 * c:16 * c + 16] = r["ys"].reshape(16, 8, D)
        n_ckv_s[0, 16 * c:16 * c + 16] = r["ckvs"].reshape(16, 8, 256)
        n_kpe_s[0, 16 * c:16 * c + 16] = r["kpes"].reshape(16, 8, 32)
        n_conv_s[0, 16 * c:16 * c + 16] = r["convs"].reshape(16, 3, 512)
        n_lru_s[0, 16 * c:16 * c + 16] = r["lrus"]
    return (y_prompt, y_sample, n_ckv_p, n_kpe_p, n_conv_p, n_lru_p, n_ckv_s, n_kpe_s, n_conv_s, n_lru_s)
```
